# Optimizing a Trainium2 kernel written in Bass

```python
import jax, jax.numpy as jnp
from jax import lax
import numpy as np

D_MODEL = 1024
BATCH = 8
SEQ = 2048
DEPTH = 1
DEC_BATCH = 128
DEC_SEQ = 4
PAST_LEN = 8192
PAGE_SIZE = 128

HEAD_DIM = 64
A_HEADS = 8
A_KV_HEADS = 2
A_WINDOW = 128
B_GROUPS = ((128, 1), (512, 4), (2048, 16))
B_HEADS_PER_GROUP = 4
B_HEADS = 3 * B_HEADS_PER_GROUP
D_FF = 4 * D_MODEL
CONV_W = 3
BLOCK = 128
RMS_EPS = 1e-6
NEG_INF = -1e30
QA = A_HEADS * HEAD_DIM
KVA = A_KV_HEADS * HEAD_DIM
QB = B_HEADS * HEAD_DIM
B_OUT = B_HEADS_PER_GROUP * HEAD_DIM
IN_SPLITS = (QA, KVA, KVA, QB, QB, QB, D_MODEL, D_MODEL)
IN_COLS = QA + 2 * KVA + 3 * QB + 2 * D_MODEL

kernel_name = "hybrid_swa_sink_dilated_convffn_step"

F32 = jnp.float32


def rmsnorm(x, gain):
    xf = x.astype(F32)
    y = xf * lax.rsqrt(jnp.mean(xf * xf, axis=-1, keepdims=True) + RMS_EPS)
    return (y * gain.astype(F32)).astype(x.dtype)


def alibi_slopes(n_heads):
    return jnp.exp2(-8.0 * jnp.arange(1, n_heads + 1, dtype=F32) / n_heads)


def softmax_stats(s, snk):
    m = jnp.max(s, axis=-1)
    if snk is not None:
        m = jnp.maximum(m, snk)
    p = jnp.exp(s - m[..., None])
    l = jnp.sum(p, axis=-1)
    if snk is not None:
        l = l + jnp.exp(snk - m)
    return p, l, m + jnp.log(l)


def banded_attention(q, k, v, slopes, max_dist, dist_scale, sinks=None):
    n, L, hq, hd = q.shape
    hk = k.shape[2]
    rep = hq // hk
    nb = -(-L // BLOCK)
    pad = nb * BLOCK - L
    qb = jnp.pad(q, ((0, 0), (0, pad), (0, 0), (0, 0))).reshape(n, nb, BLOCK, hk, rep, hd)

    def two_blocks(t):
        tb = jnp.pad(t, ((0, 0), (BLOCK, pad), (0, 0), (0, 0))).reshape(n, nb + 1, BLOCK, hk, hd)
        return jnp.concatenate([tb[:, :-1], tb[:, 1:]], axis=2)

    k2, v2 = two_blocks(k), two_blocks(v)
    s = jnp.einsum('nbqgrd,nbkgd->nbgrqk', qb.astype(F32), k2.astype(F32)) * (HEAD_DIM ** -0.5)
    dist = (jnp.arange(BLOCK)[:, None] + BLOCK) - jnp.arange(2 * BLOCK)[None, :]
    key_pos = jnp.arange(nb)[:, None] * BLOCK + jnp.arange(2 * BLOCK)[None, :] - BLOCK
    valid = ((dist >= 0) & (dist <= max_dist))[None] & (key_pos >= 0)[:, None, :]
    bias = -slopes.astype(F32).reshape(hk, rep)[:, :, None, None] * (dist * dist_scale).astype(F32)
    s = jnp.where(valid[None, :, None, None], s + bias, NEG_INF)
    snk = None if sinks is None else sinks.astype(F32).reshape(hk, rep, 1)
    p, l, lse = softmax_stats(s, snk)
    o = jnp.einsum('nbgrqk,nbkgd->nbqgrd', p, v2.astype(F32)) / jnp.moveaxis(l, -1, 2)[..., None]
    o = o.reshape(n, nb * BLOCK, hq, hd)[:, :L]
    lse = jnp.moveaxis(lse, -1, 2).reshape(n, nb * BLOCK, hq)[:, :L]
    return o, lse


def dilated_prompt_attention(q, k, v, slopes, window, dil):
    n, L, h, hd = q.shape
    ls = L // dil

    def to_sub(t):
        return t.reshape(n, ls, dil, h, hd).transpose(0, 2, 1, 3, 4).reshape(n * dil, ls, h, hd)

    o, lse = banded_attention(to_sub(q), to_sub(k), to_sub(v), slopes, window // dil, dil)
    o = o.reshape(n, dil, ls, h, hd).transpose(0, 2, 1, 3, 4).reshape(n, L, h, hd)
    lse = lse.reshape(n, dil, ls, h).transpose(0, 2, 1, 3).reshape(n, L, h)
    return o, lse


def gathered_window_attention(q, k_buf, v_buf, k_new, v_new, slopes, max_dist, dil, sinks=None):
    n, t, hq, hd = q.shape
    hk = k_new.shape[2]
    rep = hq // hk
    wb = k_buf.shape[1]
    kf = jnp.concatenate([k_buf.astype(k_new.dtype), k_new], axis=1)
    vf = jnp.concatenate([v_buf.astype(v_new.dtype), v_new], axis=1)
    steps = jnp.arange(max_dist + 1)
    idx = wb + jnp.arange(t)[:, None] - steps[None, :] * dil
    valid = idx >= 0
    idx = jnp.maximum(idx, 0)
    kg, vg = kf[:, idx], vf[:, idx]
    qg = q.reshape(n, t, hk, rep, hd)
    s = jnp.einsum('ntgrd,ntkgd->ngrtk', qg.astype(F32), kg.astype(F32)) * (HEAD_DIM ** -0.5)
    bias = -slopes.astype(F32).reshape(hk, rep)[:, :, None, None] * (steps * dil).astype(F32)
    s = jnp.where(valid, s + bias, NEG_INF)
    snk = None if sinks is None else sinks.astype(F32).reshape(hk, rep, 1)
    p, l, lse = softmax_stats(s, snk)
    o = jnp.einsum('ngrtk,ntkgd->ntgrd', p, vg.astype(F32)) / jnp.moveaxis(l, -1, 1)[..., None]
    return o.reshape(n, t, hq, hd), jnp.moveaxis(lse, -1, 1).reshape(n, t, hq)


def window_update(prev_kv, k, v, window):
    full = jnp.concatenate([prev_kv.astype(k.dtype), jnp.stack([k, v], axis=2)], axis=1)
    keep = min(window, full.shape[1])
    return full[:, full.shape[1] - keep:]


def in_project(h, w_in):
    n, t, _ = h.shape
    cuts = np.cumsum(IN_SPLITS)[:-1].tolist()
    qa, ka, va, qb, kb, vb, ga, gb = jnp.split(jnp.einsum('ntd,dc->ntc', h, w_in), cuts, axis=-1)
    heads = lambda z, nh: z.reshape(n, t, nh, HEAD_DIM)
    return (heads(qa, A_HEADS), heads(ka, A_KV_HEADS), heads(va, A_KV_HEADS),
            heads(qb, B_HEADS), heads(kb, B_HEADS), heads(vb, B_HEADS), ga, gb)


def token_mixer(h, caches, w_in, sinks_a, w_branch_a, w_branch_b, w_out):
    n, t, _ = h.shape
    qa, ka, va, qb, kb, vb, ga, gb = in_project(h, w_in)
    slopes_a, slopes_b = alibi_slopes(A_HEADS), alibi_slopes(B_HEADS)
    if caches is None:
        prev = [jnp.zeros((n, 0, 2, A_KV_HEADS, HEAD_DIM), h.dtype)] + \
               [jnp.zeros((n, 0, 2, B_HEADS_PER_GROUP, HEAD_DIM), h.dtype)] * len(B_GROUPS)
        o_a, _ = banded_attention(qa, ka, va, slopes_a, A_WINDOW - 1, 1, sinks_a)
    else:
        prev = list(caches)
        o_a, _ = gathered_window_attention(qa, prev[0][:, :, 0], prev[0][:, :, 1], ka, va,
                                           slopes_a, A_WINDOW - 1, 1, sinks_a)
    new = [window_update(prev[0], ka, va, A_WINDOW)]
    outs, lses = [], []
    for g, (win, dil) in enumerate(B_GROUPS):
        hs = slice(g * B_HEADS_PER_GROUP, (g + 1) * B_HEADS_PER_GROUP)
        qg, kg, vg, sg = qb[:, :, hs], kb[:, :, hs], vb[:, :, hs], slopes_b[hs]
        if caches is None:
            o, lse = dilated_prompt_attention(qg, kg, vg, sg, win, dil)
        else:
            c = prev[g + 1]
            o, lse = gathered_window_attention(qg, c[:, :, 0], c[:, :, 1], kg, vg, sg, win // dil, dil)
        outs.append(o)
        lses.append(lse)
        new.append(window_update(prev[g + 1], kg, vg, win))
    wts = jax.nn.softmax(jnp.stack(lses, axis=0), axis=0)
    o_b = jnp.sum(wts[..., None] * jnp.stack(outs, axis=0), axis=0)
    y_a = jnp.einsum('ntc,cd->ntd', o_a.reshape(n, t, QA).astype(h.dtype), w_branch_a)
    y_b = jnp.einsum('ntc,cd->ntd', o_b.reshape(n, t, B_OUT).astype(h.dtype), w_branch_b)
    mixed = jax.nn.sigmoid(ga) * y_a + jax.nn.sigmoid(gb) * y_b
    return jnp.einsum('ntd,de->nte', mixed, w_out), new


def conv_ffn(h, conv_prev, w_up, conv_w, conv_b, w_down):
    t = h.shape[1]
    a, g = jnp.split(jnp.einsum('ntd,df->ntf', h, w_up), 2, axis=-1)
    ap = jnp.concatenate([conv_prev.astype(a.dtype), a], axis=1)
    c = conv_b
    for j in range(CONV_W):
        c = c + conv_w[j] * ap[:, j:j + t]
    y = jnp.einsum('ntf,fd->ntd', jax.nn.gelu(c, approximate=True) * g, w_down)
    return y, ap[:, t:]


def decoder_layer(x, caches, conv_prev, w_in, sinks_a, w_branch_a, w_branch_b, w_out,
                  g_mix_pre, g_mix_post, g_ffn_pre, g_ffn_post, w_up, conv_w, conv_b, w_down):
    mix, new_kv = token_mixer(rmsnorm(x, g_mix_pre), caches, w_in, sinks_a, w_branch_a, w_branch_b, w_out)
    x = x + rmsnorm(mix, g_mix_post)
    f, new_conv = conv_ffn(rmsnorm(x, g_ffn_pre), conv_prev, w_up, conv_w, conv_b, w_down)
    x = x + rmsnorm(f, g_ffn_post)
    return x, new_kv + [new_conv]


def setup_inputs(seed: int = 0) -> dict:
    key = jax.random.key(seed)
    ks = jax.random.split(key, 24)
    nrm = lambda k, shape, scale: jax.random.normal(k, shape, F32) * scale
    bh = B_HEADS_PER_GROUP
    return {
        "x_prompt": nrm(ks[0], (BATCH, SEQ, D_MODEL), 1.0),
        "x_sample": nrm(ks[1], (DEC_BATCH, DEC_SEQ, D_MODEL), 1.0),
        "cache_a_kv": nrm(ks[2], (DEPTH, DEC_BATCH, min(A_WINDOW, PAST_LEN), 2, A_KV_HEADS, HEAD_DIM), 1.0),
        "cache_b1_kv": nrm(ks[3], (DEPTH, DEC_BATCH, min(B_GROUPS[0][0], PAST_LEN), 2, bh, HEAD_DIM), 1.0),
        "cache_b2_kv": nrm(ks[4], (DEPTH, DEC_BATCH, min(B_GROUPS[1][0], PAST_LEN), 2, bh, HEAD_DIM), 1.0),
        "cache_b3_kv": nrm(ks[5], (DEPTH, DEC_BATCH, min(B_GROUPS[2][0], PAST_LEN), 2, bh, HEAD_DIM), 1.0),
        "state_conv": nrm(ks[6], (DEPTH, DEC_BATCH, CONV_W - 1, D_FF), 1.0),
        "w_in": nrm(ks[7], (DEPTH, D_MODEL, IN_COLS), D_MODEL ** -0.5),
        "sinks_a": nrm(ks[8], (DEPTH, A_HEADS), 0.5),
        "w_branch_a": nrm(ks[9], (DEPTH, QA, D_MODEL), QA ** -0.5),
        "w_branch_b": nrm(ks[10], (DEPTH, B_OUT, D_MODEL), B_OUT ** -0.5),
        "w_out": nrm(ks[11], (DEPTH, D_MODEL, D_MODEL), D_MODEL ** -0.5),
        "norm_mix_pre": 1.0 + nrm(ks[12], (DEPTH, D_MODEL), 0.05),
        "norm_mix_post": 1.0 + nrm(ks[13], (DEPTH, D_MODEL), 0.05),
        "norm_ffn_pre": 1.0 + nrm(ks[14], (DEPTH, D_MODEL), 0.05),
        "norm_ffn_post": 1.0 + nrm(ks[15], (DEPTH, D_MODEL), 0.05),
        "w_up": nrm(ks[16], (DEPTH, D_MODEL, 2 * D_FF), D_MODEL ** -0.5),
        "conv_w": nrm(ks[17], (DEPTH, CONV_W, D_FF), CONV_W ** -0.5),
        "conv_b": nrm(ks[18], (DEPTH, D_FF), 0.02),
        "w_down": nrm(ks[19], (DEPTH, D_FF, D_MODEL), D_FF ** -0.5),
    }


def reference(x_prompt, x_sample, cache_a_kv, cache_b1_kv, cache_b2_kv, cache_b3_kv, state_conv,
              w_in, sinks_a, w_branch_a, w_branch_b, w_out, norm_mix_pre, norm_mix_post,
              norm_ffn_pre, norm_ffn_post, w_up, conv_w, conv_b, w_down):
    yp, ys = x_prompt, x_sample
    new_p = [[] for _ in range(5)]
    new_s = [[] for _ in range(5)]
    for l in range(DEPTH):
        wts = (w_in[l], sinks_a[l], w_branch_a[l], w_branch_b[l], w_out[l], norm_mix_pre[l],
               norm_mix_post[l], norm_ffn_pre[l], norm_ffn_post[l], w_up[l], conv_w[l], conv_b[l], w_down[l])
        conv0 = jnp.zeros((yp.shape[0], CONV_W - 1, D_FF), yp.dtype)
        yp, sp = decoder_layer(yp, None, conv0, *wts)
        ys, ss = decoder_layer(ys, (cache_a_kv[l], cache_b1_kv[l], cache_b2_kv[l], cache_b3_kv[l]),
                               state_conv[l], *wts)
        for i in range(5):
            new_p[i].append(sp[i])
            new_s[i].append(ss[i])
    sp = [jnp.stack(v, axis=0) for v in new_p]
    ss = [jnp.stack(v, axis=0) for v in new_s]
    return (yp, ys, sp[0], ss[0], sp[1], ss[1], sp[2], ss[2], sp[3], ss[3], sp[4], ss[4])
```

```python
import os
import numpy as np
from contextlib import ExitStack
import concourse.bass as bass
import concourse.mybir as mybir
from concourse.bass_utils import run_bass_kernel_spmd

F32 = mybir.dt.float32
BF16 = mybir.dt.bfloat16
AF = mybir.ActivationFunctionType
ALU = mybir.AluOpType

ENGS = ("pe", "act", "dve", "pool", "sp")
NEG = -30000.0
NT = 2112
SC0 = 2048


class Op:
    __slots__ = ("eng", "fn", "reads", "writes", "dma", "bg", "deps", "flag", "semi", "semval", "idx", "prevdma")

    def __init__(self, eng, fn, reads, writes, dma, bg=False):
        self.eng, self.fn, self.reads, self.writes, self.dma, self.bg = eng, fn, reads, writes, dma, bg
        self.deps = []
        self.flag = False
        self.semi = None
        self.semval = 0
        self.prevdma = None


class _Rec:
    def __init__(self):
        self.call = None

    def __getattr__(self, name):
        def f(*a, **k):
            self.call = (name, a, k)
            return self
        return f


class Sched:
    def __init__(self, nc, n_dma_sems=8, same_engine_raw=True):
        self.nc = nc
        self.ops = []
        self.n_dma_sems = n_dma_sems
        self.same_engine_raw = same_engine_raw

    def add(self, eng, fn, reads=(), writes=(), dma=False, bg=False):
        rec = _Rec()
        fn(rec)
        call = rec.call
        assert call is not None
        op = Op(eng, call, tuple(reads), tuple(writes), dma, bg)
        op.idx = len(self.ops)
        self.ops.append(op)
        return op

    def pe(self, fn, r=(), w=()):
        return self.add("pe", fn, r, w)

    def act(self, fn, r=(), w=()):
        return self.add("act", fn, r, w)

    def dve(self, fn, r=(), w=()):
        return self.add("dve", fn, r, w)

    def pool(self, fn, r=(), w=()):
        return self.add("pool", fn, r, w)

    def dma(self, eng, fn, r=(), w=(), bg=False):
        return self.add(eng, fn, r, w, dma=True, bg=bg)

    def barrier(self):
        op = Op(None, None, (), (), False)
        op.idx = len(self.ops)
        self.ops.append(op)

    def analyze(self):
        last_w, readers = {}, {}
        for op in self.ops:
            if op.eng is None:
                continue
            deps = set()
            for r in op.reads:
                p = last_w.get(r)
                if p is not None:
                    deps.add(p)
            for w in op.writes:
                p = last_w.get(w)
                if p is not None:
                    deps.add(p)
                for q in readers.get(w, ()):
                    deps.add(q)
            for r in op.reads:
                readers.setdefault(r, []).append(op.idx)
            for w in op.writes:
                last_w[w] = op.idx
                readers[w] = []
            deps.discard(op.idx)
            op.deps = sorted(deps)

    @staticmethod
    def _cost(op):
        name, a, k = op.fn
        def fe(ap):
            n = 1
            for d in ap.shape[1:]:
                n *= d
            return n
        if op.dma:
            o = k.get("out")
            nb = fe(o) * o.shape[0] * 4
            busy = float(os.environ.get("KPOOLDMA", "650")) if op.eng == "pool" else 150.0
            return busy, 2500.0 + nb / 200.0
        if op.eng == "pe":
            if name == "transpose":
                return 160.0, 220.0
            n = fe(k["rhs"])
            c = 110.0 + n * 0.21
            return c, c + 60.0
        o = k.get("out")
        n = fe(o) if o is not None else 64
        if op.eng == "pool":
            c = 250.0 + n * 2.1
        else:
            c = 180.0 + n * 1.05
        return c, c

    def schedule(self, window=int(os.environ.get('KWIN', '3000'))):
        ops = self.ops
        segs, cur = [], []
        for op in ops:
            if op.eng is None:
                segs.append(cur)
                cur = []
            else:
                cur.append(op)
        segs.append(cur)
        order = []
        pos = {}
        fin = {}
        tnow = 0.0
        reorder = not os.environ.get("KNOSCHED")
        prev_last, prev_dmas = {}, []
        for seg in segs:
            if not seg:
                continue
            CP = os.environ.get("KCP", "1") == "1"
            blv = {}
            if CP:
                succ = {}
                inseg = set(op.idx for op in seg)
                for op in seg:
                    for d in op.deps:
                        if d in inseg:
                            succ.setdefault(d, []).append(op.idx)
                for op in reversed(seg):
                    m = 0.0
                    for sidx in succ.get(op.idx, ()):
                        v = blv[sidx]
                        if v > m:
                            m = v
                    blv[op.idx] = m + self._cost(op)[1]
            queues = {e: [op for op in seg if op.eng == e] for e in ENGS}
            heads = {e: 0 for e in ENGS}
            done = set()
            t_eng = {e: tnow for e in ENGS}
            first_of_seg = {}
            seg_order = []
            nleft = len(seg)
            segset = set(op.idx for op in seg)
            while nleft:
                best = None
                bestkey = None
                NRDY = int(os.environ.get("KNRDY", "6"))
                for e in ENGS:
                    q = queues[e]
                    cnt = 0
                    nrdy = 0
                    i = heads[e]
                    while i < len(q) and cnt < ((4096 if e == 'sp' else window) if reorder else 1):
                        op = q[i]
                        i += 1
                        if op.idx in done:
                            continue
                        cnt += 1
                        ok = True
                        rdy = t_eng[e]
                        for d in op.deps:
                            if d in segset and d not in done:
                                ok = False
                                break
                            f = fin.get(d, 0.0)
                            if f > rdy:
                                rdy = f
                        if not ok:
                            continue
                        if CP:
                            if rdy <= t_eng[e] + 1e-9:
                                nrdy += 1
                                key = (t_eng[e], -blv[op.idx])
                            else:
                                key = (rdy, -blv[op.idx])
                            if best is None or key < bestkey:
                                best, bestkey = (max(rdy, t_eng[e]), e, op), key
                            if nrdy >= NRDY:
                                break
                            continue
                        if best is None or rdy < best[0] - 1e-9:
                            best = (rdy, e, op)
                        if rdy <= t_eng[e] + 1e-9:
                            break
                assert best is not None, "scheduler deadlock"
                rdy, e, op = best
                busy, lat = self._cost(op)
                fin[op.idx] = rdy + lat
                t_eng[e] = rdy + busy
                done.add(op.idx)
                q = queues[e]
                while heads[e] < len(q) and q[heads[e]].idx in done:
                    heads[e] += 1
                if e not in first_of_seg:
                    first_of_seg[e] = op
                seg_order.append(op)
                nleft -= 1
            for e, op in first_of_seg.items():
                extra = [p.idx for p in prev_last.values()] + prev_dmas
                op.deps = sorted(set(op.deps) | set(extra))
            last = {}
            for op in seg_order:
                if not op.dma:
                    last[op.eng] = op
            prev_last = last
            prev_dmas = [op.idx for op in seg_order if op.dma and not op.bg]
            order.extend(seg_order)
            tprev = tnow
            tnow = max([tnow] + [fin[op.idx] for op in seg_order])
            if os.environ.get("KVERB"):
                bz = {e: 0.0 for e in ENGS}
                for op in seg_order:
                    bz[op.eng] += self._cost(op)[0]
                print("SEG us %.1f -> %.1f" % (tprev / 1e3, tnow / 1e3), {e: int(v / 1e3) for e, v in bz.items()}, flush=True)
        if os.environ.get("KVERB"):
            print("SCHED est total us", tnow / 1e3, flush=True)
        return order

    def emit(self):
        nc = self.nc
        self.analyze()
        order = self.schedule()
        ops = self.ops
        for op in order:
            need = []
            for d in op.deps:
                p = ops[d]
                if p.dma or p.eng != op.eng or op.dma:
                    need.append(d)
                elif self.same_engine_raw and p.eng != "pe" and (set(p.writes) & set(op.reads)):
                    need.append(d)
            op.deps = need
            for d in need:
                ops[d].flag = True
        es = ExitStack()
        eng_sem = {e: es.enter_context(nc.semaphore("s_" + e)) for e in ENGS}
        dma_sems = {e: [es.enter_context(nc.semaphore("d_%s%d" % (e, i))) for i in range(self.n_dma_sems)]
                    for e in ("sp", "act", "pool")}
        cnt = {e: 0 for e in ENGS}
        dcnt = {e: 0 for e in dma_sems}
        dval = {e: [0] * self.n_dma_sems for e in dma_sems}
        dlast = {e: [None] * self.n_dma_sems for e in dma_sems}
        bg_final = {e: [] for e in ENGS}
        nbg = 0
        bgsems = [es.enter_context(nc.semaphore("bg%d" % i)) for i in range(2)]
        bgval = [0, 0]
        bglast = [None, None]
        for op in order:
            if op.dma and op.bg:
                k = nbg % 2
                nbg += 1
                bgval[k] += 16
                op.semi = bgsems[k]
                op.semval = bgval[k]
                op.prevdma = bglast[k]
                bglast[k] = op.idx
                op.flag = True
                bg_final[op.eng] = [(bgsems[i], bgval[i]) for i in range(2) if bgval[i]]
            elif op.dma:
                k = dcnt[op.eng] % self.n_dma_sems
                dcnt[op.eng] += 1
                dval[op.eng][k] += 16
                op.semi = dma_sems[op.eng][k]
                op.semval = dval[op.eng][k]
                op.prevdma = dlast[op.eng][k]
                dlast[op.eng][k] = op.idx
                op.flag = True
            elif op.flag:
                cnt[op.eng] += 1
                op.semi = eng_sem[op.eng]
                op.semval = cnt[op.eng]
        if os.environ.get("KVERB"):
            print("SEMCOUNTS", cnt, dval, "nops", len(ops), flush=True)
        per_eng = {e: [op for op in order if op.eng == e] for e in ENGS}
        final_dma = {e: [(dma_sems[e][k], dval[e][k]) for k in range(self.n_dma_sems) if dval[e][k]]
                     for e in dma_sems}

        def run(engname, eobj):
            seen = {}
            for op in per_eng[engname]:
                waits = {}
                dl = list(op.deps)
                if op.prevdma is not None:
                    dl.append(op.prevdma)
                for d in dl:
                    p = ops[d]
                    key = id(p.semi)
                    if key not in waits or waits[key][1] < p.semval:
                        waits[key] = (p.semi, p.semval)
                for key, (s, v) in waits.items():
                    if seen.get(key, 0) >= v:
                        continue
                    eobj.wait_ge(s, v)
                    seen[key] = v
                if op.fn is None:
                    continue
                name, a, k = op.fn
                inst = getattr(eobj, name)(*a, **k)
                if op.flag:
                    inst.then_inc(op.semi, 16 if op.dma else 1)
            for (s, v) in list(final_dma.get(engname, ())) + bg_final[engname]:
                if seen.get(id(s), 0) < v:
                    eobj.wait_ge(s, v)

        if os.environ.get("KSIM"):
            semv = {}
            pc = {e: 0 for e in ENGS}
            prog = {}
            for e in ENGS:
                seen, lst = {}, []
                for op in per_eng[e]:
                    waits = {}
                    dl = list(op.deps) + ([op.prevdma] if op.prevdma is not None else [])
                    for d in dl:
                        p = ops[d]
                        key = id(p.semi)
                        if key not in waits or waits[key] < p.semval:
                            waits[key] = p.semval
                    ws = [(k, v) for k, v in waits.items() if seen.get(k, 0) < v]
                    for k, v in ws:
                        seen[k] = v
                    lst.append((ws, (id(op.semi), 16 if op.dma else 1) if op.flag else None, op.idx))
                prog[e] = lst
            progress = True
            while progress:
                progress = False
                for e in ENGS:
                    while pc[e] < len(prog[e]):
                        ws, inc, idx = prog[e][pc[e]]
                        if all(semv.get(k, 0) >= v for k, v in ws):
                            if inc:
                                semv[inc[0]] = semv.get(inc[0], 0) + inc[1]
                            pc[e] += 1
                            progress = True
                        else:
                            break
            print("KSIM", {e: (pc[e], len(prog[e])) for e in ENGS}, flush=True)
            for e in ENGS:
                if pc[e] < len(prog[e]):
                    ws, inc, idx = prog[e][pc[e]]
                    print("  stuck", e, "op", idx, ops[idx].fn[0], ops[idx].reads, ops[idx].writes, flush=True)
        with nc.Block() as block:
            block.tensor(lambda e: run("pe", e))
            block.scalar(lambda e: run("act", e))
            block.vector(lambda e: run("dve", e))
            block.gpsimd(lambda e: run("pool", e))
            block.sync(lambda e: run("sp", e))
        es.close()


class Ring:
    def __init__(self, name, aps):
        self.name, self.aps, self.i = name, aps, 0

    def next(self):
        k = self.i % len(self.aps)
        self.i += 1
        return self.aps[k], (self.name, k)


def _slopes(n):
    return np.exp2(-8.0 * np.arange(1, n + 1, dtype=np.float64) / n)


def _band_tables(slopes, ds, maxd):
    k = np.arange(128)[:, None]
    q = np.arange(128)[None, :]
    out = np.full((128, 2, len(slopes), 128), NEG, np.float64)
    for kind in (0, 1):
        d = q - k + (128 if kind == 0 else 0)
        valid = (d >= 0) & (d <= maxd)
        for h, s in enumerate(slopes):
            out[:, kind, h, :] = np.where(valid, -s * ds * d, NEG)
    return out.astype(np.float32)


def _const_tables():
    sa, sb = _slopes(8), _slopes(12)
    ta = np.zeros((128, 2, 2, 4, 128), np.float32)
    for g in range(2):
        ta[:, g] = _band_tables([sa[i + 4 * g] for i in range(4)], 1.0, 127)
    tb = np.stack([_band_tables(sb[4 * g:4 * g + 4], float(ds), 128) for g, ds in enumerate((1, 4, 16))], 0)
    r = np.arange(128)[:, None]
    rn = np.arange(4)[:, None]
    t = np.arange(4)[None, :]
    ts = np.full((128, 80), NEG, np.float64)
    tn = np.full((4, 80), NEG, np.float64)
    for g in range(2):
        for i in range(4):
            s = sa[i + 4 * g]
            c0 = g * 16 + i * 4
            ts[:, c0:c0 + 4] = np.where(r >= t + 1, -s * (128 + t - r), NEG)
            tn[:, c0:c0 + 4] = np.where(rn <= t, -s * (t - rn), NEG)
    for gi, ds in ((0, 1.0), (1, 4.0), (2, 16.0)):
        base = 32 + gi * 16
        for j in range(2):
            for hh in range(2):
                s = sb[4 * gi + 2 * j + hh]
                cols = base + j * 8 + np.arange(4) * 2 + hh
                if gi == 0:
                    ts[:, cols] = np.where(r >= t, -s * (128 + t - r), NEG)
                    tn[:, cols] = np.where(rn <= t, -s * (t - rn), NEG)
                else:
                    ts[:, cols] = -s * ds * (128 - r) + 0.0 * t
                    tn[:, cols] = np.where(rn == t, 0.0, NEG)
    return ta, tb.astype(np.float32), ts.astype(np.float32), tn.astype(np.float32)


def _prep_shared(inp):
    f = lambda a: np.ascontiguousarray(a, dtype=np.float32)
    w_in = inp["w_in"][0]
    qa, ka, va = w_in[:, 0:512], w_in[:, 512:640], w_in[:, 640:768]
    qb, kb, vb = w_in[:, 768:1536], w_in[:, 1536:2304], w_in[:, 2304:3072]
    ga, gb = w_in[:, 3072:4096], w_in[:, 4096:5120]
    kp = lambda w: w.reshape(8, 128, -1).transpose(1, 0, 2)
    qperm = np.concatenate([np.r_[i * 64:(i + 1) * 64, (i + 4) * 64:(i + 5) * 64] for i in range(4)])
    sh = {}
    sh["WA"] = f(kp(np.concatenate([qa[:, qperm], ka, va], 1)))
    sh["WB"] = f(np.stack([kp(np.concatenate([qb[:, 256 * g:256 * g + 256], kb[:, 256 * g:256 * g + 256],
                                              vb[:, 256 * g:256 * g + 256]], 1)) for g in range(3)], 0))
    wba = inp["w_branch_a"][0][qperm]
    wbb = inp["w_branch_b"][0]
    wd1 = []
    for e in range(8):
        cs = slice(e * 128, (e + 1) * 128)
        g2 = kp(np.concatenate([ga[:, cs], gb[:, cs]], 1)).reshape(128, 2048)
        a2 = wba[:, cs].reshape(4, 128, 128).transpose(1, 0, 2).reshape(128, 512)
        b2 = wbb[:, cs].reshape(2, 128, 128).transpose(1, 0, 2).reshape(128, 256)
        wd1.append(np.concatenate([g2, a2, b2], 1))
    sh["WD1"] = f(np.stack(wd1, 0))
    sh["WOUT"] = f(kp(inp["w_out"][0]))
    w_up = inp["w_up"][0]
    sh["WUP"] = f(np.stack([kp(np.concatenate([w_up[:, j * 128:(j + 1) * 128],
                                               w_up[:, 4096 + j * 128:4096 + (j + 1) * 128]], 1)).reshape(128, 2048)
                            for j in range(32)], 0))
    sh["WDN"] = f(inp["w_down"][0].reshape(32, 128, 1024).transpose(1, 0, 2))
    gt = np.stack([inp["norm_mix_pre"][0].reshape(8, 128).T, inp["norm_ffn_pre"][0].reshape(8, 128).T], 1)
    sh["GT"] = f(gt)
    sh["GP"] = f(np.stack([np.broadcast_to(inp["norm_mix_post"][0], (128, 1024)),
                           np.broadcast_to(inp["norm_ffn_post"][0], (128, 1024))], 1))
    sk = inp["sinks_a"][0]
    sh["SK"] = f(np.stack([sk[0:4], sk[4:8]], 0).reshape(1, 8))
    cw = np.concatenate([inp["conv_w"][0], inp["conv_b"]], 0)
    sh["CW"] = f(cw.reshape(4, 32, 128).transpose(2, 1, 0))
    sh["IDENT"] = np.eye(128, dtype=np.float32)
    ta, tb, ts, tn = _const_tables()
    sh["TA"], sh["TB"], sh["TS"], sh["TSN"] = ta, tb, ts, tn
    return sh


def _prep_core(inp, c):
    f = lambda a: np.ascontiguousarray(a, dtype=np.float32)
    d = {}
    xs = inp["x_sample"][16 * c:16 * c + 16].transpose(1, 0, 2).reshape(64, 1024)
    d["X"] = f(np.concatenate([inp["x_prompt"][c], xs], 0))
    d["CA"] = f(inp["cache_a_kv"][0, 16 * c:16 * c + 16].reshape(16, 128, 256))
    d["CB1"] = f(inp["cache_b1_kv"][0, 16 * c:16 * c + 16].reshape(16, 128, 512))
    d["CB2"] = f(inp["cache_b2_kv"][0, 16 * c:16 * c + 16].reshape(16, 512, 512))
    d["CB3"] = f(inp["cache_b3_kv"][0, 16 * c:16 * c + 16].reshape(16, 2048, 512))
    d["STATE"] = f(inp["state_conv"][0, 16 * c:16 * c + 16].transpose(1, 0, 2).reshape(32, 4096))
    return d


import os
GROUPS = ("A", 0, 1, 2)
_PARTS = set(os.environ.get("KPARTS", "fm,tm,pa,sa").split(","))
_GSEL = os.environ.get("KGROUPS", "")


def build(debug=False, phases=("A", "B", "D1", "D2")):
    nc = bass.Bass("TRN2", target_bir_lowering=False)
    S = Sched(nc)
    din = lambda name, shape: nc.dram_tensor(name, list(shape), F32, kind="ExternalInput").ap()
    dout = lambda name, shape, dt=F32: nc.dram_tensor(name, list(shape), dt, kind="ExternalOutput").ap()

    X = din("X", (NT, 1024))
    CA, CB1, CB2, CB3 = din("CA", (16, 128, 256)), din("CB1", (16, 128, 512)), din("CB2", (16, 512, 512)), din("CB3", (16, 2048, 512))
    STATE = din("STATE", (32, 4096))
    WA, WB = din("WA", (128, 8, 768)), din("WB", (3, 128, 8, 768))
    WD1, WOUT = din("WD1", (8, 128, 2816)), din("WOUT", (128, 8, 1024))
    WUP, WDN = din("WUP", (32, 128, 2048)), din("WDN", (128, 32, 1024))
    GT_d, GP_d, SK_d, CW_d = din("GT", (128, 2, 8)), din("GP", (128, 2, 1024)), din("SK", (1, 8)), din("CW", (128, 32, 4))
    IDENT_d = din("IDENT", (128, 128))
    TA_d, TB_d, TS_d, TSN_d = din("TA", (128, 2, 2, 4, 128)), din("TB", (3, 128, 2, 4, 128)), din("TS", (128, 80)), din("TSN", (4, 80))

    Y = dout("Y", (NT, 1024))
    NKV_P = {"A": dout("NA_P", (128, 256)), 0: dout("NB1_P", (128, 512)), 1: dout("NB2_P", (512, 512)), 2: dout("NB3_P", (2048, 512))}
    NKV_S = {"A": dout("NA_S", (16, 128, 256)), 0: dout("NB1_S", (16, 128, 512)), 1: dout("NB2_S", (16, 512, 512)), 2: dout("NB3_S", (16, 2048, 512))}
    CACHE = {"A": CA, 0: CB1, 1: CB2, 2: CB3}
    WIN = {"A": 128, 0: 128, 1: 512, 2: 2048}
    NCONV = dout("NCONV", (34, 4096))
    dbg = {}

    SB_BASE, SB_END = 17408, 229376
    cur = [SB_BASE]

    hw = [0]

    def alloc(name, shape, dt):
        nbytes = int(np.prod(shape[1:])) * (2 if dt == BF16 else 4)
        off = (cur[0] + 31) // 32 * 32
        assert off + nbytes <= SB_END, ("SBUF overflow", name, off + nbytes)
        cur[0] = off + nbytes
        hw[0] = max(hw[0], cur[0])
        return nc.alloc_sbuf_tensor_at(name, list(shape), dt, offset=off).ap()

    GT = alloc("GT", (128, 2, 8), F32)
    GP = alloc("GP", (128, 2, 1024), F32)
    CW = alloc("CW", (128, 32, 4), F32)
    IDB = alloc("IDB", (128, 128), BF16)
    IDF = alloc("IDF", (128, 128), F32)
    ONES = alloc("ONES", (128, 128), BF16)
    SKR = alloc("SKR", (1, 8), F32)
    SKE = alloc("SKE", (1, 8), F32)
    ZROW = alloc("ZROW", (1, 128), F32)
    SINKROW = alloc("SINKROW", (1, 2, 4, 128), BF16)
    STAT = alloc("STAT", (128, 8, 4), F32)
    SINKS = alloc("SINKS", (1, 32), BF16)
    QBD = alloc("QBD", (128, 4, 64, 2), BF16)
    h2t_off = (cur[0] + 31) // 32 * 32
    H2T = alloc("H2T", (128, 8, NT), BF16)
    mark_D2 = cur[0]
    UB = nc.alloc_sbuf_tensor_at("UB", [128, 2, NT], F32, offset=h2t_off).ap()
    LB = nc.alloc_sbuf_tensor_at("LB", [128, 2, NT], F32, offset=h2t_off + 2 * NT * 4).ap()
    hT = alloc("hT", (128, 8, NT), BF16)
    OA = alloc("OA", (128, 4, NT), BF16)
    OB = alloc("OB", (128, 2, NT), BF16)
    PERSIST_END = cur[0]

    def scol(ap, n):
        if len(ap.shape) == 2:
            return ap.rearrange("p (t n) -> p n t", n=16)[:, n, :]
        return ap.rearrange("p c (t n) -> p c n t", n=16)[:, :, n, :]

    psum = [nc.alloc_psum_tensor("ps%d" % i, [128, 512], F32).ap() for i in range(8)]
    bank_i = [0]

    NPB = int(os.environ.get("KNPB", "2"))
    pbank_i = [0]
    mode = {"attn": False}

    def next_bank():
        if NPB and mode["attn"]:
            k = bank_i[0] % (8 - NPB)
            bank_i[0] += 1
            return psum[k], ("ps", k)
        k = bank_i[0] % 8
        bank_i[0] += 1
        return psum[k], ("ps", k)

    def next_bank_proj():
        if not NPB:
            return next_bank()
        k = 8 - NPB + pbank_i[0] % NPB
        pbank_i[0] += 1
        return psum[k], ("ps", k)

    stat_i = [0]

    def next_stat():
        k = stat_i[0] % 8
        stat_i[0] += 1
        return STAT[:, k, :], ("stat", k)

    S.dma("sp", lambda e: e.dma_start(out=GT, in_=GT_d), w=["GT"])
    S.dma("sp", lambda e: e.dma_start(out=GP, in_=GP_d), w=["GP"])
    S.dma("sp", lambda e: e.dma_start(out=CW, in_=CW_d), w=["CW"])
    S.dma("sp", lambda e: e.dma_start(out=IDF, in_=IDENT_d), w=["IDF"])
    S.dma("sp", lambda e: e.dma_start(out=SKR, in_=SK_d), w=["SKR"])
    S.dma("pool", lambda e: e.dma_start(out=IDB, in_=IDENT_d), w=["IDB"])
    S.dve(lambda e: e.memset(ONES, 1.0), w=["ONES"])
    S.dve(lambda e: e.memset(ZROW, 0.0), w=["ZROW"])
    S.dve(lambda e: e.memset(QBD, 0.0), w=["QBD"])
    S.act(lambda e: e.activation(out=SKE, in_=SKR, func=AF.Exp), r=["SKR"], w=["SKE"])
    for gi in range(8):
        S.dve(lambda e, gi=gi: e.tensor_scalar(out=SINKROW[0:1, gi // 4, gi % 4, :], in0=ZROW, scalar1=SKE[0:1, gi:gi + 1],
                                               scalar2=None, op0=ALU.add), r=["SKE", "ZROW"], w=["SINKROW"])
        S.dve(lambda e, gi=gi: e.tensor_scalar(out=SINKS[0:1, gi * 4:gi * 4 + 4], in0=ZROW[0:1, 0:4], scalar1=SKE[0:1, gi:gi + 1],
                                               scalar2=None, op0=ALU.add), r=["SKE", "ZROW"], w=["SINKROW"])

    def rms_rstd(ss_ap, key_ss, out_ap, key_out, n):
        S.dve(lambda e: e.tensor_scalar(out=out_ap, in0=ss_ap, scalar1=1.0 / n, scalar2=1e-6, op0=ALU.mult, op1=ALU.add),
              r=[key_ss], w=[key_out])
        S.act(lambda e: e.activation(out=out_ap, in_=out_ap, func=AF.Sqrt), r=[key_out], w=[key_out])
        S.dve(lambda e: e.reciprocal(out=out_ap, in_=out_ap), r=[key_out], w=[key_out])

    next_stat_parity = [0]

    def norm_transpose(src, src_key, rows, gidx, dst, dst_keys, col0, XNr, JUNK):
        st, kst = next_stat()
        xn, kxn = XNr.next()
        S.act(lambda e: e.activation(out=xn[:rows, :], in_=src[:rows, :], func=AF.Square, accum_out=st[:rows, 0:1]),
              r=[src_key], w=[kst, kxn])
        rms_rstd(st[:rows, 0:1], kst, st[:rows, 1:2], kst, 1024)
        S.dve(lambda e: e.tensor_scalar(out=xn[:rows, :], in0=src[:rows, :], scalar1=st[:rows, 1:2], scalar2=None, op0=ALU.mult),
              r=[src_key, kst], w=[kxn])
        ps, kps = next_bank()
        psb = ps.bitcast(BF16)
        for k in range(8):
            S.pe(lambda e, k=k: e.transpose(psb[:, k * 128:k * 128 + rows], xn[:rows, k * 128:(k + 1) * 128], IDB[:rows, :rows]),
                 r=[kxn, "IDB"], w=[kps])
        gbc = GT[:, gidx, :].unsqueeze(2).to_broadcast([128, 8, rows])
        src3 = psb.rearrange("p (k q) -> p k q", k=8)[:, :, 0:rows]
        if next_stat_parity[0] % 2 == 0:
            S.dve(lambda e: e.tensor_tensor(out=dst[:, :, col0:col0 + rows], in0=src3, in1=gbc, op=ALU.mult), r=[kps, "GT"], w=dst_keys)
        else:
            for hlf in range(2):
                S.act(lambda e, hlf=hlf: e.activation(out=dst[:, 4 * hlf:4 * hlf + 4, col0:col0 + rows], in_=src3[:, 4 * hlf:4 * hlf + 4, :], func=AF.Copy), r=[kps], w=dst_keys)
            S.dve(lambda e: e.tensor_tensor(out=dst[:, :, col0:col0 + rows], in0=dst[:, :, col0:col0 + rows], in1=gbc, op=ALU.mult), r=[kps, "GT"], w=dst_keys)
        next_stat_parity[0] += 2

    mark_A = cur[0]
    XOFF = {}
    for nm_, sz_ in (("XIN0", 4096), ("XIN1", 4096), ("XN0", 2048), ("XN1", 2048)):
        XOFF[nm_] = (cur[0] + 31) // 32 * 32
        cur[0] = XOFF[nm_] + sz_
    hw[0] = max(hw[0], cur[0])
    XINr = Ring("XIN", [nc.alloc_sbuf_tensor_at("XIN%d" % i, [128, 1024], F32, offset=XOFF["XIN%d" % i]).ap() for i in range(2)])
    XNr = Ring("XN", [nc.alloc_sbuf_tensor_at("XN%d" % i, [128, 1024], BF16, offset=XOFF["XN%d" % i]).ap() for i in range(2)])
    JUNK = None
    for blk in range(17):
        rows = 128 if blk < 16 else 64
        xin, kx = XINr.next()
        S.dma("sp", lambda e, xin=xin, blk=blk, rows=rows: e.dma_start(out=xin[:rows, :], in_=X[blk * 128:blk * 128 + rows, :]), w=[kx])
        norm_transpose(xin, kx, rows, 0, hT, [("hT", blk)], blk * 128, XNr, JUNK)

    WT = alloc("WT", (128, 8, 768), BF16)
    QT = alloc("QT", (128, 4, NT), BF16)
    KT = alloc("KT", (128, 2, NT), BF16)
    VE = alloc("VE", (128, 16, 256), BF16)
    TAB = alloc("TAB", (128, 2048), F32)
    TS = alloc("TS", (128, 80), F32)
    TSN = alloc("TSN", (4, 80), F32)
    fences = []
    TMPr = Ring("TMP", [alloc("TMP%d" % i, (128, 512), F32) for i in range(2)]
                + [nc.alloc_sbuf_tensor_at("TMP2", [128, 512], F32, offset=XOFF["XN0"]).ap()])
    fences.append((("TMP", 2), ("XN", 0), TMPr.aps[2]))
    PTr = Ring("PT", [alloc("PT%d" % i, (128, 512), BF16) for i in range(4)])
    STGr = Ring("STG", [alloc("STG%d" % i, (128, 512), F32) for i in range(3)])
    LRr = Ring("LR", [alloc("LR%d" % i, (128, 512), F32) for i in range(2)])
    CKr = Ring("CK", [alloc("CK%d" % i, (128, 4, 512), BF16) for i in range(3)]
               + [nc.alloc_sbuf_tensor_at("CK3", [128, 4, 512], BF16, offset=XOFF["XIN0"]).ap()])
    fences.append((("CK", 3), ("XIN", 0), CKr.aps[3]))
    KCTr = Ring("KCT", [alloc("KCT%d" % i, (128, 4, 2, 128), BF16) for i in range(2)]
                + [nc.alloc_sbuf_tensor_at("KCT%d" % (2 + i), [128, 4, 2, 128], BF16, offset=XOFF["XIN1"] + 2048 * i).ap() for i in range(2)])
    fences.append((("KCT", 2), ("XIN", 1), KCTr.aps[2]))
    fences.append((("KCT", 3), ("XIN", 1), KCTr.aps[3]))
    for (newk, oldk, ap_) in fences:
        S.dve(lambda e, ap_=ap_: e.memset(ap_.rearrange("p a b c -> p (a b c)")[:, 0:2] if len(ap_.shape) == 4 else (ap_.rearrange("p a b -> p (a b)")[:, 0:2] if len(ap_.shape) == 3 else ap_[:, 0:2]), 0.0),
              w=[newk, oldk])
    NKVr = Ring("NKV", [alloc("NKV%d" % i, (4, 256), BF16) for i in range(4)])
    TMPNr = Ring("TMPN", [alloc("TMPN%d" % i, (4, 32), F32) for i in range(2)])
    PTNr = Ring("PTN", [alloc("PTN%d" % i, (4, 32), BF16) for i in range(2)])

    S.dma("sp", lambda e: e.dma_start(out=TS, in_=TS_d), w=["TS"])
    S.dma("sp", lambda e: e.dma_start(out=TSN, in_=TSN_d), w=["TSN"])

    def hview(g, k):
        base = hT[:, k, 0:2048]
        if g in ("A", 0):
            return base.rearrange("p (b i) -> p b i", i=128)
        if g == 1:
            return base.rearrange("p (bb i r) -> p r bb i", r=4, i=128)
        return base.rearrange("p (i r) -> p r i", r=16)

    ticks = []

    tick_w = []

    def tick(wt=2.0):
        k = ("tick", len(ticks))
        ticks.append(k)
        tick_w.append(wt)
        return k

    def bg_shifts():
        pieces = []
        for g in GROUPS:
            W = WIN[g]
            if g == "A":
                pieces += [(g, 0, 8, 4, W), (g, 8, 16, 4, W)]
            elif g == 0:
                pieces += [(g, n0, n0 + 4, 4, W) for n0 in range(0, 16, 4)]
            elif g == 1:
                pieces += [(g, n0, n0 + 1, 4, W) for n0 in range(16)]
            else:
                for n0 in range(16):
                    pieces += [(g, n0, n0 + 1, 4 + 511 * q, 4 + 511 * (q + 1)) for q in range(4)]
        cum = np.cumsum(tick_w)
        for i, (g, n0, n1, r0, r1) in enumerate(pieces):
            tk = ticks[int(np.searchsorted(cum, (i + 0.3) * cum[-1] / len(pieces)))]
            S.dma("sp", lambda e, g=g, n0=n0, n1=n1, r0=r0, r1=r1: e.dma_start(out=NKV_S[g][n0:n1, r0 - 4:r1 - 4, :], in_=CACHE[g][n0:n1, r0:r1, :]),
                  r=[tk] + ([("bgc", i - 2)] if i >= 2 else []), w=[("bgc", i)], bg=True)

    def do_group(g):
        isA = g == "A"
        nq = 4 if isA else 2
        nk = 1 if isA else 2
        kvc0 = 512 if isA else 256
        ncols = 256 if isA else 512
        vc0 = 128 if isA else 256
        W = WIN[g]
        wsrc = WA if isA else WB[g]
        S.dma("pool", lambda e: e.dma_start(out=WT, in_=wsrc), w=["WT"])
        tsrc = TA_d if isA else TB_d[g]
        tn = 2048 if isA else 1024
        S.dma("sp", lambda e: e.dma_start(out=TAB[:, 0:tn], in_=tsrc.rearrange("p a b c d -> p (a b c d)") if isA
                                          else tsrc.rearrange("p a b c -> p (a b c)")), w=["TAB"])
        allh = [("hT", b) for b in range(17)]
        evi = [0]
        for (dst, dname, dch, wc0) in [(QT, "QT", i, i * 128) for i in range(nq)] + [(KT, "KT", i, nq * 128 + i * 128) for i in range(nk)]:
            for s in (range(int(os.environ.get("KFMLIM", "5"))) if "fm" in _PARTS else ()):
                n = 512 if s < 4 else 64
                ps, kps = next_bank_proj()
                for k in range(8):
                    rhs = hT[:, k, SC0:SC0 + 64] if s == 4 else hT[:, k, s * 512:(s + 1) * 512]
                    S.pe(lambda e, ps=ps, k=k, rhs=rhs, wc0=wc0, n=n: e.matmul(ps[:, 0:n], lhsT=WT[:, k, wc0:wc0 + 128], rhs=rhs,
                                                                             start=(k == 0), stop=(k == 7)),
                         r=["WT"] + allh, w=[kps])
                src = ps[:, 0:n]
                wkeys = [(dname, dch, s)]
                if s == 4:
                    o = dst[:, dch, SC0:SC0 + 64]
                elif g in ("A", 0):
                    o = dst[:, dch, s * 512:s * 512 + n]
                else:
                    rr = 4 if g == 1 else 16
                    mm = 512 // rr
                    o = dst[:, dch, 0:2048].rearrange("p (r m) -> p m r", r=rr)[:, mm * s:mm * (s + 1), :]
                    src = ps[:, 0:512].rearrange("p (m r) -> p m r", r=rr)
                    wkeys = [(dname, dch, q) for q in range(4)]
                if evi[0] % 2 == 0:
                    S.act(lambda e, o=o, src=src: e.copy(out=o, in_=src), r=[kps], w=wkeys)
                else:
                    S.dve(lambda e, o=o, src=src: e.tensor_copy(out=o, in_=src), r=[kps], w=wkeys)
                evi[0] += 1
        for kb in (range(int(os.environ.get("KTM0", "0")), int(os.environ.get("KTM1", "17"))) if "tm" in _PARTS else ()):
            rows = 128 if kb < 16 else 64
            ps, kps = next_bank_proj()
            need = kb == 16 or (g in ("A", 0) and kb == 15) or (g == 1 and kb % 4 == 3) or g == 2
            c_lo = 0 if need else vc0
            for k in range(8):
                if kb == 16:
                    lhsT = hT[:, k, SC0:SC0 + 64]
                elif g == 1:
                    lhsT = hview(g, k)[:, kb // 4, kb % 4, :]
                else:
                    lhsT = hview(g, k)[:, kb, :]
                S.pe(lambda e, ps=ps, k=k, lhsT=lhsT, rows=rows, c_lo=c_lo: e.matmul(ps[:rows, c_lo:ncols], lhsT=lhsT, rhs=WT[:, k, kvc0 + c_lo:kvc0 + ncols],
                                                                         start=(k == 0), stop=(k == 7)), r=["WT"] + allh, w=[kps])
            if kb < 16:
                S.act(lambda e, ps=ps, kb=kb: e.copy(out=VE[:, kb, 0:ncols - vc0], in_=ps[:, vc0:ncols]), r=[kps], w=[("VE", kb), tick(1.0)])
            if need:
                stg, kstg = STGr.next()
                if True:
                    S.act(lambda e, ps=ps, stg=stg, rows=rows: e.copy(out=stg[:rows, 0:ncols], in_=ps[:rows, 0:ncols]), r=[kps], w=[kstg])
                else:
                    S.dve(lambda e, ps=ps, stg=stg, rows=rows: e.tensor_copy(out=stg[:rows, 0:ncols], in_=ps[:rows, 0:ncols]), r=[kps], w=[kstg])
                if kb == 16:
                    for t in range(4):
                        S.dma("sp", lambda e, stg=stg, t=t: e.dma_start(out=NKV_S[g][:, W - 4 + t, :], in_=stg[16 * t:16 * t + 16, 0:ncols]),
                              r=[kstg], w=[("newrows", g, t)])
                else:
                    if g in ("A", 0):
                        dstd = NKV_P[g]
                    elif g == 1:
                        dstd = NKV_P[g].rearrange("(i r) c -> r i c", r=4)[kb // 4]
                    else:
                        dstd = NKV_P[g].rearrange("(i r) c -> r i c", r=16)[kb]
                    S.dma("sp", lambda e, stg=stg, dstd=dstd: e.dma_start(out=dstd, in_=stg[:, 0:ncols]), r=[kstg])
        for b in (range(16) if "pa" in _PARTS else ()):
            if g in ("A", 0):
                has_prev = b > 0
            elif g == 1:
                has_prev = b % 4 > 0
            else:
                has_prev = False
            kinds = ([(b - 1, 0)] if has_prev else []) + [(b, 1)]
            psU, kU = next_bank()
            psL, kL = next_bank()
            qk = [("QT", i, b // 4) for i in range(nq)]
            if isA:
                for gg in range(2):
                    pts = []
                    for (kb, kind) in kinds:
                        psS, kS = next_bank()
                        S.pe(lambda e, psS=psS, gg=gg, kb=kb: e.matmul(psS.rearrange("p (i q) -> p i q", i=4), lhsT=KT[64 * gg:64 * gg + 64, 0, kb * 128:(kb + 1) * 128],
                                                                       rhs=QT[64 * gg:64 * gg + 64, 0:4, b * 128:(b + 1) * 128], start=True, stop=True),
                             r=[("KT", 0, kb // 4)] + qk, w=[kS])
                        tmp, kT = TMPr.next()
                        tab = TAB[:, (gg * 2 + kind) * 512:(gg * 2 + kind + 1) * 512]
                        S.dve(lambda e, tmp=tmp, psS=psS, tab=tab: e.scalar_tensor_tensor(out=tmp, in0=psS, scalar=0.125, in1=tab, op0=ALU.mult, op1=ALU.add),
                              r=[kS, "TAB"], w=[kT])
                        pt, kP = PTr.next()
                        S.act(lambda e, pt=pt, tmp=tmp: e.activation(out=pt, in_=tmp, func=AF.Exp), r=[kT], w=[kP] + ([tick(5.0)] if (kind == 1 and gg == 0) else []))
                        pts.append((pt, kP, kb))
                    for idx, (pt, kP, kb) in enumerate(pts):
                        S.pe(lambda e, pt=pt, gg=gg, idx=idx: e.matmul(psL[64 * gg:64 * gg + 64, :], lhsT=ONES[:, 0:64], rhs=pt, start=(idx == 0), stop=False),
                             r=[kP, "ONES"], w=[kL])
                    S.pe(lambda e, gg=gg: e.matmul(psL[64 * gg:64 * gg + 64, :], lhsT=ONES[0:1, 0:64], rhs=SINKROW[0:1, gg].rearrange("p a b -> p (a b)"),
                                                   start=False, stop=True), r=["SINKROW", "ONES"], w=[kL])
                    for idx, (pt, kP, kb) in enumerate(pts):
                        S.pe(lambda e, pt=pt, gg=gg, kb=kb, idx=idx, n=len(pts): e.matmul(
                            psU[64 * gg:64 * gg + 64, :], lhsT=VE[:, kb, gg * 64:gg * 64 + 64], rhs=pt,
                            start=(idx == 0), stop=(idx == n - 1)), r=[kP, ("VE", kb)], w=[kU])
                lr, kLr = LRr.next()
                S.act(lambda e, lr=lr: e.activation(out=lr, in_=psL, func=AF.Ln), r=[kL], w=[kLr])
                S.act(lambda e, lr=lr: e.activation(out=lr, in_=lr, func=AF.Exp, scale=-1.0), r=[kLr], w=[kLr])
                S.dve(lambda e, lr=lr: e.tensor_tensor(out=OA[:, :, b * 128:(b + 1) * 128], in0=psU.rearrange("p (i q) -> p i q", i=4),
                                                       in1=lr.rearrange("p (i q) -> p i q", i=4), op=ALU.mult), r=[kU, kLr], w=[("OA", b)])
            else:
                pts = []
                for (kb, kind) in kinds:
                    tmp, kT = TMPr.next()
                    for hh in range(2):
                        psS, kS = next_bank()
                        for j in range(2):
                            S.pe(lambda e, psS=psS, hh=hh, j=j, kb=kb: e.matmul(psS[:, j * 128:(j + 1) * 128], lhsT=KT[64 * hh:64 * hh + 64, j, kb * 128:(kb + 1) * 128],
                                                                               rhs=QT[64 * hh:64 * hh + 64, j, b * 128:(b + 1) * 128], start=True, stop=True),
                                 r=[("KT", j, kb // 4)] + qk, w=[kS])
                        tab = TAB[:, kind * 512:(kind + 1) * 512].rearrange("p (j hh q) -> p j hh q", j=2, hh=2)[:, :, hh, :]
                        tv = tmp.rearrange("p (j hh q) -> p j hh q", j=2, hh=2)[:, :, hh, :]
                        S.dve(lambda e, tv=tv, psS=psS, tab=tab: e.scalar_tensor_tensor(out=tv, in0=psS[:, 0:256].rearrange("p (j q) -> p j q", j=2), scalar=0.125, in1=tab,
                                                                                      op0=ALU.mult, op1=ALU.add), r=[kS, "TAB"], w=[kT])
                    pt, kP = PTr.next()
                    S.act(lambda e, pt=pt, tmp=tmp: e.activation(out=pt, in_=tmp, func=AF.Exp), r=[kT], w=[kP] + ([tick(5.0)] if kind == 1 else []))
                    pts.append((pt, kP, kb))
                for idx, (pt, kP, kb) in enumerate(pts):
                    S.pe(lambda e, pt=pt, idx=idx, n=len(pts): e.matmul(psL, lhsT=ONES, rhs=pt, start=(idx == 0), stop=(idx == n - 1)), r=[kP, "ONES"], w=[kL])
                for h in range(4):
                    hh, j = h % 2, h // 2
                    for idx, (pt, kP, kb) in enumerate(pts):
                        S.pe(lambda e, pt=pt, h=h, hh=hh, j=j, kb=kb, idx=idx, n=len(pts): e.matmul(
                            psU[64 * hh:64 * hh + 64, j * 128:(j + 1) * 128], lhsT=VE[:, kb, h * 64:h * 64 + 64], rhs=pt[:, h * 128:(h + 1) * 128],
                            start=(idx == 0), stop=(idx == n - 1)), r=[kP, ("VE", kb)], w=[kU])
                if g == 0:
                    uv = UB[:, :, b * 128:(b + 1) * 128]
                    lv = LB[:, :, b * 128:(b + 1) * 128]
                elif g == 1:
                    uv = UB[:, :, 0:2048].rearrange("p j (bb i r) -> p j r bb i", r=4, i=128)[:, :, b // 4, b % 4, :]
                    lv = LB[:, :, 0:2048].rearrange("p j (bb i r) -> p j r bb i", r=4, i=128)[:, :, b // 4, b % 4, :]
                else:
                    uv = UB[:, :, 0:2048].rearrange("p j (i r) -> p j r i", r=16)[:, :, b, :]
                    lv = LB[:, :, 0:2048].rearrange("p j (i r) -> p j r i", r=16)[:, :, b, :]
                pu = psU[:, 0:256].rearrange("p (j q) -> p j q", j=2)
                wk, rk = ("UB", g), ([("UB", g - 1)] if g > 0 else [])
                if os.environ.get("KNOEVAC"):
                    continue
                if g == 0:
                    S.dve(lambda e, uv=uv, pu=pu: e.tensor_copy(out=uv, in_=pu), r=[kU] + rk, w=[wk])
                else:
                    S.dve(lambda e, uv=uv, pu=pu: e.tensor_tensor(out=uv, in0=uv, in1=pu, op=ALU.add), r=[kU] + rk, w=[wk])
                for hh in range(2):
                    pl = psL.rearrange("p (j hh q) -> p j hh q", j=2, hh=2)[64 * hh:64 * hh + 64, :, hh, :]
                    lvv = lv[64 * hh:64 * hh + 64]
                    if g == 0:
                        S.dve(lambda e, lvv=lvv, pl=pl: e.tensor_copy(out=lvv, in_=pl), r=[kL] + rk, w=[wk])
                    else:
                        S.dve(lambda e, lvv=lvv, pl=pl: e.tensor_tensor(out=lvv, in0=lvv, in1=pl, op=ALU.add), r=[kL] + rk, w=[wk])

        ntile = 1 if g in ("A", 0) else 4
        nc_ = 32 if isA else 16
        ts0 = {"A": 0, 0: 32, 1: 48, 2: 64}[g]
        nkc = 1 if isA else 2
        qs = [("QT", i, 4) for i in range(nq)]
        S.act(lambda e: e.copy(out=QBD[0:64, 0:nq, :, 0], in_=QT[0:64, 0:nq, SC0:NT]), r=qs, w=["QBD"])
        S.dve(lambda e: e.tensor_copy(out=QBD[64:128, 0:nq, :, 1], in_=QT[64:128, 0:nq, SC0:NT]), r=qs, w=["QBD"])
        for n in (range(16) if "sa" in _PARTS else ()):
            ck, kck = CKr.next()
            if g in ("A", 0):
                src = CACHE[g][n]
                S.dma("pool", lambda e, ck=ck, src=src: e.dma_start(out=ck[:, 0, 0:ncols], in_=src), w=[kck])
            elif g == 1:
                src = CACHE[g][n].rearrange("(m r) c -> m r c", r=4)
                S.dma("pool", lambda e, ck=ck, src=src: e.dma_start(out=ck, in_=src), w=[kck])
            else:
                src = CACHE[g][n].rearrange("(m r) c -> m r c", r=16)[:, 0:4, :]
                S.dma("pool", lambda e, ck=ck, src=src: e.dma_start(out=ck, in_=src), w=[kck])
            nkv, knkv = NKVr.next()
            S.dma("pool", lambda e, nkv=nkv, n=n: e.dma_start(out=nkv[:, 0:ncols - vc0], in_=NKV_S[g][n, W - 4:W, vc0:ncols]),
                  r=[("newrows", g, t) for t in range(4)], w=[knkv])
            kct, kkct = KCTr.next()
            psT, kpsT = next_bank()
            psTb = psT.bitcast(BF16)
            for tl in range(ntile):
                for c in range(nkc):
                    S.pe(lambda e, tl=tl, c=c, ck=ck: e.transpose(psTb[:, (tl * 2 + c) * 128:(tl * 2 + c + 1) * 128], ck[:, tl, c * 128:(c + 1) * 128], IDB),
                         r=[kck, "IDB"], w=[kpsT])
            ncp = 128 if isA else ntile * 256
            S.act(lambda e, kct=kct: e.copy(out=kct.rearrange("p a b c -> p (a b c)")[:, 0:ncp], in_=psTb[:, 0:ncp]), r=[kpsT], w=[kkct, tick(6.0)])
            psS, kS = next_bank()
            if isA:
                rhs = QBD[:, 0:4, :, :].rearrange("p i (t n) g -> p n g i t", n=16)[:, n]
                S.pe(lambda e, rhs=rhs, kct=kct: e.matmul(psS[:, 0:32].rearrange("p (g i t) -> p g i t", g=2, i=4), lhsT=kct[:, 0, 0, :], rhs=rhs, start=True, stop=True),
                     r=[kkct, "QBD"], w=[kS])
                S.pe(lambda e, rhs=rhs: e.matmul(psS[0:4, 256:288].rearrange("p (g i t) -> p g i t", g=2, i=4), lhsT=scol(KT[:, 0, SC0:NT], n), rhs=rhs, start=True, stop=True),
                     r=[("KT", 0, 4), "QBD"], w=[kS])
            else:
                for j in range(2):
                    rhs = QBD[:, j, :, :].rearrange("p (t n) h -> p n t h", n=16)[:, n]
                    if ntile == 1:
                        S.pe(lambda e, j=j, rhs=rhs, kct=kct: e.matmul(psS[:, j * 8:j * 8 + 8].rearrange("p (t h) -> p t h", h=2), lhsT=kct[:, 0, j, :], rhs=rhs, start=True, stop=True),
                             r=[kkct, "QBD"], w=[kS])
                    else:
                        for t in range(4):
                            S.pe(lambda e, j=j, t=t, rhs=rhs, kct=kct: e.matmul(psS[:, j * 8 + t * 2:j * 8 + t * 2 + 2], lhsT=kct[:, t, j, :], rhs=rhs[:, t, :], start=True, stop=True),
                                 r=[kkct, "QBD"], w=[kS])
                    S.pe(lambda e, j=j, rhs=rhs: e.matmul(psS[0:4, 256 + j * 8:256 + j * 8 + 8].rearrange("p (t h) -> p t h", h=2), lhsT=scol(KT[:, j, SC0:NT], n), rhs=rhs, start=True, stop=True),
                         r=[("KT", j, 4), "QBD"], w=[kS])
            tmp, kT = TMPr.next()
            S.dve(lambda e, tmp=tmp, psS=psS: e.scalar_tensor_tensor(out=tmp[:, 0:nc_], in0=psS[:, 0:nc_], scalar=0.125, in1=TS[:, ts0:ts0 + nc_], op0=ALU.mult, op1=ALU.add),
                  r=[kS, "TS"], w=[kT])
            pt, kP = PTr.next()
            S.act(lambda e, pt=pt, tmp=tmp: e.activation(out=pt[:, 0:nc_], in_=tmp[:, 0:nc_], func=AF.Exp), r=[kT], w=[kP])
            tmpn, kTn = TMPNr.next()
            S.dve(lambda e, tmpn=tmpn, psS=psS: e.scalar_tensor_tensor(out=tmpn[:, 0:nc_], in0=psS[0:4, 256:256 + nc_], scalar=0.125, in1=TSN[:, ts0:ts0 + nc_], op0=ALU.mult, op1=ALU.add),
                  r=[kS, "TSN"], w=[kTn])
            ptn, kPn = PTNr.next()
            S.act(lambda e, ptn=ptn, tmpn=tmpn: e.activation(out=ptn[:, 0:nc_], in_=tmpn[:, 0:nc_], func=AF.Exp), r=[kTn], w=[kPn])
            psU, kU = next_bank()
            psL = psU[:, 256:512]
            S.pe(lambda e, pt=pt: e.matmul(psL[:, 0:nc_], lhsT=ONES, rhs=pt[:, 0:nc_], start=True, stop=False), r=[kP, "ONES"], w=[kU])
            S.pe(lambda e, ptn=ptn: e.matmul(psL[:, 0:nc_], lhsT=ONES[0:4, :], rhs=ptn[:, 0:nc_], start=False, stop=(not isA)), r=[kPn, "ONES"], w=[kU])
            if isA:
                S.pe(lambda e: e.matmul(psL[:, 0:nc_], lhsT=ONES[0:1, :], rhs=SINKS, start=False, stop=True), r=["SINKROW", "ONES"], w=[kU])
                S.pe(lambda e, ck=ck, pt=pt: e.matmul(psU[:, 0:32], lhsT=ck[:, 0, 128:256], rhs=pt[:, 0:32], start=True, stop=False), r=[kP, kck], w=[kU])
                S.pe(lambda e, nkv=nkv, ptn=ptn: e.matmul(psU[:, 0:32], lhsT=nkv[:, 0:128], rhs=ptn[:, 0:32], start=False, stop=True), r=[kPn, knkv], w=[kU])
                lr, kLr = LRr.next()
                for gg in range(2):
                    S.dve(lambda e, lr=lr, gg=gg: e.reciprocal(out=lr[64 * gg:64 * gg + 64, 0:16], in_=psL[64 * gg:64 * gg + 64, gg * 16:gg * 16 + 16]), r=[kU], w=[kLr])
                for gg in range(2):
                    S.dve(lambda e, lr=lr, n=n, gg=gg: e.tensor_tensor(out=scol(OA[64 * gg:64 * gg + 64, :, SC0:NT], n), in0=psU[64 * gg:64 * gg + 64, gg * 16:gg * 16 + 16].rearrange("p (i t) -> p i t", i=4),
                                                                   in1=lr[64 * gg:64 * gg + 64, 0:16].rearrange("p (i t) -> p i t", i=4), op=ALU.mult), r=[kU, kLr], w=[("OA", 16)])
            else:
                for j in range(2):
                    vj = slice(256 + j * 128, 256 + (j + 1) * 128)
                    vn = slice(j * 128, (j + 1) * 128)
                    if ntile == 1:
                        S.pe(lambda e, j=j, ck=ck, pt=pt, vj=vj: e.matmul(psU[:, j * 8:j * 8 + 8], lhsT=ck[:, 0, vj], rhs=pt[:, j * 8:j * 8 + 8], start=True, stop=False), r=[kP, kck], w=[kU])
                        S.pe(lambda e, j=j, nkv=nkv, ptn=ptn, vn=vn: e.matmul(psU[:, j * 8:j * 8 + 8], lhsT=nkv[:, vn], rhs=ptn[:, j * 8:j * 8 + 8], start=False, stop=True), r=[kPn, knkv], w=[kU])
                    else:
                        S.pe(lambda e, j=j, nkv=nkv, ptn=ptn, vn=vn: e.matmul(psU[:, j * 8:j * 8 + 8], lhsT=nkv[:, vn], rhs=ptn[:, j * 8:j * 8 + 8], start=True, stop=False, skip_group_check=True), r=[kPn, knkv], w=[kU])
                        for t in range(4):
                            S.pe(lambda e, j=j, t=t, ck=ck, pt=pt, vj=vj: e.matmul(psU[:, j * 8 + t * 2:j * 8 + t * 2 + 2], lhsT=ck[:, t, vj], rhs=pt[:, j * 8 + t * 2:j * 8 + t * 2 + 2],
                                                                                 start=False, stop=True, skip_group_check=True), r=[kP, kck], w=[kU])
                wk, rk = ("UB", g), ([("UB", g - 1)] if g > 0 else [])
                for hh in range(2):
                    hs = slice(64 * hh, 64 * hh + 64)
                    pu = psU[hs, 0:16].rearrange("p (j t h) -> p j t h", j=2, h=2)[:, :, :, hh]
                    pl = psL[hs, 0:16].rearrange("p (j t h) -> p j t h", j=2, h=2)[:, :, :, hh]
                    uv = scol(UB[hs, :, SC0:NT], n)
                    lvv = scol(LB[hs, :, SC0:NT], n)
                    if g == 0:
                        S.dve(lambda e, uv=uv, pu=pu: e.tensor_copy(out=uv, in_=pu), r=[kU] + rk, w=[wk])
                        S.dve(lambda e, lvv=lvv, pl=pl: e.tensor_copy(out=lvv, in_=pl), r=[kU] + rk, w=[wk])
                    else:
                        S.dve(lambda e, uv=uv, pu=pu: e.tensor_tensor(out=uv, in0=uv, in1=pu, op=ALU.add), r=[kU] + rk, w=[wk])
                        S.dve(lambda e, lvv=lvv, pl=pl: e.tensor_tensor(out=lvv, in0=lvv, in1=pl, op=ALU.add), r=[kU] + rk, w=[wk])

    if "B" in phases:
        mode["attn"] = True
        for g in GROUPS:
            if _GSEL and str(g) not in _GSEL.split(","):
                continue
            do_group(g)
        mode["attn"] = False
        bg_shifts()
        for s in range(5):
            c0, n = (s * 512, 512) if s < 4 else (SC0, 64)
            for j in range(2):
                lr, kLr = LRr.next()
                S.act(lambda e, lr=lr, j=j, c0=c0, n=n: e.activation(out=lr[:, 0:n], in_=LB[:, j, c0:c0 + n], func=AF.Ln), r=[("UB", 2)], w=[kLr])
                S.act(lambda e, lr=lr, n=n: e.activation(out=lr[:, 0:n], in_=lr[:, 0:n], func=AF.Exp, scale=-1.0), r=[kLr], w=[kLr])
                S.dve(lambda e, lr=lr, j=j, c0=c0, n=n: e.tensor_tensor(out=OB[:, j, c0:c0 + n], in0=UB[:, j, c0:c0 + n], in1=lr[:, 0:n], op=ALU.mult),
                      r=[("UB", 2), kLr], w=[("OB", s)])
    if debug:
        dbg["hT"] = dout("D_hT", (128, 8, NT), BF16)
        dbg["OA"] = dout("D_OA", (128, 4, NT), BF16)
        dbg["OB"] = dout("D_OB", (128, 2, NT), BF16)
        S.dma("sp", lambda e: e.dma_start(out=dbg["hT"], in_=hT), r=[("hT", b) for b in range(17)])
        S.dma("sp", lambda e: e.dma_start(out=dbg["OA"], in_=OA), r=[("OA", b) for b in range(17)])
        S.dma("sp", lambda e: e.dma_start(out=dbg["OB"], in_=OB), r=[("OB", s) for s in range(5)])

    TILES = [(0, 512), (512, 512), (1024, 512), (1536, 512), (SC0, 64)]
    if os.environ.get("KVERB"):
        print("SBUF end of phase B", cur[0], flush=True)
    S.barrier()
    cur[0] = PERSIST_END
    if "D1" in phases:
        MIXT = alloc("MIXT", (128, 8, NT), BF16)
        WOUTs = alloc("WOUTs", (128, 8, 1024), BF16)
        WD1r = Ring("WD1", [alloc("WD1_%d" % i, (128, 2816), BF16) for i in range(2)])
        SGr = Ring("SG", [alloc("SG%d" % i, (128, 512), F32) for i in range(4)])
        XIN2r = Ring("XIN2", [alloc("XIN2_%d" % i, (128, 1024), F32) for i in range(3)])
        X1r = Ring("X1", [alloc("X1_%d" % i, (128, 1024), F32) for i in range(3)])
        XN2r = Ring("XN2", [alloc("XN2_%d" % i, (128, 1024), BF16) for i in range(3)])
        JUNK2 = alloc("JUNK2", (128, 512), BF16)
        for ei in range(8):
            wd, kwd = WD1r.next()
            S.dma("pool", lambda e, wd=wd, ei=ei: e.dma_start(out=wd, in_=WD1[ei]), w=[kwd])
            if ei == 2:
                S.dma("pool", lambda e: e.dma_start(out=WOUTs, in_=WOUT), w=["WOUT"])
            for ti, (c0, n) in enumerate(TILES):
                hk = [("hT", b) for b in range(17)]
                banks = [next_bank() for _ in range(4)]
                (pGA, kGA), (pGB, kGB), (pYA, kYA), (pYB, kYB) = banks
                for k in range(8):
                    S.pe(lambda e, wd=wd, k=k, c0=c0, n=n, pGA=pGA: e.matmul(pGA[:, 0:n], lhsT=wd[:, k * 256:k * 256 + 128], rhs=hT[:, k, c0:c0 + n], start=(k == 0), stop=(k == 7)),
                         r=[kwd] + hk, w=[kGA])
                for k in range(8):
                    S.pe(lambda e, wd=wd, k=k, c0=c0, n=n, pGB=pGB: e.matmul(pGB[:, 0:n], lhsT=wd[:, k * 256 + 128:k * 256 + 256], rhs=hT[:, k, c0:c0 + n], start=(k == 0), stop=(k == 7)),
                         r=[kwd] + hk, w=[kGB])
                for i in range(4):
                    S.pe(lambda e, wd=wd, i=i, c0=c0, n=n, pYA=pYA: e.matmul(pYA[:, 0:n], lhsT=wd[:, 2048 + i * 128:2048 + (i + 1) * 128], rhs=OA[:, i, c0:c0 + n], start=(i == 0), stop=(i == 3)),
                         r=[kwd] + [("OA", b) for b in range(17)], w=[kYA])
                for j in range(2):
                    S.pe(lambda e, wd=wd, j=j, c0=c0, n=n, pYB=pYB: e.matmul(pYB[:, 0:n], lhsT=wd[:, 2560 + j * 128:2560 + (j + 1) * 128], rhs=OB[:, j, c0:c0 + n], start=(j == 0), stop=(j == 1)),
                         r=[kwd] + [("OB", s) for s in range(5)], w=[kYB])
                sa, ksa = SGr.next()
                sb_, ksb = SGr.next()
                S.act(lambda e, sa=sa, pGA=pGA, n=n: e.activation(out=sa[:, 0:n], in_=pGA[:, 0:n], func=AF.Sigmoid), r=[kGA], w=[ksa])
                S.act(lambda e, sb_=sb_, pGB=pGB, n=n: e.activation(out=sb_[:, 0:n], in_=pGB[:, 0:n], func=AF.Sigmoid), r=[kGB], w=[ksb])
                S.dve(lambda e, sa=sa, pYA=pYA, n=n: e.tensor_tensor(out=sa[:, 0:n], in0=sa[:, 0:n], in1=pYA[:, 0:n], op=ALU.mult), r=[ksa, kYA], w=[ksa])
                S.dve(lambda e, sb_=sb_, pYB=pYB, n=n: e.tensor_tensor(out=sb_[:, 0:n], in0=sb_[:, 0:n], in1=pYB[:, 0:n], op=ALU.mult), r=[ksb, kYB], w=[ksb])
                S.pool(lambda e, sa=sa, sb_=sb_, ei=ei, c0=c0, n=n: e.tensor_tensor(out=MIXT[:, ei, c0:c0 + n], in0=sa[:, 0:n], in1=sb_[:, 0:n], op=ALU.add),
                       r=[ksa, ksb], w=[("MIXT", ti)])
        for blk in range(17):
            rows = 128 if blk < 16 else 64
            col0 = blk * 128
            ti = blk // 4
            ph = [next_bank(), next_bank()]
            for half in range(2):
                pm, kpm = ph[half]
                for k in range(8):
                    S.pe(lambda e, pm=pm, k=k, half=half, rows=rows, col0=col0: e.matmul(pm[:rows, :], lhsT=MIXT[:, k, col0:col0 + rows], rhs=WOUTs[:, k, half * 512:(half + 1) * 512],
                                                                                      start=(k == 0), stop=(k == 7)), r=[("MIXT", ti), "WOUT"], w=[kpm])
            st, kst = next_stat()
            for half in range(2):
                pm, kpm = ph[half]
                S.act(lambda e, pm=pm, half=half, rows=rows, st=st: e.activation(out=JUNK2[:rows, 0:512], in_=pm[:rows, :], func=AF.Square, accum_out=st[:rows, half:half + 1]),
                      r=[kpm], w=[kst, "JUNK2"])
            S.dve(lambda e, st=st, rows=rows: e.tensor_tensor(out=st[:rows, 2:3], in0=st[:rows, 0:1], in1=st[:rows, 1:2], op=ALU.add), r=[kst], w=[kst])
            rms_rstd(st[:rows, 2:3], kst, st[:rows, 3:4], kst, 1024)
            xin, kx = XIN2r.next()
            S.dma("sp", lambda e, xin=xin, rows=rows, col0=col0: e.dma_start(out=xin[:rows, :], in_=X[col0:col0 + rows, :]), w=[kx])
            x1, kx1 = X1r.next()
            for half in range(2):
                pm, kpm = ph[half]
                S.dve(lambda e, pm=pm, half=half, rows=rows, st=st, x1=x1: e.scalar_tensor_tensor(out=x1[:rows, half * 512:(half + 1) * 512], in0=pm[:rows, :], scalar=st[:rows, 3:4],
                                                                                              in1=GP[:rows, 0, half * 512:(half + 1) * 512], op0=ALU.mult, op1=ALU.mult),
                      r=[kpm, kst, "GP"], w=[kx1])
            S.pool(lambda e, x1=x1, xin=xin, rows=rows: e.tensor_tensor(out=x1[:rows, :], in0=x1[:rows, :], in1=xin[:rows, :], op=ALU.add), r=[kx1, kx], w=[kx1])
            S.dma("sp", lambda e, x1=x1, rows=rows, col0=col0: e.dma_start(out=Y[col0:col0 + rows, :], in_=x1[:rows, :]), r=[kx1], w=[("Yx1", blk)])
            norm_transpose(x1, kx1, rows, 1, H2T, [("H2T", blk)], col0, XN2r, JUNK2)
    if debug:
        dbg["H2T"] = dout("D_H2T", (128, 8, NT), BF16)
        S.dma("sp", lambda e: e.dma_start(out=dbg["H2T"], in_=H2T), r=[("H2T", b) for b in range(17)])

    S.barrier()
    cur[0] = mark_D2
    if "D2" in phases:
        WDNs = alloc("WDNs", (128, 32, 1024), BF16)
        UT = alloc("UT", (128, 32, 576), BF16)
        WUPr = Ring("WUP", [alloc("WUP%d" % i, (128, 2048), BF16) for i in range(3)])
        ABr = Ring("AB", [alloc("AB%d" % i, (128, 516), F32) for i in range(2)])
        CCr = Ring("CC", [alloc("CC%d" % i, (128, 512), F32) for i in range(2)])
        GLr = Ring("GL", [alloc("GL%d" % i, (128, 512), F32) for i in range(2)])
        CARRY = alloc("CARRY", (128, 32, 2), F32)
        STT = alloc("STT", (128, 32, 32), F32)
        TAILA = alloc("TAILA", (128, 32, 34), F32)
        SSTGr = Ring("SSTG", [alloc("SSTG%d" % i, (32, 1024), F32) for i in range(2)])
        YBr = Ring("YB", [alloc("YB%d" % i, (128, 1024), F32) for i in range(2)])
        X1Rr = Ring("X1R", [alloc("X1R%d" % i, (128, 1024), F32) for i in range(2)])
        JUNK3 = alloc("JUNK3", (128, 512), BF16)
        S.dve(lambda e: e.memset(CARRY, 0.0), w=["CARRY"])
        for j0 in range(0, 32, 8):
            sstg, ksstg = SSTGr.next()
            S.dma("sp", lambda e, sstg=sstg, j0=j0: e.dma_start(out=sstg, in_=STATE[:, j0 * 128:(j0 + 8) * 128]), w=[ksstg])
            ps, kps = next_bank()
            for j in range(j0, j0 + 8):
                S.pe(lambda e, ps=ps, j=j, j0=j0, sstg=sstg: e.transpose(ps[:, (j - j0) * 32:(j - j0 + 1) * 32], sstg[:, (j - j0) * 128:(j - j0 + 1) * 128], IDF[0:32, 0:32]),
                     r=[ksstg, "IDF"], w=[kps])
            S.act(lambda e, ps=ps, j0=j0: e.copy(out=STT[:, j0:j0 + 8, :].rearrange("p a b -> p (a b)"), in_=ps[:, 0:256]), r=[kps], w=["STT"])
        def up_part(ti, j, wu, kwu, c0, n, smp, u0):
            hk = [("H2T", b) for b in range(17)]
            (pA, kA), (pG, kG) = next_bank(), next_bank()
            for k in range(8):
                S.pe(lambda e, k=k: e.matmul(pA[:, 0:n], lhsT=wu[:, k * 256:k * 256 + 128], rhs=H2T[:, k, c0:c0 + n], start=(k == 0), stop=(k == 7)),
                     r=[kwu] + hk, w=[kA])
            for k in range(8):
                S.pe(lambda e, k=k: e.matmul(pG[:, 0:n], lhsT=wu[:, k * 256 + 128:k * 256 + 256], rhs=H2T[:, k, c0:c0 + n], start=(k == 0), stop=(k == 7)),
                     r=[kwu] + hk, w=[kG])
            ab, kab = ABr.next()
            cc, kcc = CCr.next()
            gl, kgl = GLr.next()
            sh = 16 if smp else 1
            hal = 2 * sh
            if smp:
                S.pool(lambda e: e.tensor_copy(out=ab[:, 0:32], in_=STT[:, j, :]), r=["STT"], w=[kab])
            else:
                S.pool(lambda e: e.tensor_copy(out=ab[:, 0:2], in_=CARRY[:, j, :]), r=["CARRY"], w=[kab])
            S.act(lambda e: e.copy(out=ab[:, hal:hal + n], in_=pA[:, 0:n]), r=[kA], w=[kab])
            if not smp:
                S.pool(lambda e: e.tensor_copy(out=CARRY[:, j, :], in_=ab[:, n:n + 2]), r=[kab], w=["CARRY"])
                if ti == 3:
                    S.pool(lambda e: e.tensor_copy(out=TAILA[:, j, 32:34], in_=ab[:, n:n + 2]), r=[kab], w=["TAILA"])
            else:
                S.pool(lambda e: e.tensor_copy(out=TAILA[:, j, 0:32], in_=ab[:, 32 + 32:32 + 64]), r=[kab], w=["TAILA"])
            S.act(lambda e: e.activation(out=cc[:, 0:n], in_=pA[:, 0:n], func=AF.Identity, scale=CW[:, j, 2:3], bias=CW[:, j, 3:4]),
                  r=[kA, "CW"], w=[kcc])
            S.dve(lambda e: e.scalar_tensor_tensor(out=cc[:, 0:n], in0=ab[:, sh:sh + n], scalar=CW[:, j, 1:2], in1=cc[:, 0:n], op0=ALU.mult, op1=ALU.add),
                  r=[kab, kcc, "CW"], w=[kcc])
            S.dve(lambda e: e.scalar_tensor_tensor(out=cc[:, 0:n], in0=ab[:, 0:n], scalar=CW[:, j, 0:1], in1=cc[:, 0:n], op0=ALU.mult, op1=ALU.add),
                  r=[kab, kcc, "CW"], w=[kcc])
            S.act(lambda e: e.activation(out=gl[:, 0:n], in_=cc[:, 0:n], func=AF.Gelu_apprx_tanh), r=[kcc], w=[kgl])
            S.dve(lambda e: e.tensor_tensor(out=UT[:, j, u0:u0 + n], in0=gl[:, 0:n], in1=pG[:, 0:n], op=ALU.mult), r=[kgl, kG], w=[("UT", j, u0)])

        def down_block(blk, rows, col0, u0):
            ph = [next_bank(), next_bank()]
            for half in range(2):
                pf, kpf = ph[half]
                for j in range(32):
                    S.pe(lambda e, pf=pf, j=j, half=half: e.matmul(pf[:rows, :], lhsT=UT[:, j, u0:u0 + rows], rhs=WDNs[:, j, half * 512:(half + 1) * 512],
                                                                  start=(j == 0), stop=(j == 31)), r=[("UT", j, 0 if u0 < 512 else 512), ("WDN", j // 8)], w=[kpf])
            st, kst = next_stat()
            for half in range(2):
                pf, kpf = ph[half]
                S.act(lambda e, pf=pf, half=half: e.activation(out=JUNK3[:rows, :], in_=pf[:rows, :], func=AF.Square, accum_out=st[:rows, half:half + 1]),
                      r=[kpf], w=[kst, "JUNK3"])
            S.dve(lambda e: e.tensor_tensor(out=st[:rows, 2:3], in0=st[:rows, 0:1], in1=st[:rows, 1:2], op=ALU.add), r=[kst], w=[kst])
            rms_rstd(st[:rows, 2:3], kst, st[:rows, 3:4], kst, 1024)
            x1r, kx1r = X1Rr.next()
            S.dma("sp", lambda e: e.dma_start(out=x1r[:rows, :], in_=Y[col0:col0 + rows, :]), r=[("Yx1", blk)], w=[kx1r])
            yb, kyb = YBr.next()
            for half in range(2):
                pf, kpf = ph[half]
                S.dve(lambda e, pf=pf, half=half: e.scalar_tensor_tensor(out=yb[:rows, half * 512:(half + 1) * 512], in0=pf[:rows, :], scalar=st[:rows, 3:4],
                                                                        in1=GP[:rows, 1, half * 512:(half + 1) * 512], op0=ALU.mult, op1=ALU.mult),
                      r=[kpf, kst, "GP"], w=[kyb])
            S.pool(lambda e: e.tensor_tensor(out=yb[:rows, :], in0=yb[:rows, :], in1=x1r[:rows, :], op=ALU.add), r=[kyb, kx1r], w=[kyb])
            S.dma("sp", lambda e: e.dma_start(out=Y[col0:col0 + rows, :], in_=yb[:rows, :]), r=[kyb, kx1r], w=[("Yx1", blk)])

        for ti in range(4):
            c0 = ti * 512
            for j in range(32):
                wu, kwu = WUPr.next()
                S.dma("pool", lambda e, wu=wu, j=j: e.dma_start(out=wu, in_=WUP[j]), w=[kwu])
                if ti == 0 and j in (3, 6, 9, 12):
                    q4 = (j - 3) // 3
                    S.dma("pool", lambda e, q4=q4: e.dma_start(out=WDNs[:, q4 * 8:(q4 + 1) * 8, :], in_=WDN[:, q4 * 8:(q4 + 1) * 8, :]), w=[("WDN", q4)])
                up_part(ti, j, wu, kwu, c0, 512, False, 0)
                if ti == 3:
                    up_part(ti, j, wu, kwu, SC0, 64, True, 512)
            for bl in range(4):
                down_block(ti * 4 + bl, 128, c0 + bl * 128, bl * 128)
            if ti == 3:
                down_block(16, 64, SC0, 512)
        for j0 in range(0, 32, 4):
            ps, kps = next_bank()
            for j in range(j0, j0 + 4):
                S.pe(lambda e, ps=ps, j=j, j0=j0: e.transpose(ps[0:34, (j - j0) * 128:(j - j0 + 1) * 128], TAILA[:, j, :], IDF), r=["TAILA", "IDF"], w=[kps])
            yb, kyb = YBr.next()
            S.act(lambda e, ps=ps, yb=yb: e.copy(out=yb[0:34, 0:512], in_=ps[0:34, :]), r=[kps], w=[kyb])
            S.dma("sp", lambda e, yb=yb, j0=j0: e.dma_start(out=NCONV[:, j0 * 128:(j0 + 4) * 128], in_=yb[0:34, 0:512]), r=[kyb])
    if os.environ.get("KVERB"):
        print("SBUF high water", hw[0], "of", SB_END, flush=True)
    S.emit()
    return nc, sorted(dbg.keys())


_CACHE = {}


def _get_nc(debug=False, phases=("A", "B", "D1", "D2")):
    key = (debug, tuple(phases))
    if key not in _CACHE:
        _CACHE[key] = build(debug, phases)
    return _CACHE[key]


def run_cores(inputs, debug=False, phases=("A", "B", "D1", "D2"), cores=range(8)):
    nc, dbgnames = _get_nc(debug, phases)
    sh = _prep_shared(inputs)
    in_maps = []
    for c in cores:
        m = dict(sh)
        m.update(_prep_core(inputs, c))
        in_maps.append(m)
    res = run_bass_kernel_spmd(nc, in_maps, core_ids=list(range(len(in_maps))))
    return res.results


def kernel(**inputs):
    inputs = {k: np.asarray(v) for k, v in inputs.items()}
    res = run_cores(inputs)
    yp = np.stack([r["Y"][0:2048] for r in res], 0)
    ys = np.concatenate([r["Y"][2048:2112].reshape(4, 16, 1024).transpose(1, 0, 2) for r in res], 0)
    outs = [yp.astype(np.float32), ys.astype(np.float32)]
    for (pn, sn, W, H) in (("NA_P", "NA_S", 128, 2), ("NB1_P", "NB1_S", 128, 4), ("NB2_P", "NB2_S", 512, 4), ("NB3_P", "NB3_S", 2048, 4)):
        outs.append(np.stack([r[pn] for r in res], 0).reshape(1, 8, W, 2, H, 64).astype(np.float32))
        outs.append(np.concatenate([r[sn] for r in res], 0).reshape(1, 128, W, 2, H, 64).astype(np.float32))
    outs.append(np.stack([r["NCONV"][32:34] for r in res], 0).reshape(1, 8, 2, 4096).astype(np.float32))
    outs.append(np.concatenate([r["NCONV"][0:32].reshape(2, 16, 4096).transpose(1, 0, 2) for r in res], 0).reshape(1, 128, 2, 4096).astype(np.float32))
    return tuple(outs)
```

```python
import os
import numpy as np
from contextlib import ExitStack
import concourse.bass as bass
import concourse.mybir as mybir
from concourse.bass_utils import run_bass_kernel_spmd

F32 = mybir.dt.float32
BF16 = mybir.dt.bfloat16
AF = mybir.ActivationFunctionType
ALU = mybir.AluOpType

ENGS = ("pe", "act", "dve", "pool", "sp")
NEG = -30000.0
NT = 2112
SC0 = 2048


class Op:
    __slots__ = ("eng", "fn", "reads", "writes", "dma", "bg", "deps", "flag", "semi", "semval", "idx", "prevdma")

    def __init__(self, eng, fn, reads, writes, dma, bg=False):
        self.eng, self.fn, self.reads, self.writes, self.dma, self.bg = eng, fn, reads, writes, dma, bg
        self.deps = []
        self.flag = False
        self.semi = None
        self.semval = 0
        self.prevdma = None


class _Rec:
    def __init__(self):
        self.call = None

    def __getattr__(self, name):
        def f(*a, **k):
            self.call = (name, a, k)
            return self
        return f


class Sched:
    def __init__(self, nc, n_dma_sems=8, same_engine_raw=True):
        self.nc = nc
        self.ops = []
        self.n_dma_sems = n_dma_sems
        self.same_engine_raw = same_engine_raw

    def add(self, eng, fn, reads=(), writes=(), dma=False, bg=False):
        rec = _Rec()
        fn(rec)
        call = rec.call
        assert call is not None
        op = Op(eng, call, tuple(reads), tuple(writes), dma, bg)
        op.idx = len(self.ops)
        self.ops.append(op)
        return op

    def pe(self, fn, r=(), w=()):
        return self.add("pe", fn, r, w)

    def act(self, fn, r=(), w=()):
        return self.add("act", fn, r, w)

    def dve(self, fn, r=(), w=()):
        return self.add("dve", fn, r, w)

    def pool(self, fn, r=(), w=()):
        return self.add("pool", fn, r, w)

    def dma(self, eng, fn, r=(), w=(), bg=False):
        return self.add(eng, fn, r, w, dma=True, bg=bg)

    def barrier(self):
        op = Op(None, None, (), (), False)
        op.idx = len(self.ops)
        self.ops.append(op)

    def analyze(self):
        last_w, readers = {}, {}
        for op in self.ops:
            if op.eng is None:
                continue
            deps = set()
            for r in op.reads:
                p = last_w.get(r)
                if p is not None:
                    deps.add(p)
            for w in op.writes:
                p = last_w.get(w)
                if p is not None:
                    deps.add(p)
                for q in readers.get(w, ()):
                    deps.add(q)
            for r in op.reads:
                readers.setdefault(r, []).append(op.idx)
            for w in op.writes:
                last_w[w] = op.idx
                readers[w] = []
            deps.discard(op.idx)
            op.deps = sorted(deps)

    @staticmethod
    def _cost(op):
        name, a, k = op.fn
        def fe(ap):
            n = 1
            for d in ap.shape[1:]:
                n *= d
            return n
        if op.dma:
            o = k.get("out")
            nb = fe(o) * o.shape[0] * 4
            busy = 1200.0 if op.eng == "pool" else 150.0
            return busy, 2500.0 + nb / 200.0
        if op.eng == "pe":
            if name == "transpose":
                return 160.0, 220.0
            n = fe(k["rhs"])
            c = 110.0 + n * 0.21
            return c, c + 60.0
        o = k.get("out")
        n = fe(o) if o is not None else 64
        if op.eng == "pool":
            c = 250.0 + n * 2.1
        else:
            c = 180.0 + n * 1.05
        return c, c

    def schedule(self, window=int(os.environ.get('KWIN', '3000'))):
        ops = self.ops
        segs, cur = [], []
        for op in ops:
            if op.eng is None:
                segs.append(cur)
                cur = []
            else:
                cur.append(op)
        segs.append(cur)
        order = []
        pos = {}
        fin = {}
        tnow = 0.0
        reorder = not os.environ.get("KNOSCHED")
        prev_last, prev_dmas = {}, []
        for seg in segs:
            if not seg:
                continue
            queues = {e: [op for op in seg if op.eng == e] for e in ENGS}
            heads = {e: 0 for e in ENGS}
            done = set()
            t_eng = {e: tnow for e in ENGS}
            first_of_seg = {}
            seg_order = []
            nleft = len(seg)
            segset = set(op.idx for op in seg)
            while nleft:
                best = None
                for e in ENGS:
                    q = queues[e]
                    cnt = 0
                    i = heads[e]
                    while i < len(q) and cnt < ((4096 if e == 'sp' else window) if reorder else 1):
                        op = q[i]
                        i += 1
                        if op.idx in done:
                            continue
                        cnt += 1
                        ok = True
                        rdy = t_eng[e]
                        for d in op.deps:
                            if d in segset and d not in done:
                                ok = False
                                break
                            f = fin.get(d, 0.0)
                            if f > rdy:
                                rdy = f
                        if not ok:
                            continue
                        if best is None or rdy < best[0] - 1e-9:
                            best = (rdy, e, op)
                        if rdy <= t_eng[e] + 1e-9:
                            break
                assert best is not None, "scheduler deadlock"
                rdy, e, op = best
                busy, lat = self._cost(op)
                fin[op.idx] = rdy + lat
                t_eng[e] = rdy + busy
                done.add(op.idx)
                q = queues[e]
                while heads[e] < len(q) and q[heads[e]].idx in done:
                    heads[e] += 1
                if e not in first_of_seg:
                    first_of_seg[e] = op
                seg_order.append(op)
                nleft -= 1
            for e, op in first_of_seg.items():
                extra = [p.idx for p in prev_last.values()] + prev_dmas
                op.deps = sorted(set(op.deps) | set(extra))
            last = {}
            for op in seg_order:
                if not op.dma:
                    last[op.eng] = op
            prev_last = last
            prev_dmas = [op.idx for op in seg_order if op.dma and not op.bg]
            order.extend(seg_order)
            tprev = tnow
            tnow = max([tnow] + [fin[op.idx] for op in seg_order])
            if os.environ.get("KVERB"):
                bz = {e: 0.0 for e in ENGS}
                for op in seg_order:
                    bz[op.eng] += self._cost(op)[0]
                print("SEG us %.1f -> %.1f" % (tprev / 1e3, tnow / 1e3), {e: int(v / 1e3) for e, v in bz.items()}, flush=True)
        if os.environ.get("KVERB"):
            print("SCHED est total us", tnow / 1e3, flush=True)
        return order

    def emit(self):
        nc = self.nc
        self.analyze()
        order = self.schedule()
        ops = self.ops
        for op in order:
            need = []
            for d in op.deps:
                p = ops[d]
                if p.dma or p.eng != op.eng or op.dma:
                    need.append(d)
                elif self.same_engine_raw and p.eng != "pe" and (set(p.writes) & set(op.reads)):
                    need.append(d)
            op.deps = need
            for d in need:
                ops[d].flag = True
        es = ExitStack()
        eng_sem = {e: es.enter_context(nc.semaphore("s_" + e)) for e in ENGS}
        dma_sems = {e: [es.enter_context(nc.semaphore("d_%s%d" % (e, i))) for i in range(self.n_dma_sems)]
                    for e in ("sp", "act", "pool")}
        cnt = {e: 0 for e in ENGS}
        dcnt = {e: 0 for e in dma_sems}
        dval = {e: [0] * self.n_dma_sems for e in dma_sems}
        dlast = {e: [None] * self.n_dma_sems for e in dma_sems}
        bg_final = {e: [] for e in ENGS}
        nbg = 0
        bgsems = [es.enter_context(nc.semaphore("bg%d" % i)) for i in range(2)]
        bgval = [0, 0]
        bglast = [None, None]
        for op in order:
            if op.dma and op.bg:
                k = nbg % 2
                nbg += 1
                bgval[k] += 16
                op.semi = bgsems[k]
                op.semval = bgval[k]
                op.prevdma = bglast[k]
                bglast[k] = op.idx
                op.flag = True
                bg_final[op.eng] = [(bgsems[i], bgval[i]) for i in range(2) if bgval[i]]
            elif op.dma:
                k = dcnt[op.eng] % self.n_dma_sems
                dcnt[op.eng] += 1
                dval[op.eng][k] += 16
                op.semi = dma_sems[op.eng][k]
                op.semval = dval[op.eng][k]
                op.prevdma = dlast[op.eng][k]
                dlast[op.eng][k] = op.idx
                op.flag = True
            elif op.flag:
                cnt[op.eng] += 1
                op.semi = eng_sem[op.eng]
                op.semval = cnt[op.eng]
        if os.environ.get("KVERB"):
            print("SEMCOUNTS", cnt, dval, "nops", len(ops), flush=True)
        per_eng = {e: [op for op in order if op.eng == e] for e in ENGS}
        final_dma = {e: [(dma_sems[e][k], dval[e][k]) for k in range(self.n_dma_sems) if dval[e][k]]
                     for e in dma_sems}

        def run(engname, eobj):
            seen = {}
            for op in per_eng[engname]:
                waits = {}
                dl = list(op.deps)
                if op.prevdma is not None:
                    dl.append(op.prevdma)
                for d in dl:
                    p = ops[d]
                    key = id(p.semi)
                    if key not in waits or waits[key][1] < p.semval:
                        waits[key] = (p.semi, p.semval)
                for key, (s, v) in waits.items():
                    if seen.get(key, 0) >= v:
                        continue
                    eobj.wait_ge(s, v)
                    seen[key] = v
                if op.fn is None:
                    continue
                name, a, k = op.fn
                inst = getattr(eobj, name)(*a, **k)
                if op.flag:
                    inst.then_inc(op.semi, 16 if op.dma else 1)
            for (s, v) in list(final_dma.get(engname, ())) + bg_final[engname]:
                if seen.get(id(s), 0) < v:
                    eobj.wait_ge(s, v)

        if os.environ.get("KSIM"):
            semv = {}
            pc = {e: 0 for e in ENGS}
            prog = {}
            for e in ENGS:
                seen, lst = {}, []
                for op in per_eng[e]:
                    waits = {}
                    dl = list(op.deps) + ([op.prevdma] if op.prevdma is not None else [])
                    for d in dl:
                        p = ops[d]
                        key = id(p.semi)
                        if key not in waits or waits[key] < p.semval:
                            waits[key] = p.semval
                    ws = [(k, v) for k, v in waits.items() if seen.get(k, 0) < v]
                    for k, v in ws:
                        seen[k] = v
                    lst.append((ws, (id(op.semi), 16 if op.dma else 1) if op.flag else None, op.idx))
                prog[e] = lst
            progress = True
            while progress:
                progress = False
                for e in ENGS:
                    while pc[e] < len(prog[e]):
                        ws, inc, idx = prog[e][pc[e]]
                        if all(semv.get(k, 0) >= v for k, v in ws):
                            if inc:
                                semv[inc[0]] = semv.get(inc[0], 0) + inc[1]
                            pc[e] += 1
                            progress = True
                        else:
                            break
            print("KSIM", {e: (pc[e], len(prog[e])) for e in ENGS}, flush=True)
            for e in ENGS:
                if pc[e] < len(prog[e]):
                    ws, inc, idx = prog[e][pc[e]]
                    print("  stuck", e, "op", idx, ops[idx].fn[0], ops[idx].reads, ops[idx].writes, flush=True)
        with nc.Block() as block:
            block.tensor(lambda e: run("pe", e))
            block.scalar(lambda e: run("act", e))
            block.vector(lambda e: run("dve", e))
            block.gpsimd(lambda e: run("pool", e))
            block.sync(lambda e: run("sp", e))
        es.close()


class Ring:
    def __init__(self, name, aps):
        self.name, self.aps, self.i = name, aps, 0

    def next(self):
        k = self.i % len(self.aps)
        self.i += 1
        return self.aps[k], (self.name, k)


def _slopes(n):
    return np.exp2(-8.0 * np.arange(1, n + 1, dtype=np.float64) / n)


def _band_tables(slopes, ds, maxd):
    k = np.arange(128)[:, None]
    q = np.arange(128)[None, :]
    out = np.full((128, 2, len(slopes), 128), NEG, np.float64)
    for kind in (0, 1):
        d = q - k + (128 if kind == 0 else 0)
        valid = (d >= 0) & (d <= maxd)
        for h, s in enumerate(slopes):
            out[:, kind, h, :] = np.where(valid, -s * ds * d, NEG)
    return out.astype(np.float32)


def _const_tables():
    sa, sb = _slopes(8), _slopes(12)
    ta = np.zeros((128, 2, 2, 4, 128), np.float32)
    for g in range(2):
        ta[:, g] = _band_tables([sa[i + 4 * g] for i in range(4)], 1.0, 127)
    tb = np.stack([_band_tables(sb[4 * g:4 * g + 4], float(ds), 128) for g, ds in enumerate((1, 4, 16))], 0)
    r = np.arange(128)[:, None]
    rn = np.arange(4)[:, None]
    t = np.arange(4)[None, :]
    ts = np.full((128, 80), NEG, np.float64)
    tn = np.full((4, 80), NEG, np.float64)
    for g in range(2):
        for i in range(4):
            s = sa[i + 4 * g]
            c0 = g * 16 + i * 4
            ts[:, c0:c0 + 4] = np.where(r >= t + 1, -s * (128 + t - r), NEG)
            tn[:, c0:c0 + 4] = np.where(rn <= t, -s * (t - rn), NEG)
    for gi, ds in ((0, 1.0), (1, 4.0), (2, 16.0)):
        base = 32 + gi * 16
        for j in range(2):
            for hh in range(2):
                s = sb[4 * gi + 2 * j + hh]
                cols = base + j * 8 + np.arange(4) * 2 + hh
                if gi == 0:
                    ts[:, cols] = np.where(r >= t, -s * (128 + t - r), NEG)
                    tn[:, cols] = np.where(rn <= t, -s * (t - rn), NEG)
                else:
                    ts[:, cols] = -s * ds * (128 - r) + 0.0 * t
                    tn[:, cols] = np.where(rn == t, 0.0, NEG)
    return ta, tb.astype(np.float32), ts.astype(np.float32), tn.astype(np.float32)


def _prep_shared(inp):
    f = lambda a: np.ascontiguousarray(a, dtype=np.float32)
    w_in = inp["w_in"][0]
    qa, ka, va = w_in[:, 0:512], w_in[:, 512:640], w_in[:, 640:768]
    qb, kb, vb = w_in[:, 768:1536], w_in[:, 1536:2304], w_in[:, 2304:3072]
    ga, gb = w_in[:, 3072:4096], w_in[:, 4096:5120]
    kp = lambda w: w.reshape(8, 128, -1).transpose(1, 0, 2)
    qperm = np.concatenate([np.r_[i * 64:(i + 1) * 64, (i + 4) * 64:(i + 5) * 64] for i in range(4)])
    sh = {}
    sh["WA"] = f(kp(np.concatenate([qa[:, qperm], ka, va], 1)))
    sh["WB"] = f(np.stack([kp(np.concatenate([qb[:, 256 * g:256 * g + 256], kb[:, 256 * g:256 * g + 256],
                                              vb[:, 256 * g:256 * g + 256]], 1)) for g in range(3)], 0))
    wba = inp["w_branch_a"][0][qperm]
    wbb = inp["w_branch_b"][0]
    wd1 = []
    for e in range(8):
        cs = slice(e * 128, (e + 1) * 128)
        g2 = kp(np.concatenate([ga[:, cs], gb[:, cs]], 1)).reshape(128, 2048)
        a2 = wba[:, cs].reshape(4, 128, 128).transpose(1, 0, 2).reshape(128, 512)
        b2 = wbb[:, cs].reshape(2, 128, 128).transpose(1, 0, 2).reshape(128, 256)
        wd1.append(np.concatenate([g2, a2, b2], 1))
    sh["WD1"] = f(np.stack(wd1, 0))
    sh["WOUT"] = f(kp(inp["w_out"][0]))
    w_up = inp["w_up"][0]
    sh["WUP"] = f(np.stack([kp(np.concatenate([w_up[:, j * 128:(j + 1) * 128],
                                               w_up[:, 4096 + j * 128:4096 + (j + 1) * 128]], 1)).reshape(128, 2048)
                            for j in range(32)], 0))
    sh["WDN"] = f(inp["w_down"][0].reshape(32, 128, 1024).transpose(1, 0, 2))
    gt = np.stack([inp["norm_mix_pre"][0].reshape(8, 128).T, inp["norm_ffn_pre"][0].reshape(8, 128).T], 1)
    sh["GT"] = f(gt)
    sh["GP"] = f(np.stack([np.broadcast_to(inp["norm_mix_post"][0], (128, 1024)),
                           np.broadcast_to(inp["norm_ffn_post"][0], (128, 1024))], 1))
    sk = inp["sinks_a"][0]
    sh["SK"] = f(np.stack([sk[0:4], sk[4:8]], 0).reshape(1, 8))
    cw = np.concatenate([inp["conv_w"][0], inp["conv_b"]], 0)
    sh["CW"] = f(cw.reshape(4, 32, 128).transpose(2, 1, 0))
    sh["IDENT"] = np.eye(128, dtype=np.float32)
    ta, tb, ts, tn = _const_tables()
    sh["TA"], sh["TB"], sh["TS"], sh["TSN"] = ta, tb, ts, tn
    return sh


def _prep_core(inp, c):
    f = lambda a: np.ascontiguousarray(a, dtype=np.float32)
    d = {}
    xs = inp["x_sample"][16 * c:16 * c + 16].transpose(1, 0, 2).reshape(64, 1024)
    d["X"] = f(np.concatenate([inp["x_prompt"][c], xs], 0))
    d["CA"] = f(inp["cache_a_kv"][0, 16 * c:16 * c + 16].reshape(16, 128, 256))
    d["CB1"] = f(inp["cache_b1_kv"][0, 16 * c:16 * c + 16].reshape(16, 128, 512))
    d["CB2"] = f(inp["cache_b2_kv"][0, 16 * c:16 * c + 16].reshape(16, 512, 512))
    d["CB3"] = f(inp["cache_b3_kv"][0, 16 * c:16 * c + 16].reshape(16, 2048, 512))
    d["STATE"] = f(inp["state_conv"][0, 16 * c:16 * c + 16].transpose(1, 0, 2).reshape(32, 4096))
    return d


import os
GROUPS = ("A", 0, 1, 2)
_PARTS = set(os.environ.get("KPARTS", "fm,tm,pa,sa").split(","))
_GSEL = os.environ.get("KGROUPS", "")


def build(debug=False, phases=("A", "B", "D1", "D2")):
    nc = bass.Bass("TRN2", target_bir_lowering=False)
    S = Sched(nc)
    din = lambda name, shape: nc.dram_tensor(name, list(shape), F32, kind="ExternalInput").ap()
    dout = lambda name, shape, dt=F32: nc.dram_tensor(name, list(shape), dt, kind="ExternalOutput").ap()

    X = din("X", (NT, 1024))
    CA, CB1, CB2, CB3 = din("CA", (16, 128, 256)), din("CB1", (16, 128, 512)), din("CB2", (16, 512, 512)), din("CB3", (16, 2048, 512))
    STATE = din("STATE", (32, 4096))
    WA, WB = din("WA", (128, 8, 768)), din("WB", (3, 128, 8, 768))
    WD1, WOUT = din("WD1", (8, 128, 2816)), din("WOUT", (128, 8, 1024))
    WUP, WDN = din("WUP", (32, 128, 2048)), din("WDN", (128, 32, 1024))
    GT_d, GP_d, SK_d, CW_d = din("GT", (128, 2, 8)), din("GP", (128, 2, 1024)), din("SK", (1, 8)), din("CW", (128, 32, 4))
    IDENT_d = din("IDENT", (128, 128))
    TA_d, TB_d, TS_d, TSN_d = din("TA", (128, 2, 2, 4, 128)), din("TB", (3, 128, 2, 4, 128)), din("TS", (128, 80)), din("TSN", (4, 80))

    Y = dout("Y", (NT, 1024))
    NKV_P = {"A": dout("NA_P", (128, 256)), 0: dout("NB1_P", (128, 512)), 1: dout("NB2_P", (512, 512)), 2: dout("NB3_P", (2048, 512))}
    NKV_S = {"A": dout("NA_S", (16, 128, 256)), 0: dout("NB1_S", (16, 128, 512)), 1: dout("NB2_S", (16, 512, 512)), 2: dout("NB3_S", (16, 2048, 512))}
    CACHE = {"A": CA, 0: CB1, 1: CB2, 2: CB3}
    WIN = {"A": 128, 0: 128, 1: 512, 2: 2048}
    NCONV = dout("NCONV", (34, 4096))
    dbg = {}

    SB_BASE, SB_END = 17408, 229376
    cur = [SB_BASE]

    hw = [0]

    def alloc(name, shape, dt):
        nbytes = int(np.prod(shape[1:])) * (2 if dt == BF16 else 4)
        off = (cur[0] + 31) // 32 * 32
        assert off + nbytes <= SB_END, ("SBUF overflow", name, off + nbytes)
        cur[0] = off + nbytes
        hw[0] = max(hw[0], cur[0])
        return nc.alloc_sbuf_tensor_at(name, list(shape), dt, offset=off).ap()

    GT = alloc("GT", (128, 2, 8), F32)
    GP = alloc("GP", (128, 2, 1024), F32)
    CW = alloc("CW", (128, 32, 4), F32)
    IDB = alloc("IDB", (128, 128), BF16)
    IDF = alloc("IDF", (128, 128), F32)
    ONES = alloc("ONES", (128, 128), BF16)
    SKR = alloc("SKR", (1, 8), F32)
    SKE = alloc("SKE", (1, 8), F32)
    ZROW = alloc("ZROW", (1, 128), F32)
    SINKROW = alloc("SINKROW", (1, 2, 4, 128), BF16)
    STAT = alloc("STAT", (128, 8, 4), F32)
    SINKS = alloc("SINKS", (1, 32), BF16)
    QBD = alloc("QBD", (128, 4, 64, 2), BF16)
    h2t_off = (cur[0] + 31) // 32 * 32
    H2T = alloc("H2T", (128, 8, NT), BF16)
    mark_D2 = cur[0]
    UB = nc.alloc_sbuf_tensor_at("UB", [128, 2, NT], F32, offset=h2t_off).ap()
    LB = nc.alloc_sbuf_tensor_at("LB", [128, 2, NT], F32, offset=h2t_off + 2 * NT * 4).ap()
    hT = alloc("hT", (128, 8, NT), BF16)
    OA = alloc("OA", (128, 4, NT), BF16)
    OB = alloc("OB", (128, 2, NT), BF16)
    PERSIST_END = cur[0]

    def scol(ap, n):
        if len(ap.shape) == 2:
            return ap.rearrange("p (t n) -> p n t", n=16)[:, n, :]
        return ap.rearrange("p c (t n) -> p c n t", n=16)[:, :, n, :]

    psum = [nc.alloc_psum_tensor("ps%d" % i, [128, 512], F32).ap() for i in range(8)]
    bank_i = [0]

    NPB = int(os.environ.get("KNPB", "2"))
    pbank_i = [0]
    mode = {"attn": False}

    def next_bank():
        if NPB and mode["attn"]:
            k = bank_i[0] % (8 - NPB)
            bank_i[0] += 1
            return psum[k], ("ps", k)
        k = bank_i[0] % 8
        bank_i[0] += 1
        return psum[k], ("ps", k)

    def next_bank_proj():
        if not NPB:
            return next_bank()
        k = 8 - NPB + pbank_i[0] % NPB
        pbank_i[0] += 1
        return psum[k], ("ps", k)

    stat_i = [0]

    def next_stat():
        k = stat_i[0] % 8
        stat_i[0] += 1
        return STAT[:, k, :], ("stat", k)

    S.dma("sp", lambda e: e.dma_start(out=GT, in_=GT_d), w=["GT"])
    S.dma("sp", lambda e: e.dma_start(out=GP, in_=GP_d), w=["GP"])
    S.dma("sp", lambda e: e.dma_start(out=CW, in_=CW_d), w=["CW"])
    S.dma("sp", lambda e: e.dma_start(out=IDF, in_=IDENT_d), w=["IDF"])
    S.dma("sp", lambda e: e.dma_start(out=SKR, in_=SK_d), w=["SKR"])
    S.dma("pool", lambda e: e.dma_start(out=IDB, in_=IDENT_d), w=["IDB"])
    S.dve(lambda e: e.memset(ONES, 1.0), w=["ONES"])
    S.dve(lambda e: e.memset(ZROW, 0.0), w=["ZROW"])
    S.dve(lambda e: e.memset(QBD, 0.0), w=["QBD"])
    S.act(lambda e: e.activation(out=SKE, in_=SKR, func=AF.Exp), r=["SKR"], w=["SKE"])
    for gi in range(8):
        S.dve(lambda e, gi=gi: e.tensor_scalar(out=SINKROW[0:1, gi // 4, gi % 4, :], in0=ZROW, scalar1=SKE[0:1, gi:gi + 1],
                                               scalar2=None, op0=ALU.add), r=["SKE", "ZROW"], w=["SINKROW"])
        S.dve(lambda e, gi=gi: e.tensor_scalar(out=SINKS[0:1, gi * 4:gi * 4 + 4], in0=ZROW[0:1, 0:4], scalar1=SKE[0:1, gi:gi + 1],
                                               scalar2=None, op0=ALU.add), r=["SKE", "ZROW"], w=["SINKROW"])

    def rms_rstd(ss_ap, key_ss, out_ap, key_out, n):
        S.dve(lambda e: e.tensor_scalar(out=out_ap, in0=ss_ap, scalar1=1.0 / n, scalar2=1e-6, op0=ALU.mult, op1=ALU.add),
              r=[key_ss], w=[key_out])
        S.act(lambda e: e.activation(out=out_ap, in_=out_ap, func=AF.Sqrt), r=[key_out], w=[key_out])
        S.dve(lambda e: e.reciprocal(out=out_ap, in_=out_ap), r=[key_out], w=[key_out])

    next_stat_parity = [0]

    def norm_transpose(src, src_key, rows, gidx, dst, dst_keys, col0, XNr, JUNK):
        st, kst = next_stat()
        xn, kxn = XNr.next()
        S.act(lambda e: e.activation(out=xn[:rows, :], in_=src[:rows, :], func=AF.Square, accum_out=st[:rows, 0:1]),
              r=[src_key], w=[kst, kxn])
        rms_rstd(st[:rows, 0:1], kst, st[:rows, 1:2], kst, 1024)
        S.dve(lambda e: e.tensor_scalar(out=xn[:rows, :], in0=src[:rows, :], scalar1=st[:rows, 1:2], scalar2=None, op0=ALU.mult),
              r=[src_key, kst], w=[kxn])
        ps, kps = next_bank()
        psb = ps.bitcast(BF16)
        for k in range(8):
            S.pe(lambda e, k=k: e.transpose(psb[:, k * 128:k * 128 + rows], xn[:rows, k * 128:(k + 1) * 128], IDB[:rows, :rows]),
                 r=[kxn, "IDB"], w=[kps])
        gbc = GT[:, gidx, :].unsqueeze(2).to_broadcast([128, 8, rows])
        src3 = psb.rearrange("p (k q) -> p k q", k=8)[:, :, 0:rows]
        if next_stat_parity[0] % 2 == 0:
            S.dve(lambda e: e.tensor_tensor(out=dst[:, :, col0:col0 + rows], in0=src3, in1=gbc, op=ALU.mult), r=[kps, "GT"], w=dst_keys)
        else:
            for hlf in range(2):
                S.act(lambda e, hlf=hlf: e.activation(out=dst[:, 4 * hlf:4 * hlf + 4, col0:col0 + rows], in_=src3[:, 4 * hlf:4 * hlf + 4, :], func=AF.Copy), r=[kps], w=dst_keys)
            S.dve(lambda e: e.tensor_tensor(out=dst[:, :, col0:col0 + rows], in0=dst[:, :, col0:col0 + rows], in1=gbc, op=ALU.mult), r=[kps, "GT"], w=dst_keys)
        next_stat_parity[0] += 2

    mark_A = cur[0]
    XOFF = {}
    for nm_, sz_ in (("XIN0", 4096), ("XIN1", 4096), ("XN0", 2048), ("XN1", 2048)):
        XOFF[nm_] = (cur[0] + 31) // 32 * 32
        cur[0] = XOFF[nm_] + sz_
    hw[0] = max(hw[0], cur[0])
    XINr = Ring("XIN", [nc.alloc_sbuf_tensor_at("XIN%d" % i, [128, 1024], F32, offset=XOFF["XIN%d" % i]).ap() for i in range(2)])
    XNr = Ring("XN", [nc.alloc_sbuf_tensor_at("XN%d" % i, [128, 1024], BF16, offset=XOFF["XN%d" % i]).ap() for i in range(2)])
    JUNK = None
    for blk in range(17):
        rows = 128 if blk < 16 else 64
        xin, kx = XINr.next()
        S.dma("sp", lambda e, xin=xin, blk=blk, rows=rows: e.dma_start(out=xin[:rows, :], in_=X[blk * 128:blk * 128 + rows, :]), w=[kx])
        norm_transpose(xin, kx, rows, 0, hT, [("hT", blk)], blk * 128, XNr, JUNK)

    WT = alloc("WT", (128, 8, 768), BF16)
    QT = alloc("QT", (128, 4, NT), BF16)
    KT = alloc("KT", (128, 2, NT), BF16)
    VE = alloc("VE", (128, 16, 256), BF16)
    TAB = alloc("TAB", (128, 2048), F32)
    TS = alloc("TS", (128, 80), F32)
    TSN = alloc("TSN", (4, 80), F32)
    fences = []
    TMPr = Ring("TMP", [alloc("TMP%d" % i, (128, 512), F32) for i in range(2)]
                + [nc.alloc_sbuf_tensor_at("TMP2", [128, 512], F32, offset=XOFF["XN0"]).ap()])
    fences.append((("TMP", 2), ("XN", 0), TMPr.aps[2]))
    PTr = Ring("PT", [alloc("PT%d" % i, (128, 512), BF16) for i in range(4)])
    STGr = Ring("STG", [alloc("STG%d" % i, (128, 512), F32) for i in range(3)])
    LRr = Ring("LR", [alloc("LR%d" % i, (128, 512), F32) for i in range(2)])
    CKr = Ring("CK", [alloc("CK%d" % i, (128, 4, 512), BF16) for i in range(3)]
               + [nc.alloc_sbuf_tensor_at("CK3", [128, 4, 512], BF16, offset=XOFF["XIN0"]).ap()])
    fences.append((("CK", 3), ("XIN", 0), CKr.aps[3]))
    KCTr = Ring("KCT", [alloc("KCT%d" % i, (128, 4, 2, 128), BF16) for i in range(2)]
                + [nc.alloc_sbuf_tensor_at("KCT%d" % (2 + i), [128, 4, 2, 128], BF16, offset=XOFF["XIN1"] + 2048 * i).ap() for i in range(2)])
    fences.append((("KCT", 2), ("XIN", 1), KCTr.aps[2]))
    fences.append((("KCT", 3), ("XIN", 1), KCTr.aps[3]))
    for (newk, oldk, ap_) in fences:
        S.dve(lambda e, ap_=ap_: e.memset(ap_.rearrange("p a b c -> p (a b c)")[:, 0:2] if len(ap_.shape) == 4 else (ap_.rearrange("p a b -> p (a b)")[:, 0:2] if len(ap_.shape) == 3 else ap_[:, 0:2]), 0.0),
              w=[newk, oldk])
    NKVr = Ring("NKV", [alloc("NKV%d" % i, (4, 256), BF16) for i in range(4)])
    TMPNr = Ring("TMPN", [alloc("TMPN%d" % i, (4, 32), F32) for i in range(2)])
    PTNr = Ring("PTN", [alloc("PTN%d" % i, (4, 32), BF16) for i in range(2)])

    S.dma("sp", lambda e: e.dma_start(out=TS, in_=TS_d), w=["TS"])
    S.dma("sp", lambda e: e.dma_start(out=TSN, in_=TSN_d), w=["TSN"])

    def hview(g, k):
        base = hT[:, k, 0:2048]
        if g in ("A", 0):
            return base.rearrange("p (b i) -> p b i", i=128)
        if g == 1:
            return base.rearrange("p (bb i r) -> p r bb i", r=4, i=128)
        return base.rearrange("p (i r) -> p r i", r=16)

    ticks = []

    tick_w = []

    def tick(wt=2.0):
        k = ("tick", len(ticks))
        ticks.append(k)
        tick_w.append(wt)
        return k

    def bg_shifts():
        pieces = []
        for g in GROUPS:
            W = WIN[g]
            if g == "A":
                pieces += [(g, 0, 8, 4, W), (g, 8, 16, 4, W)]
            elif g == 0:
                pieces += [(g, n0, n0 + 4, 4, W) for n0 in range(0, 16, 4)]
            elif g == 1:
                pieces += [(g, n0, n0 + 1, 4, W) for n0 in range(16)]
            else:
                for n0 in range(16):
                    pieces += [(g, n0, n0 + 1, 4 + 511 * q, 4 + 511 * (q + 1)) for q in range(4)]
        cum = np.cumsum(tick_w)
        for i, (g, n0, n1, r0, r1) in enumerate(pieces):
            tk = ticks[int(np.searchsorted(cum, (i + 0.3) * cum[-1] / len(pieces)))]
            S.dma("sp", lambda e, g=g, n0=n0, n1=n1, r0=r0, r1=r1: e.dma_start(out=NKV_S[g][n0:n1, r0 - 4:r1 - 4, :], in_=CACHE[g][n0:n1, r0:r1, :]),
                  r=[tk] + ([("bgc", i - 2)] if i >= 2 else []), w=[("bgc", i)], bg=True)

    def do_group(g):
        isA = g == "A"
        nq = 4 if isA else 2
        nk = 1 if isA else 2
        kvc0 = 512 if isA else 256
        ncols = 256 if isA else 512
        vc0 = 128 if isA else 256
        W = WIN[g]
        wsrc = WA if isA else WB[g]
        S.dma("pool", lambda e: e.dma_start(out=WT, in_=wsrc), w=["WT"])
        tsrc = TA_d if isA else TB_d[g]
        tn = 2048 if isA else 1024
        S.dma("sp", lambda e: e.dma_start(out=TAB[:, 0:tn], in_=tsrc.rearrange("p a b c d -> p (a b c d)") if isA
                                          else tsrc.rearrange("p a b c -> p (a b c)")), w=["TAB"])
        allh = [("hT", b) for b in range(17)]
        evi = [0]
        for (dst, dname, dch, wc0) in [(QT, "QT", i, i * 128) for i in range(nq)] + [(KT, "KT", i, nq * 128 + i * 128) for i in range(nk)]:
            for s in (range(int(os.environ.get("KFMLIM", "5"))) if "fm" in _PARTS else ()):
                n = 512 if s < 4 else 64
                ps, kps = next_bank_proj()
                for k in range(8):
                    rhs = hT[:, k, SC0:SC0 + 64] if s == 4 else hT[:, k, s * 512:(s + 1) * 512]
                    S.pe(lambda e, ps=ps, k=k, rhs=rhs, wc0=wc0, n=n: e.matmul(ps[:, 0:n], lhsT=WT[:, k, wc0:wc0 + 128], rhs=rhs,
                                                                             start=(k == 0), stop=(k == 7)),
                         r=["WT"] + allh, w=[kps])
                src = ps[:, 0:n]
                wkeys = [(dname, dch, s)]
                if s == 4:
                    o = dst[:, dch, SC0:SC0 + 64]
                elif g in ("A", 0):
                    o = dst[:, dch, s * 512:s * 512 + n]
                else:
                    rr = 4 if g == 1 else 16
                    mm = 512 // rr
                    o = dst[:, dch, 0:2048].rearrange("p (r m) -> p m r", r=rr)[:, mm * s:mm * (s + 1), :]
                    src = ps[:, 0:512].rearrange("p (m r) -> p m r", r=rr)
                    wkeys = [(dname, dch, q) for q in range(4)]
                if evi[0] % 2 == 0:
                    S.act(lambda e, o=o, src=src: e.copy(out=o, in_=src), r=[kps], w=wkeys)
                else:
                    S.dve(lambda e, o=o, src=src: e.tensor_copy(out=o, in_=src), r=[kps], w=wkeys)
                evi[0] += 1
        for kb in (range(int(os.environ.get("KTM0", "0")), int(os.environ.get("KTM1", "17"))) if "tm" in _PARTS else ()):
            rows = 128 if kb < 16 else 64
            ps, kps = next_bank_proj()
            need = kb == 16 or (g in ("A", 0) and kb == 15) or (g == 1 and kb % 4 == 3) or g == 2
            c_lo = 0 if need else vc0
            for k in range(8):
                if kb == 16:
                    lhsT = hT[:, k, SC0:SC0 + 64]
                elif g == 1:
                    lhsT = hview(g, k)[:, kb // 4, kb % 4, :]
                else:
                    lhsT = hview(g, k)[:, kb, :]
                S.pe(lambda e, ps=ps, k=k, lhsT=lhsT, rows=rows, c_lo=c_lo: e.matmul(ps[:rows, c_lo:ncols], lhsT=lhsT, rhs=WT[:, k, kvc0 + c_lo:kvc0 + ncols],
                                                                         start=(k == 0), stop=(k == 7)), r=["WT"] + allh, w=[kps])
            if kb < 16:
                S.act(lambda e, ps=ps, kb=kb: e.copy(out=VE[:, kb, 0:ncols - vc0], in_=ps[:, vc0:ncols]), r=[kps], w=[("VE", kb), tick(1.0)])
            if need:
                stg, kstg = STGr.next()
                if True:
                    S.act(lambda e, ps=ps, stg=stg, rows=rows: e.copy(out=stg[:rows, 0:ncols], in_=ps[:rows, 0:ncols]), r=[kps], w=[kstg])
                else:
                    S.dve(lambda e, ps=ps, stg=stg, rows=rows: e.tensor_copy(out=stg[:rows, 0:ncols], in_=ps[:rows, 0:ncols]), r=[kps], w=[kstg])
                if kb == 16:
                    for t in range(4):
                        S.dma(os.environ.get("KSTQ", "act"), lambda e, stg=stg, t=t: e.dma_start(out=NKV_S[g][:, W - 4 + t, :], in_=stg[16 * t:16 * t + 16, 0:ncols]),
                              r=[kstg], w=[("newrows", g, t)])
                else:
                    if g in ("A", 0):
                        dstd = NKV_P[g]
                    elif g == 1:
                        dstd = NKV_P[g].rearrange("(i r) c -> r i c", r=4)[kb // 4]
                    else:
                        dstd = NKV_P[g].rearrange("(i r) c -> r i c", r=16)[kb]
                    S.dma(os.environ.get("KSTQ", "act"), lambda e, stg=stg, dstd=dstd: e.dma_start(out=dstd, in_=stg[:, 0:ncols]), r=[kstg])
        for b in (range(16) if "pa" in _PARTS else ()):
            if g in ("A", 0):
                has_prev = b > 0
            elif g == 1:
                has_prev = b % 4 > 0
            else:
                has_prev = False
            kinds = ([(b - 1, 0)] if has_prev else []) + [(b, 1)]
            psU, kU = next_bank()
            psL, kL = next_bank()
            qk = [("QT", i, b // 4) for i in range(nq)]
            if isA:
                for gg in range(2):
                    pts = []
                    for (kb, kind) in kinds:
                        psS, kS = next_bank()
                        S.pe(lambda e, psS=psS, gg=gg, kb=kb: e.matmul(psS.rearrange("p (i q) -> p i q", i=4), lhsT=KT[64 * gg:64 * gg + 64, 0, kb * 128:(kb + 1) * 128],
                                                                       rhs=QT[64 * gg:64 * gg + 64, 0:4, b * 128:(b + 1) * 128], start=True, stop=True),
                             r=[("KT", 0, kb // 4)] + qk, w=[kS])
                        tmp, kT = TMPr.next()
                        tab = TAB[:, (gg * 2 + kind) * 512:(gg * 2 + kind + 1) * 512]
                        S.dve(lambda e, tmp=tmp, psS=psS, tab=tab: e.scalar_tensor_tensor(out=tmp, in0=psS, scalar=0.125, in1=tab, op0=ALU.mult, op1=ALU.add),
                              r=[kS, "TAB"], w=[kT])
                        pt, kP = PTr.next()
                        S.act(lambda e, pt=pt, tmp=tmp: e.activation(out=pt, in_=tmp, func=AF.Exp), r=[kT], w=[kP] + ([tick(5.0)] if (kind == 1 and gg == 0) else []))
                        pts.append((pt, kP, kb))
                    for idx, (pt, kP, kb) in enumerate(pts):
                        S.pe(lambda e, pt=pt, gg=gg, idx=idx: e.matmul(psL[64 * gg:64 * gg + 64, :], lhsT=ONES[:, 0:64], rhs=pt, start=(idx == 0), stop=False),
                             r=[kP, "ONES"], w=[kL])
                    S.pe(lambda e, gg=gg: e.matmul(psL[64 * gg:64 * gg + 64, :], lhsT=ONES[0:1, 0:64], rhs=SINKROW[0:1, gg].rearrange("p a b -> p (a b)"),
                                                   start=False, stop=True), r=["SINKROW", "ONES"], w=[kL])
                    for idx, (pt, kP, kb) in enumerate(pts):
                        S.pe(lambda e, pt=pt, gg=gg, kb=kb, idx=idx, n=len(pts): e.matmul(
                            psU[64 * gg:64 * gg + 64, :], lhsT=VE[:, kb, gg * 64:gg * 64 + 64], rhs=pt,
                            start=(idx == 0), stop=(idx == n - 1)), r=[kP, ("VE", kb)], w=[kU])
                lr, kLr = LRr.next()
                S.act(lambda e, lr=lr: e.activation(out=lr, in_=psL, func=AF.Ln), r=[kL], w=[kLr])
                S.act(lambda e, lr=lr: e.activation(out=lr, in_=lr, func=AF.Exp, scale=-1.0), r=[kLr], w=[kLr])
                S.dve(lambda e, lr=lr: e.tensor_tensor(out=OA[:, :, b * 128:(b + 1) * 128], in0=psU.rearrange("p (i q) -> p i q", i=4),
                                                       in1=lr.rearrange("p (i q) -> p i q", i=4), op=ALU.mult), r=[kU, kLr], w=[("OA", b)])
            else:
                pts = []
                for (kb, kind) in kinds:
                    tmp, kT = TMPr.next()
                    for hh in range(2):
                        psS, kS = next_bank()
                        for j in range(2):
                            S.pe(lambda e, psS=psS, hh=hh, j=j, kb=kb: e.matmul(psS[:, j * 128:(j + 1) * 128], lhsT=KT[64 * hh:64 * hh + 64, j, kb * 128:(kb + 1) * 128],
                                                                               rhs=QT[64 * hh:64 * hh + 64, j, b * 128:(b + 1) * 128], start=True, stop=True),
                                 r=[("KT", j, kb // 4)] + qk, w=[kS])
                        tab = TAB[:, kind * 512:(kind + 1) * 512].rearrange("p (j hh q) -> p j hh q", j=2, hh=2)[:, :, hh, :]
                        tv = tmp.rearrange("p (j hh q) -> p j hh q", j=2, hh=2)[:, :, hh, :]
                        S.dve(lambda e, tv=tv, psS=psS, tab=tab: e.scalar_tensor_tensor(out=tv, in0=psS[:, 0:256].rearrange("p (j q) -> p j q", j=2), scalar=0.125, in1=tab,
                                                                                      op0=ALU.mult, op1=ALU.add), r=[kS, "TAB"], w=[kT])
                    pt, kP = PTr.next()
                    S.act(lambda e, pt=pt, tmp=tmp: e.activation(out=pt, in_=tmp, func=AF.Exp), r=[kT], w=[kP] + ([tick(5.0)] if kind == 1 else []))
                    pts.append((pt, kP, kb))
                for idx, (pt, kP, kb) in enumerate(pts):
                    S.pe(lambda e, pt=pt, idx=idx, n=len(pts): e.matmul(psL, lhsT=ONES, rhs=pt, start=(idx == 0), stop=(idx == n - 1)), r=[kP, "ONES"], w=[kL])
                for h in range(4):
                    hh, j = h % 2, h // 2
                    for idx, (pt, kP, kb) in enumerate(pts):
                        S.pe(lambda e, pt=pt, h=h, hh=hh, j=j, kb=kb, idx=idx, n=len(pts): e.matmul(
                            psU[64 * hh:64 * hh + 64, j * 128:(j + 1) * 128], lhsT=VE[:, kb, h * 64:h * 64 + 64], rhs=pt[:, h * 128:(h + 1) * 128],
                            start=(idx == 0), stop=(idx == n - 1)), r=[kP, ("VE", kb)], w=[kU])
                if g == 0:
                    uv = UB[:, :, b * 128:(b + 1) * 128]
                    lv = LB[:, :, b * 128:(b + 1) * 128]
                elif g == 1:
                    uv = UB[:, :, 0:2048].rearrange("p j (bb i r) -> p j r bb i", r=4, i=128)[:, :, b // 4, b % 4, :]
                    lv = LB[:, :, 0:2048].rearrange("p j (bb i r) -> p j r bb i", r=4, i=128)[:, :, b // 4, b % 4, :]
                else:
                    uv = UB[:, :, 0:2048].rearrange("p j (i r) -> p j r i", r=16)[:, :, b, :]
                    lv = LB[:, :, 0:2048].rearrange("p j (i r) -> p j r i", r=16)[:, :, b, :]
                pu = psU[:, 0:256].rearrange("p (j q) -> p j q", j=2)
                wk, rk = ("UB", g), ([("UB", g - 1)] if g > 0 else [])
                if os.environ.get("KNOEVAC"):
                    continue
                if g == 0:
                    S.dve(lambda e, uv=uv, pu=pu: e.tensor_copy(out=uv, in_=pu), r=[kU] + rk, w=[wk])
                else:
                    S.dve(lambda e, uv=uv, pu=pu: e.tensor_tensor(out=uv, in0=uv, in1=pu, op=ALU.add), r=[kU] + rk, w=[wk])
                for hh in range(2):
                    pl = psL.rearrange("p (j hh q) -> p j hh q", j=2, hh=2)[64 * hh:64 * hh + 64, :, hh, :]
                    lvv = lv[64 * hh:64 * hh + 64]
                    if g == 0:
                        S.dve(lambda e, lvv=lvv, pl=pl: e.tensor_copy(out=lvv, in_=pl), r=[kL] + rk, w=[wk])
                    else:
                        S.dve(lambda e, lvv=lvv, pl=pl: e.tensor_tensor(out=lvv, in0=lvv, in1=pl, op=ALU.add), r=[kL] + rk, w=[wk])

        ntile = 1 if g in ("A", 0) else 4
        nc_ = 32 if isA else 16
        ts0 = {"A": 0, 0: 32, 1: 48, 2: 64}[g]
        nkc = 1 if isA else 2
        qs = [("QT", i, 4) for i in range(nq)]
        S.act(lambda e: e.copy(out=QBD[0:64, 0:nq, :, 0], in_=QT[0:64, 0:nq, SC0:NT]), r=qs, w=["QBD"])
        S.dve(lambda e: e.tensor_copy(out=QBD[64:128, 0:nq, :, 1], in_=QT[64:128, 0:nq, SC0:NT]), r=qs, w=["QBD"])
        for n in (range(16) if "sa" in _PARTS else ()):
            ck, kck = CKr.next()
            if g in ("A", 0):
                src = CACHE[g][n]
                S.dma("pool", lambda e, ck=ck, src=src: e.dma_start(out=ck[:, 0, 0:ncols], in_=src), w=[kck])
            elif g == 1:
                src = CACHE[g][n].rearrange("(m r) c -> m r c", r=4)
                S.dma("pool", lambda e, ck=ck, src=src: e.dma_start(out=ck, in_=src), w=[kck])
            else:
                src = CACHE[g][n].rearrange("(m r) c -> m r c", r=16)[:, 0:4, :]
                S.dma("pool", lambda e, ck=ck, src=src: e.dma_start(out=ck, in_=src), w=[kck])
            nkv, knkv = NKVr.next()
            S.dma("pool", lambda e, nkv=nkv, n=n: e.dma_start(out=nkv[:, 0:ncols - vc0], in_=NKV_S[g][n, W - 4:W, vc0:ncols]),
                  r=[("newrows", g, t) for t in range(4)], w=[knkv])
            kct, kkct = KCTr.next()
            psT, kpsT = next_bank()
            psTb = psT.bitcast(BF16)
            for tl in range(ntile):
                for c in range(nkc):
                    S.pe(lambda e, tl=tl, c=c, ck=ck: e.transpose(psTb[:, (tl * 2 + c) * 128:(tl * 2 + c + 1) * 128], ck[:, tl, c * 128:(c + 1) * 128], IDB),
                         r=[kck, "IDB"], w=[kpsT])
            ncp = 128 if isA else ntile * 256
            S.act(lambda e, kct=kct: e.copy(out=kct.rearrange("p a b c -> p (a b c)")[:, 0:ncp], in_=psTb[:, 0:ncp]), r=[kpsT], w=[kkct, tick(6.0)])
            psS, kS = next_bank()
            if isA:
                rhs = QBD[:, 0:4, :, :].rearrange("p i (t n) g -> p n g i t", n=16)[:, n]
                S.pe(lambda e, rhs=rhs, kct=kct: e.matmul(psS[:, 0:32].rearrange("p (g i t) -> p g i t", g=2, i=4), lhsT=kct[:, 0, 0, :], rhs=rhs, start=True, stop=True),
                     r=[kkct, "QBD"], w=[kS])
                S.pe(lambda e, rhs=rhs: e.matmul(psS[0:4, 256:288].rearrange("p (g i t) -> p g i t", g=2, i=4), lhsT=scol(KT[:, 0, SC0:NT], n), rhs=rhs, start=True, stop=True),
                     r=[("KT", 0, 4), "QBD"], w=[kS])
            else:
                for j in range(2):
                    rhs = QBD[:, j, :, :].rearrange("p (t n) h -> p n t h", n=16)[:, n]
                    if ntile == 1:
                        S.pe(lambda e, j=j, rhs=rhs, kct=kct: e.matmul(psS[:, j * 8:j * 8 + 8].rearrange("p (t h) -> p t h", h=2), lhsT=kct[:, 0, j, :], rhs=rhs, start=True, stop=True),
                             r=[kkct, "QBD"], w=[kS])
                    else:
                        for t in range(4):
                            S.pe(lambda e, j=j, t=t, rhs=rhs, kct=kct: e.matmul(psS[:, j * 8 + t * 2:j * 8 + t * 2 + 2], lhsT=kct[:, t, j, :], rhs=rhs[:, t, :], start=True, stop=True),
                                 r=[kkct, "QBD"], w=[kS])
                    S.pe(lambda e, j=j, rhs=rhs: e.matmul(psS[0:4, 256 + j * 8:256 + j * 8 + 8].rearrange("p (t h) -> p t h", h=2), lhsT=scol(KT[:, j, SC0:NT], n), rhs=rhs, start=True, stop=True),
                         r=[("KT", j, 4), "QBD"], w=[kS])
            tmp, kT = TMPr.next()
            S.dve(lambda e, tmp=tmp, psS=psS: e.scalar_tensor_tensor(out=tmp[:, 0:nc_], in0=psS[:, 0:nc_], scalar=0.125, in1=TS[:, ts0:ts0 + nc_], op0=ALU.mult, op1=ALU.add),
                  r=[kS, "TS"], w=[kT])
            pt, kP = PTr.next()
            S.act(lambda e, pt=pt, tmp=tmp: e.activation(out=pt[:, 0:nc_], in_=tmp[:, 0:nc_], func=AF.Exp), r=[kT], w=[kP])
            tmpn, kTn = TMPNr.next()
            S.dve(lambda e, tmpn=tmpn, psS=psS: e.scalar_tensor_tensor(out=tmpn[:, 0:nc_], in0=psS[0:4, 256:256 + nc_], scalar=0.125, in1=TSN[:, ts0:ts0 + nc_], op0=ALU.mult, op1=ALU.add),
                  r=[kS, "TSN"], w=[kTn])
            ptn, kPn = PTNr.next()
            S.act(lambda e, ptn=ptn, tmpn=tmpn: e.activation(out=ptn[:, 0:nc_], in_=tmpn[:, 0:nc_], func=AF.Exp), r=[kTn], w=[kPn])
            psU, kU = next_bank()
            psL = psU[:, 256:512]
            S.pe(lambda e, pt=pt: e.matmul(psL[:, 0:nc_], lhsT=ONES, rhs=pt[:, 0:nc_], start=True, stop=False), r=[kP, "ONES"], w=[kU])
            S.pe(lambda e, ptn=ptn: e.matmul(psL[:, 0:nc_], lhsT=ONES[0:4, :], rhs=ptn[:, 0:nc_], start=False, stop=(not isA)), r=[kPn, "ONES"], w=[kU])
            if isA:
                S.pe(lambda e: e.matmul(psL[:, 0:nc_], lhsT=ONES[0:1, :], rhs=SINKS, start=False, stop=True), r=["SINKROW", "ONES"], w=[kU])
                S.pe(lambda e, ck=ck, pt=pt: e.matmul(psU[:, 0:32], lhsT=ck[:, 0, 128:256], rhs=pt[:, 0:32], start=True, stop=False), r=[kP, kck], w=[kU])
                S.pe(lambda e, nkv=nkv, ptn=ptn: e.matmul(psU[:, 0:32], lhsT=nkv[:, 0:128], rhs=ptn[:, 0:32], start=False, stop=True), r=[kPn, knkv], w=[kU])
                lr, kLr = LRr.next()
                for gg in range(2):
                    S.dve(lambda e, lr=lr, gg=gg: e.reciprocal(out=lr[64 * gg:64 * gg + 64, 0:16], in_=psL[64 * gg:64 * gg + 64, gg * 16:gg * 16 + 16]), r=[kU], w=[kLr])
                for gg in range(2):
                    S.dve(lambda e, lr=lr, n=n, gg=gg: e.tensor_tensor(out=scol(OA[64 * gg:64 * gg + 64, :, SC0:NT], n), in0=psU[64 * gg:64 * gg + 64, gg * 16:gg * 16 + 16].rearrange("p (i t) -> p i t", i=4),
                                                                   in1=lr[64 * gg:64 * gg + 64, 0:16].rearrange("p (i t) -> p i t", i=4), op=ALU.mult), r=[kU, kLr], w=[("OA", 16)])
            else:
                for j in range(2):
                    vj = slice(256 + j * 128, 256 + (j + 1) * 128)
                    vn = slice(j * 128, (j + 1) * 128)
                    if ntile == 1:
                        S.pe(lambda e, j=j, ck=ck, pt=pt, vj=vj: e.matmul(psU[:, j * 8:j * 8 + 8], lhsT=ck[:, 0, vj], rhs=pt[:, j * 8:j * 8 + 8], start=True, stop=False), r=[kP, kck], w=[kU])
                        S.pe(lambda e, j=j, nkv=nkv, ptn=ptn, vn=vn: e.matmul(psU[:, j * 8:j * 8 + 8], lhsT=nkv[:, vn], rhs=ptn[:, j * 8:j * 8 + 8], start=False, stop=True), r=[kPn, knkv], w=[kU])
                    else:
                        S.pe(lambda e, j=j, nkv=nkv, ptn=ptn, vn=vn: e.matmul(psU[:, j * 8:j * 8 + 8], lhsT=nkv[:, vn], rhs=ptn[:, j * 8:j * 8 + 8], start=True, stop=False, skip_group_check=True), r=[kPn, knkv], w=[kU])
                        for t in range(4):
                            S.pe(lambda e, j=j, t=t, ck=ck, pt=pt, vj=vj: e.matmul(psU[:, j * 8 + t * 2:j * 8 + t * 2 + 2], lhsT=ck[:, t, vj], rhs=pt[:, j * 8 + t * 2:j * 8 + t * 2 + 2],
                                                                                 start=False, stop=True, skip_group_check=True), r=[kP, kck], w=[kU])
                wk, rk = ("UB", g), ([("UB", g - 1)] if g > 0 else [])
                for hh in range(2):
                    hs = slice(64 * hh, 64 * hh + 64)
                    pu = psU[hs, 0:16].rearrange("p (j t h) -> p j t h", j=2, h=2)[:, :, :, hh]
                    pl = psL[hs, 0:16].rearrange("p (j t h) -> p j t h", j=2, h=2)[:, :, :, hh]
                    uv = scol(UB[hs, :, SC0:NT], n)
                    lvv = scol(LB[hs, :, SC0:NT], n)
                    if g == 0:
                        S.dve(lambda e, uv=uv, pu=pu: e.tensor_copy(out=uv, in_=pu), r=[kU] + rk, w=[wk])
                        S.dve(lambda e, lvv=lvv, pl=pl: e.tensor_copy(out=lvv, in_=pl), r=[kU] + rk, w=[wk])
                    else:
                        S.dve(lambda e, uv=uv, pu=pu: e.tensor_tensor(out=uv, in0=uv, in1=pu, op=ALU.add), r=[kU] + rk, w=[wk])
                        S.dve(lambda e, lvv=lvv, pl=pl: e.tensor_tensor(out=lvv, in0=lvv, in1=pl, op=ALU.add), r=[kU] + rk, w=[wk])

    if "B" in phases:
        mode["attn"] = True
        for g in GROUPS:
            if _GSEL and str(g) not in _GSEL.split(","):
                continue
            do_group(g)
        mode["attn"] = False
        bg_shifts()
        for s in range(5):
            c0, n = (s * 512, 512) if s < 4 else (SC0, 64)
            for j in range(2):
                lr, kLr = LRr.next()
                S.act(lambda e, lr=lr, j=j, c0=c0, n=n: e.activation(out=lr[:, 0:n], in_=LB[:, j, c0:c0 + n], func=AF.Ln), r=[("UB", 2)], w=[kLr])
                S.act(lambda e, lr=lr, n=n: e.activation(out=lr[:, 0:n], in_=lr[:, 0:n], func=AF.Exp, scale=-1.0), r=[kLr], w=[kLr])
                S.dve(lambda e, lr=lr, j=j, c0=c0, n=n: e.tensor_tensor(out=OB[:, j, c0:c0 + n], in0=UB[:, j, c0:c0 + n], in1=lr[:, 0:n], op=ALU.mult),
                      r=[("UB", 2), kLr], w=[("OB", s)])
    if debug:
        dbg["hT"] = dout("D_hT", (128, 8, NT), BF16)
        dbg["OA"] = dout("D_OA", (128, 4, NT), BF16)
        dbg["OB"] = dout("D_OB", (128, 2, NT), BF16)
        S.dma("sp", lambda e: e.dma_start(out=dbg["hT"], in_=hT), r=[("hT", b) for b in range(17)])
        S.dma("sp", lambda e: e.dma_start(out=dbg["OA"], in_=OA), r=[("OA", b) for b in range(17)])
        S.dma("sp", lambda e: e.dma_start(out=dbg["OB"], in_=OB), r=[("OB", s) for s in range(5)])

    TILES = [(0, 512), (512, 512), (1024, 512), (1536, 512), (SC0, 64)]
    if os.environ.get("KVERB"):
        print("SBUF end of phase B", cur[0], flush=True)
    S.barrier()
    cur[0] = PERSIST_END
    if "D1" in phases:
        MIXT = alloc("MIXT", (128, 8, NT), BF16)
        WOUTs = alloc("WOUTs", (128, 8, 1024), BF16)
        WD1r = Ring("WD1", [alloc("WD1_%d" % i, (128, 2816), BF16) for i in range(2)])
        SGr = Ring("SG", [alloc("SG%d" % i, (128, 512), F32) for i in range(4)])
        XIN2r = Ring("XIN2", [alloc("XIN2_%d" % i, (128, 1024), F32) for i in range(3)])
        X1r = Ring("X1", [alloc("X1_%d" % i, (128, 1024), F32) for i in range(3)])
        XN2r = Ring("XN2", [alloc("XN2_%d" % i, (128, 1024), BF16) for i in range(3)])
        JUNK2 = alloc("JUNK2", (128, 512), BF16)
        for ei in range(8):
            wd, kwd = WD1r.next()
            S.dma("pool", lambda e, wd=wd, ei=ei: e.dma_start(out=wd, in_=WD1[ei]), w=[kwd])
            if ei == 2:
                S.dma("pool", lambda e: e.dma_start(out=WOUTs, in_=WOUT), w=["WOUT"])
            for ti, (c0, n) in enumerate(TILES):
                hk = [("hT", b) for b in range(17)]
                banks = [next_bank() for _ in range(4)]
                (pGA, kGA), (pGB, kGB), (pYA, kYA), (pYB, kYB) = banks
                for k in range(8):
                    S.pe(lambda e, wd=wd, k=k, c0=c0, n=n, pGA=pGA: e.matmul(pGA[:, 0:n], lhsT=wd[:, k * 256:k * 256 + 128], rhs=hT[:, k, c0:c0 + n], start=(k == 0), stop=(k == 7)),
                         r=[kwd] + hk, w=[kGA])
                for k in range(8):
                    S.pe(lambda e, wd=wd, k=k, c0=c0, n=n, pGB=pGB: e.matmul(pGB[:, 0:n], lhsT=wd[:, k * 256 + 128:k * 256 + 256], rhs=hT[:, k, c0:c0 + n], start=(k == 0), stop=(k == 7)),
                         r=[kwd] + hk, w=[kGB])
                for i in range(4):
                    S.pe(lambda e, wd=wd, i=i, c0=c0, n=n, pYA=pYA: e.matmul(pYA[:, 0:n], lhsT=wd[:, 2048 + i * 128:2048 + (i + 1) * 128], rhs=OA[:, i, c0:c0 + n], start=(i == 0), stop=(i == 3)),
                         r=[kwd] + [("OA", b) for b in range(17)], w=[kYA])
                for j in range(2):
                    S.pe(lambda e, wd=wd, j=j, c0=c0, n=n, pYB=pYB: e.matmul(pYB[:, 0:n], lhsT=wd[:, 2560 + j * 128:2560 + (j + 1) * 128], rhs=OB[:, j, c0:c0 + n], start=(j == 0), stop=(j == 1)),
                         r=[kwd] + [("OB", s) for s in range(5)], w=[kYB])
                sa, ksa = SGr.next()
                sb_, ksb = SGr.next()
                S.act(lambda e, sa=sa, pGA=pGA, n=n: e.activation(out=sa[:, 0:n], in_=pGA[:, 0:n], func=AF.Sigmoid), r=[kGA], w=[ksa])
                S.act(lambda e, sb_=sb_, pGB=pGB, n=n: e.activation(out=sb_[:, 0:n], in_=pGB[:, 0:n], func=AF.Sigmoid), r=[kGB], w=[ksb])
                S.dve(lambda e, sa=sa, pYA=pYA, n=n: e.tensor_tensor(out=sa[:, 0:n], in0=sa[:, 0:n], in1=pYA[:, 0:n], op=ALU.mult), r=[ksa, kYA], w=[ksa])
                S.dve(lambda e, sb_=sb_, pYB=pYB, n=n: e.tensor_tensor(out=sb_[:, 0:n], in0=sb_[:, 0:n], in1=pYB[:, 0:n], op=ALU.mult), r=[ksb, kYB], w=[ksb])
                S.pool(lambda e, sa=sa, sb_=sb_, ei=ei, c0=c0, n=n: e.tensor_tensor(out=MIXT[:, ei, c0:c0 + n], in0=sa[:, 0:n], in1=sb_[:, 0:n], op=ALU.add),
                       r=[ksa, ksb], w=[("MIXT", ti)])
        for blk in range(17):
            rows = 128 if blk < 16 else 64
            col0 = blk * 128
            ti = blk // 4
            ph = [next_bank(), next_bank()]
            for half in range(2):
                pm, kpm = ph[half]
                for k in range(8):
                    S.pe(lambda e, pm=pm, k=k, half=half, rows=rows, col0=col0: e.matmul(pm[:rows, :], lhsT=MIXT[:, k, col0:col0 + rows], rhs=WOUTs[:, k, half * 512:(half + 1) * 512],
                                                                                      start=(k == 0), stop=(k == 7)), r=[("MIXT", ti), "WOUT"], w=[kpm])
            st, kst = next_stat()
            for half in range(2):
                pm, kpm = ph[half]
                S.act(lambda e, pm=pm, half=half, rows=rows, st=st: e.activation(out=JUNK2[:rows, 0:512], in_=pm[:rows, :], func=AF.Square, accum_out=st[:rows, half:half + 1]),
                      r=[kpm], w=[kst, "JUNK2"])
            S.dve(lambda e, st=st, rows=rows: e.tensor_tensor(out=st[:rows, 2:3], in0=st[:rows, 0:1], in1=st[:rows, 1:2], op=ALU.add), r=[kst], w=[kst])
            rms_rstd(st[:rows, 2:3], kst, st[:rows, 3:4], kst, 1024)
            xin, kx = XIN2r.next()
            S.dma("sp", lambda e, xin=xin, rows=rows, col0=col0: e.dma_start(out=xin[:rows, :], in_=X[col0:col0 + rows, :]), w=[kx])
            x1, kx1 = X1r.next()
            for half in range(2):
                pm, kpm = ph[half]
                S.dve(lambda e, pm=pm, half=half, rows=rows, st=st, x1=x1: e.scalar_tensor_tensor(out=x1[:rows, half * 512:(half + 1) * 512], in0=pm[:rows, :], scalar=st[:rows, 3:4],
                                                                                              in1=GP[:rows, 0, half * 512:(half + 1) * 512], op0=ALU.mult, op1=ALU.mult),
                      r=[kpm, kst, "GP"], w=[kx1])
            S.pool(lambda e, x1=x1, xin=xin, rows=rows: e.tensor_tensor(out=x1[:rows, :], in0=x1[:rows, :], in1=xin[:rows, :], op=ALU.add), r=[kx1, kx], w=[kx1])
            S.dma("sp", lambda e, x1=x1, rows=rows, col0=col0: e.dma_start(out=Y[col0:col0 + rows, :], in_=x1[:rows, :]), r=[kx1], w=[("Yx1", blk)])
            norm_transpose(x1, kx1, rows, 1, H2T, [("H2T", blk)], col0, XN2r, JUNK2)
    if debug:
        dbg["H2T"] = dout("D_H2T", (128, 8, NT), BF16)
        S.dma("sp", lambda e: e.dma_start(out=dbg["H2T"], in_=H2T), r=[("H2T", b) for b in range(17)])

    S.barrier()
    cur[0] = mark_D2
    if "D2" in phases:
        WDNs = alloc("WDNs", (128, 32, 1024), BF16)
        UT = alloc("UT", (128, 32, 576), BF16)
        WUPr = Ring("WUP", [alloc("WUP%d" % i, (128, 2048), BF16) for i in range(3)])
        ABr = Ring("AB", [alloc("AB%d" % i, (128, 516), F32) for i in range(2)])
        CCr = Ring("CC", [alloc("CC%d" % i, (128, 512), F32) for i in range(2)])
        GLr = Ring("GL", [alloc("GL%d" % i, (128, 512), F32) for i in range(2)])
        CARRY = alloc("CARRY", (128, 32, 2), F32)
        STT = alloc("STT", (128, 32, 32), F32)
        TAILA = alloc("TAILA", (128, 32, 34), F32)
        SSTGr = Ring("SSTG", [alloc("SSTG%d" % i, (32, 1024), F32) for i in range(2)])
        YBr = Ring("YB", [alloc("YB%d" % i, (128, 1024), F32) for i in range(2)])
        X1Rr = Ring("X1R", [alloc("X1R%d" % i, (128, 1024), F32) for i in range(2)])
        JUNK3 = alloc("JUNK3", (128, 512), BF16)
        S.dve(lambda e: e.memset(CARRY, 0.0), w=["CARRY"])
        for j0 in range(0, 32, 8):
            sstg, ksstg = SSTGr.next()
            S.dma("sp", lambda e, sstg=sstg, j0=j0: e.dma_start(out=sstg, in_=STATE[:, j0 * 128:(j0 + 8) * 128]), w=[ksstg])
            ps, kps = next_bank()
            for j in range(j0, j0 + 8):
                S.pe(lambda e, ps=ps, j=j, j0=j0, sstg=sstg: e.transpose(ps[:, (j - j0) * 32:(j - j0 + 1) * 32], sstg[:, (j - j0) * 128:(j - j0 + 1) * 128], IDF[0:32, 0:32]),
                     r=[ksstg, "IDF"], w=[kps])
            S.act(lambda e, ps=ps, j0=j0: e.copy(out=STT[:, j0:j0 + 8, :].rearrange("p a b -> p (a b)"), in_=ps[:, 0:256]), r=[kps], w=["STT"])
        def up_part(ti, j, wu, kwu, c0, n, smp, u0):
            hk = [("H2T", b) for b in range(17)]
            (pA, kA), (pG, kG) = next_bank(), next_bank()
            for k in range(8):
                S.pe(lambda e, k=k: e.matmul(pA[:, 0:n], lhsT=wu[:, k * 256:k * 256 + 128], rhs=H2T[:, k, c0:c0 + n], start=(k == 0), stop=(k == 7)),
                     r=[kwu] + hk, w=[kA])
            for k in range(8):
                S.pe(lambda e, k=k: e.matmul(pG[:, 0:n], lhsT=wu[:, k * 256 + 128:k * 256 + 256], rhs=H2T[:, k, c0:c0 + n], start=(k == 0), stop=(k == 7)),
                     r=[kwu] + hk, w=[kG])
            ab, kab = ABr.next()
            cc, kcc = CCr.next()
            gl, kgl = GLr.next()
            sh = 16 if smp else 1
            hal = 2 * sh
            if smp:
                S.pool(lambda e: e.tensor_copy(out=ab[:, 0:32], in_=STT[:, j, :]), r=["STT"], w=[kab])
            else:
                S.pool(lambda e: e.tensor_copy(out=ab[:, 0:2], in_=CARRY[:, j, :]), r=["CARRY"], w=[kab])
            S.act(lambda e: e.copy(out=ab[:, hal:hal + n], in_=pA[:, 0:n]), r=[kA], w=[kab])
            if not smp:
                S.pool(lambda e: e.tensor_copy(out=CARRY[:, j, :], in_=ab[:, n:n + 2]), r=[kab], w=["CARRY"])
                if ti == 3:
                    S.pool(lambda e: e.tensor_copy(out=TAILA[:, j, 32:34], in_=ab[:, n:n + 2]), r=[kab], w=["TAILA"])
            else:
                S.pool(lambda e: e.tensor_copy(out=TAILA[:, j, 0:32], in_=ab[:, 32 + 32:32 + 64]), r=[kab], w=["TAILA"])
            S.act(lambda e: e.activation(out=cc[:, 0:n], in_=pA[:, 0:n], func=AF.Identity, scale=CW[:, j, 2:3], bias=CW[:, j, 3:4]),
                  r=[kA, "CW"], w=[kcc])
            S.dve(lambda e: e.scalar_tensor_tensor(out=cc[:, 0:n], in0=ab[:, sh:sh + n], scalar=CW[:, j, 1:2], in1=cc[:, 0:n], op0=ALU.mult, op1=ALU.add),
                  r=[kab, kcc, "CW"], w=[kcc])
            S.dve(lambda e: e.scalar_tensor_tensor(out=cc[:, 0:n], in0=ab[:, 0:n], scalar=CW[:, j, 0:1], in1=cc[:, 0:n], op0=ALU.mult, op1=ALU.add),
                  r=[kab, kcc, "CW"], w=[kcc])
            S.act(lambda e: e.activation(out=gl[:, 0:n], in_=cc[:, 0:n], func=AF.Gelu_apprx_tanh), r=[kcc], w=[kgl])
            S.dve(lambda e: e.tensor_tensor(out=UT[:, j, u0:u0 + n], in0=gl[:, 0:n], in1=pG[:, 0:n], op=ALU.mult), r=[kgl, kG], w=[("UT", j, u0)])

        def down_block(blk, rows, col0, u0):
            ph = [next_bank(), next_bank()]
            for half in range(2):
                pf, kpf = ph[half]
                for j in range(32):
                    S.pe(lambda e, pf=pf, j=j, half=half: e.matmul(pf[:rows, :], lhsT=UT[:, j, u0:u0 + rows], rhs=WDNs[:, j, half * 512:(half + 1) * 512],
                                                                  start=(j == 0), stop=(j == 31)), r=[("UT", j, 0 if u0 < 512 else 512), ("WDN", j // 8)], w=[kpf])
            st, kst = next_stat()
            for half in range(2):
                pf, kpf = ph[half]
                S.act(lambda e, pf=pf, half=half: e.activation(out=JUNK3[:rows, :], in_=pf[:rows, :], func=AF.Square, accum_out=st[:rows, half:half + 1]),
                      r=[kpf], w=[kst, "JUNK3"])
            S.dve(lambda e: e.tensor_tensor(out=st[:rows, 2:3], in0=st[:rows, 0:1], in1=st[:rows, 1:2], op=ALU.add), r=[kst], w=[kst])
            rms_rstd(st[:rows, 2:3], kst, st[:rows, 3:4], kst, 1024)
            x1r, kx1r = X1Rr.next()
            S.dma("sp", lambda e: e.dma_start(out=x1r[:rows, :], in_=Y[col0:col0 + rows, :]), r=[("Yx1", blk)], w=[kx1r])
            yb, kyb = YBr.next()
            for half in range(2):
                pf, kpf = ph[half]
                S.dve(lambda e, pf=pf, half=half: e.scalar_tensor_tensor(out=yb[:rows, half * 512:(half + 1) * 512], in0=pf[:rows, :], scalar=st[:rows, 3:4],
                                                                        in1=GP[:rows, 1, half * 512:(half + 1) * 512], op0=ALU.mult, op1=ALU.mult),
                      r=[kpf, kst, "GP"], w=[kyb])
            S.pool(lambda e: e.tensor_tensor(out=yb[:rows, :], in0=yb[:rows, :], in1=x1r[:rows, :], op=ALU.add), r=[kyb, kx1r], w=[kyb])
            S.dma("sp", lambda e: e.dma_start(out=Y[col0:col0 + rows, :], in_=yb[:rows, :]), r=[kyb, kx1r], w=[("Yx1", blk)])

        for ti in range(4):
            c0 = ti * 512
            for j in range(32):
                wu, kwu = WUPr.next()
                S.dma("pool", lambda e, wu=wu, j=j: e.dma_start(out=wu, in_=WUP[j]), w=[kwu])
                if ti == 0 and j in (3, 6, 9, 12):
                    q4 = (j - 3) // 3
                    S.dma("pool", lambda e, q4=q4: e.dma_start(out=WDNs[:, q4 * 8:(q4 + 1) * 8, :], in_=WDN[:, q4 * 8:(q4 + 1) * 8, :]), w=[("WDN", q4)])
                up_part(ti, j, wu, kwu, c0, 512, False, 0)
                if ti == 3:
                    up_part(ti, j, wu, kwu, SC0, 64, True, 512)
            for bl in range(4):
                down_block(ti * 4 + bl, 128, c0 + bl * 128, bl * 128)
            if ti == 3:
                down_block(16, 64, SC0, 512)
        for j0 in range(0, 32, 4):
            ps, kps = next_bank()
            for j in range(j0, j0 + 4):
                S.pe(lambda e, ps=ps, j=j, j0=j0: e.transpose(ps[0:34, (j - j0) * 128:(j - j0 + 1) * 128], TAILA[:, j, :], IDF), r=["TAILA", "IDF"], w=[kps])
            yb, kyb = YBr.next()
            S.act(lambda e, ps=ps, yb=yb: e.copy(out=yb[0:34, 0:512], in_=ps[0:34, :]), r=[kps], w=[kyb])
            S.dma("sp", lambda e, yb=yb, j0=j0: e.dma_start(out=NCONV[:, j0 * 128:(j0 + 4) * 128], in_=yb[0:34, 0:512]), r=[kyb])
    if os.environ.get("KVERB"):
        print("SBUF high water", hw[0], "of", SB_END, flush=True)
    S.emit()
    return nc, sorted(dbg.keys())


_CACHE = {}


def _get_nc(debug=False, phases=("A", "B", "D1", "D2")):
    key = (debug, tuple(phases))
    if key not in _CACHE:
        _CACHE[key] = build(debug, phases)
    return _CACHE[key]


def run_cores(inputs, debug=False, phases=("A", "B", "D1", "D2"), cores=range(8)):
    nc, dbgnames = _get_nc(debug, phases)
    sh = _prep_shared(inputs)
    in_maps = []
    for c in cores:
        m = dict(sh)
        m.update(_prep_core(inputs, c))
        in_maps.append(m)
    res = run_bass_kernel_spmd(nc, in_maps, core_ids=list(range(len(in_maps))))
    return res.results


def kernel(**inputs):
    inputs = {k: np.asarray(v) for k, v in inputs.items()}
    res = run_cores(inputs)
    yp = np.stack([r["Y"][0:2048] for r in res], 0)
    ys = np.concatenate([r["Y"][2048:2112].reshape(4, 16, 1024).transpose(1, 0, 2) for r in res], 0)
    outs = [yp.astype(np.float32), ys.astype(np.float32)]
    for (pn, sn, W, H) in (("NA_P", "NA_S", 128, 2), ("NB1_P", "NB1_S", 128, 4), ("NB2_P", "NB2_S", 512, 4), ("NB3_P", "NB3_S", 2048, 4)):
        outs.append(np.stack([r[pn] for r in res], 0).reshape(1, 8, W, 2, H, 64).astype(np.float32))
        outs.append(np.concatenate([r[sn] for r in res], 0).reshape(1, 128, W, 2, H, 64).astype(np.float32))
    outs.append(np.stack([r["NCONV"][32:34] for r in res], 0).reshape(1, 8, 2, 4096).astype(np.float32))
    outs.append(np.concatenate([r["NCONV"][0:32].reshape(2, 16, 4096).transpose(1, 0, 2) for r in res], 0).reshape(1, 128, 2, 4096).astype(np.float32))
    return tuple(outs)
```

```python
import os
import numpy as np
from contextlib import ExitStack
import concourse.bass as bass
import concourse.mybir as mybir
from concourse.bass_utils import run_bass_kernel_spmd

F32 = mybir.dt.float32
BF16 = mybir.dt.bfloat16
AF = mybir.ActivationFunctionType
ALU = mybir.AluOpType

ENGS = ("pe", "act", "dve", "pool", "sp")
NEG = -30000.0
NT = 2112
SC0 = 2048


class Op:
    __slots__ = ("eng", "fn", "reads", "writes", "dma", "bg", "deps", "flag", "semi", "semval", "idx", "prevdma")

    def __init__(self, eng, fn, reads, writes, dma, bg=False):
        self.eng, self.fn, self.reads, self.writes, self.dma, self.bg = eng, fn, reads, writes, dma, bg
        self.deps = []
        self.flag = False
        self.semi = None
        self.semval = 0
        self.prevdma = None


class _Rec:
    def __init__(self):
        self.call = None

    def __getattr__(self, name):
        def f(*a, **k):
            self.call = (name, a, k)
            return self
        return f


class Sched:
    def __init__(self, nc, n_dma_sems=8, same_engine_raw=True):
        self.nc = nc
        self.ops = []
        self.n_dma_sems = n_dma_sems
        self.same_engine_raw = same_engine_raw

    def add(self, eng, fn, reads=(), writes=(), dma=False, bg=False):
        rec = _Rec()
        fn(rec)
        call = rec.call
        assert call is not None
        op = Op(eng, call, tuple(reads), tuple(writes), dma, bg)
        op.idx = len(self.ops)
        self.ops.append(op)
        return op

    def pe(self, fn, r=(), w=()):
        return self.add("pe", fn, r, w)

    def act(self, fn, r=(), w=()):
        return self.add("act", fn, r, w)

    def dve(self, fn, r=(), w=()):
        return self.add("dve", fn, r, w)

    def pool(self, fn, r=(), w=()):
        return self.add("pool", fn, r, w)

    def dma(self, eng, fn, r=(), w=(), bg=False):
        return self.add(eng, fn, r, w, dma=True, bg=bg)

    def barrier(self):
        op = Op(None, None, (), (), False)
        op.idx = len(self.ops)
        self.ops.append(op)

    def analyze(self):
        last_w, readers = {}, {}
        for op in self.ops:
            if op.eng is None:
                continue
            deps = set()
            for r in op.reads:
                p = last_w.get(r)
                if p is not None:
                    deps.add(p)
            for w in op.writes:
                p = last_w.get(w)
                if p is not None:
                    deps.add(p)
                for q in readers.get(w, ()):
                    deps.add(q)
            for r in op.reads:
                readers.setdefault(r, []).append(op.idx)
            for w in op.writes:
                last_w[w] = op.idx
                readers[w] = []
            deps.discard(op.idx)
            op.deps = sorted(deps)

    @staticmethod
    def _cost(op):
        name, a, k = op.fn
        def fe(ap):
            n = 1
            for d in ap.shape[1:]:
                n *= d
            return n
        if op.dma:
            o = k.get("out")
            nb = fe(o) * o.shape[0] * 4
            busy = 1200.0 if op.eng == "pool" else 150.0
            return busy, 2500.0 + nb / 200.0
        if op.eng == "pe":
            if name == "transpose":
                return 160.0, 220.0
            n = fe(k["rhs"])
            c = 110.0 + n * 0.21
            return c, c + 60.0
        o = k.get("out")
        n = fe(o) if o is not None else 64
        if op.eng == "pool":
            c = 250.0 + n * 2.1
        else:
            c = 180.0 + n * 1.05
        return c, c

    def schedule(self, window=int(os.environ.get('KWIN', '3000'))):
        ops = self.ops
        segs, cur = [], []
        for op in ops:
            if op.eng is None:
                segs.append(cur)
                cur = []
            else:
                cur.append(op)
        segs.append(cur)
        order = []
        pos = {}
        fin = {}
        tnow = 0.0
        reorder = not os.environ.get("KNOSCHED")
        prev_last, prev_dmas = {}, []
        for seg in segs:
            if not seg:
                continue
            queues = {e: [op for op in seg if op.eng == e] for e in ENGS}
            heads = {e: 0 for e in ENGS}
            done = set()
            t_eng = {e: tnow for e in ENGS}
            first_of_seg = {}
            seg_order = []
            nleft = len(seg)
            segset = set(op.idx for op in seg)
            while nleft:
                best = None
                for e in ENGS:
                    q = queues[e]
                    cnt = 0
                    i = heads[e]
                    while i < len(q) and cnt < ((4096 if e == 'sp' else window) if reorder else 1):
                        op = q[i]
                        i += 1
                        if op.idx in done:
                            continue
                        cnt += 1
                        ok = True
                        rdy = t_eng[e]
                        for d in op.deps:
                            if d in segset and d not in done:
                                ok = False
                                break
                            f = fin.get(d, 0.0)
                            if f > rdy:
                                rdy = f
                        if not ok:
                            continue
                        if best is None or rdy < best[0] - 1e-9:
                            best = (rdy, e, op)
                        if rdy <= t_eng[e] + 1e-9:
                            break
                assert best is not None, "scheduler deadlock"
                rdy, e, op = best
                busy, lat = self._cost(op)
                fin[op.idx] = rdy + lat
                t_eng[e] = rdy + busy
                done.add(op.idx)
                q = queues[e]
                while heads[e] < len(q) and q[heads[e]].idx in done:
                    heads[e] += 1
                if e not in first_of_seg:
                    first_of_seg[e] = op
                seg_order.append(op)
                nleft -= 1
            for e, op in first_of_seg.items():
                extra = [p.idx for p in prev_last.values()] + prev_dmas
                op.deps = sorted(set(op.deps) | set(extra))
            last = {}
            for op in seg_order:
                if not op.dma:
                    last[op.eng] = op
            prev_last = last
            prev_dmas = [op.idx for op in seg_order if op.dma and not op.bg]
            order.extend(seg_order)
            tprev = tnow
            tnow = max([tnow] + [fin[op.idx] for op in seg_order])
            if os.environ.get("KVERB"):
                bz = {e: 0.0 for e in ENGS}
                for op in seg_order:
                    bz[op.eng] += self._cost(op)[0]
                print("SEG us %.1f -> %.1f" % (tprev / 1e3, tnow / 1e3), {e: int(v / 1e3) for e, v in bz.items()}, flush=True)
        if os.environ.get("KVERB"):
            print("SCHED est total us", tnow / 1e3, flush=True)
        return order

    def emit(self):
        nc = self.nc
        self.analyze()
        order = self.schedule()
        ops = self.ops
        for op in order:
            need = []
            for d in op.deps:
                p = ops[d]
                if p.dma or p.eng != op.eng or op.dma:
                    need.append(d)
                elif self.same_engine_raw and p.eng != "pe" and (set(p.writes) & set(op.reads)):
                    need.append(d)
            op.deps = need
            for d in need:
                ops[d].flag = True
        es = ExitStack()
        eng_sem = {e: es.enter_context(nc.semaphore("s_" + e)) for e in ENGS}
        dma_sems = {e: [es.enter_context(nc.semaphore("d_%s%d" % (e, i))) for i in range(self.n_dma_sems)]
                    for e in ("sp", "act", "pool")}
        cnt = {e: 0 for e in ENGS}
        dcnt = {e: 0 for e in dma_sems}
        dval = {e: [0] * self.n_dma_sems for e in dma_sems}
        dlast = {e: [None] * self.n_dma_sems for e in dma_sems}
        bg_final = {e: [] for e in ENGS}
        nbg = 0
        bgsems = [es.enter_context(nc.semaphore("bg%d" % i)) for i in range(2)]
        bgval = [0, 0]
        bglast = [None, None]
        for op in order:
            if op.dma and op.bg:
                k = nbg % 2
                nbg += 1
                bgval[k] += 16
                op.semi = bgsems[k]
                op.semval = bgval[k]
                op.prevdma = bglast[k]
                bglast[k] = op.idx
                op.flag = True
                bg_final[op.eng] = [(bgsems[i], bgval[i]) for i in range(2) if bgval[i]]
            elif op.dma:
                k = dcnt[op.eng] % self.n_dma_sems
                dcnt[op.eng] += 1
                dval[op.eng][k] += 16
                op.semi = dma_sems[op.eng][k]
                op.semval = dval[op.eng][k]
                op.prevdma = dlast[op.eng][k]
                dlast[op.eng][k] = op.idx
                op.flag = True
            elif op.flag:
                cnt[op.eng] += 1
                op.semi = eng_sem[op.eng]
                op.semval = cnt[op.eng]
        if os.environ.get("KVERB"):
            print("SEMCOUNTS", cnt, dval, "nops", len(ops), flush=True)
        per_eng = {e: [op for op in order if op.eng == e] for e in ENGS}
        final_dma = {e: [(dma_sems[e][k], dval[e][k]) for k in range(self.n_dma_sems) if dval[e][k]]
                     for e in dma_sems}

        def run(engname, eobj):
            seen = {}
            for op in per_eng[engname]:
                waits = {}
                dl = list(op.deps)
                if op.prevdma is not None:
                    dl.append(op.prevdma)
                for d in dl:
                    p = ops[d]
                    key = id(p.semi)
                    if key not in waits or waits[key][1] < p.semval:
                        waits[key] = (p.semi, p.semval)
                for key, (s, v) in waits.items():
                    if seen.get(key, 0) >= v:
                        continue
                    eobj.wait_ge(s, v)
                    seen[key] = v
                if op.fn is None:
                    continue
                name, a, k = op.fn
                inst = getattr(eobj, name)(*a, **k)
                if op.flag:
                    inst.then_inc(op.semi, 16 if op.dma else 1)
            for (s, v) in list(final_dma.get(engname, ())) + bg_final[engname]:
                if seen.get(id(s), 0) < v:
                    eobj.wait_ge(s, v)

        if os.environ.get("KSIM"):
            semv = {}
            pc = {e: 0 for e in ENGS}
            prog = {}
            for e in ENGS:
                seen, lst = {}, []
                for op in per_eng[e]:
                    waits = {}
                    dl = list(op.deps) + ([op.prevdma] if op.prevdma is not None else [])
                    for d in dl:
                        p = ops[d]
                        key = id(p.semi)
                        if key not in waits or waits[key] < p.semval:
                            waits[key] = p.semval
                    ws = [(k, v) for k, v in waits.items() if seen.get(k, 0) < v]
                    for k, v in ws:
                        seen[k] = v
                    lst.append((ws, (id(op.semi), 16 if op.dma else 1) if op.flag else None, op.idx))
                prog[e] = lst
            progress = True
            while progress:
                progress = False
                for e in ENGS:
                    while pc[e] < len(prog[e]):
                        ws, inc, idx = prog[e][pc[e]]
                        if all(semv.get(k, 0) >= v for k, v in ws):
                            if inc:
                                semv[inc[0]] = semv.get(inc[0], 0) + inc[1]
                            pc[e] += 1
                            progress = True
                        else:
                            break
            print("KSIM", {e: (pc[e], len(prog[e])) for e in ENGS}, flush=True)
            for e in ENGS:
                if pc[e] < len(prog[e]):
                    ws, inc, idx = prog[e][pc[e]]
                    print("  stuck", e, "op", idx, ops[idx].fn[0], ops[idx].reads, ops[idx].writes, flush=True)
        with nc.Block() as block:
            block.tensor(lambda e: run("pe", e))
            block.scalar(lambda e: run("act", e))
            block.vector(lambda e: run("dve", e))
            block.gpsimd(lambda e: run("pool", e))
            block.sync(lambda e: run("sp", e))
        es.close()


class Ring:
    def __init__(self, name, aps):
        self.name, self.aps, self.i = name, aps, 0

    def next(self):
        k = self.i % len(self.aps)
        self.i += 1
        return self.aps[k], (self.name, k)


def _slopes(n):
    return np.exp2(-8.0 * np.arange(1, n + 1, dtype=np.float64) / n)


def _band_tables(slopes, ds, maxd):
    k = np.arange(128)[:, None]
    q = np.arange(128)[None, :]
    out = np.full((128, 2, len(slopes), 128), NEG, np.float64)
    for kind in (0, 1):
        d = q - k + (128 if kind == 0 else 0)
        valid = (d >= 0) & (d <= maxd)
        for h, s in enumerate(slopes):
            out[:, kind, h, :] = np.where(valid, -s * ds * d, NEG)
    return out.astype(np.float32)


def _const_tables():
    sa, sb = _slopes(8), _slopes(12)
    ta = np.zeros((128, 2, 2, 4, 128), np.float32)
    for g in range(2):
        ta[:, g] = _band_tables([sa[i + 4 * g] for i in range(4)], 1.0, 127)
    tb = np.stack([_band_tables(sb[4 * g:4 * g + 4], float(ds), 128) for g, ds in enumerate((1, 4, 16))], 0)
    r = np.arange(128)[:, None]
    rn = np.arange(4)[:, None]
    t = np.arange(4)[None, :]
    ts = np.full((128, 80), NEG, np.float64)
    tn = np.full((4, 80), NEG, np.float64)
    for g in range(2):
        for i in range(4):
            s = sa[i + 4 * g]
            c0 = g * 16 + i * 4
            ts[:, c0:c0 + 4] = np.where(r >= t + 1, -s * (128 + t - r), NEG)
            tn[:, c0:c0 + 4] = np.where(rn <= t, -s * (t - rn), NEG)
    for gi, ds in ((0, 1.0), (1, 4.0), (2, 16.0)):
        base = 32 + gi * 16
        for j in range(2):
            for hh in range(2):
                s = sb[4 * gi + 2 * j + hh]
                cols = base + j * 8 + np.arange(4) * 2 + hh
                if gi == 0:
                    ts[:, cols] = np.where(r >= t, -s * (128 + t - r), NEG)
                    tn[:, cols] = np.where(rn <= t, -s * (t - rn), NEG)
                else:
                    ts[:, cols] = -s * ds * (128 - r) + 0.0 * t
                    tn[:, cols] = np.where(rn == t, 0.0, NEG)
    return ta, tb.astype(np.float32), ts.astype(np.float32), tn.astype(np.float32)


def _prep_shared(inp):
    f = lambda a: np.ascontiguousarray(a, dtype=np.float32)
    w_in = inp["w_in"][0]
    qa, ka, va = w_in[:, 0:512], w_in[:, 512:640], w_in[:, 640:768]
    qb, kb, vb = w_in[:, 768:1536], w_in[:, 1536:2304], w_in[:, 2304:3072]
    ga, gb = w_in[:, 3072:4096], w_in[:, 4096:5120]
    kp = lambda w: w.reshape(8, 128, -1).transpose(1, 0, 2)
    qperm = np.concatenate([np.r_[i * 64:(i + 1) * 64, (i + 4) * 64:(i + 5) * 64] for i in range(4)])
    sh = {}
    sh["WA"] = f(kp(np.concatenate([qa[:, qperm], ka, va], 1)))
    sh["WB"] = f(np.stack([kp(np.concatenate([qb[:, 256 * g:256 * g + 256], kb[:, 256 * g:256 * g + 256],
                                              vb[:, 256 * g:256 * g + 256]], 1)) for g in range(3)], 0))
    wba = inp["w_branch_a"][0][qperm]
    wbb = inp["w_branch_b"][0]
    wd1 = []
    for e in range(8):
        cs = slice(e * 128, (e + 1) * 128)
        g2 = kp(np.concatenate([ga[:, cs], gb[:, cs]], 1)).reshape(128, 2048)
        a2 = wba[:, cs].reshape(4, 128, 128).transpose(1, 0, 2).reshape(128, 512)
        b2 = wbb[:, cs].reshape(2, 128, 128).transpose(1, 0, 2).reshape(128, 256)
        wd1.append(np.concatenate([g2, a2, b2], 1))
    sh["WD1"] = f(np.stack(wd1, 0))
    sh["WOUT"] = f(kp(inp["w_out"][0]))
    w_up = inp["w_up"][0]
    sh["WUP"] = f(np.stack([kp(np.concatenate([w_up[:, j * 128:(j + 1) * 128],
                                               w_up[:, 4096 + j * 128:4096 + (j + 1) * 128]], 1)).reshape(128, 2048)
                            for j in range(32)], 0))
    sh["WDN"] = f(inp["w_down"][0].reshape(32, 128, 1024).transpose(1, 0, 2))
    gt = np.stack([inp["norm_mix_pre"][0].reshape(8, 128).T, inp["norm_ffn_pre"][0].reshape(8, 128).T], 1)
    sh["GT"] = f(gt)
    sh["GP"] = f(np.stack([np.broadcast_to(inp["norm_mix_post"][0], (128, 1024)),
                           np.broadcast_to(inp["norm_ffn_post"][0], (128, 1024))], 1))
    sk = inp["sinks_a"][0]
    sh["SK"] = f(np.stack([sk[0:4], sk[4:8]], 0).reshape(1, 8))
    cw = np.concatenate([inp["conv_w"][0], inp["conv_b"]], 0)
    sh["CW"] = f(cw.reshape(4, 32, 128).transpose(2, 1, 0))
    sh["IDENT"] = np.eye(128, dtype=np.float32)
    ta, tb, ts, tn = _const_tables()
    sh["TA"], sh["TB"], sh["TS"], sh["TSN"] = ta, tb, ts, tn
    return sh


def _prep_core(inp, c):
    f = lambda a: np.ascontiguousarray(a, dtype=np.float32)
    d = {}
    xs = inp["x_sample"][16 * c:16 * c + 16].transpose(1, 0, 2).reshape(64, 1024)
    d["X"] = f(np.concatenate([inp["x_prompt"][c], xs], 0))
    d["CA"] = f(inp["cache_a_kv"][0, 16 * c:16 * c + 16].reshape(16, 128, 256))
    d["CB1"] = f(inp["cache_b1_kv"][0, 16 * c:16 * c + 16].reshape(16, 128, 512))
    d["CB2"] = f(inp["cache_b2_kv"][0, 16 * c:16 * c + 16].reshape(16, 512, 512))
    d["CB3"] = f(inp["cache_b3_kv"][0, 16 * c:16 * c + 16].reshape(16, 2048, 512))
    d["STATE"] = f(inp["state_conv"][0, 16 * c:16 * c + 16].transpose(1, 0, 2).reshape(32, 4096))
    return d


import os
GROUPS = ("A", 0, 1, 2)
_PARTS = set(os.environ.get("KPARTS", "fm,tm,pa,sa").split(","))
_GSEL = os.environ.get("KGROUPS", "")


def build(debug=False, phases=("A", "B", "D1", "D2")):
    nc = bass.Bass("TRN2", target_bir_lowering=False)
    S = Sched(nc)
    din = lambda name, shape: nc.dram_tensor(name, list(shape), F32, kind="ExternalInput").ap()
    dout = lambda name, shape, dt=F32: nc.dram_tensor(name, list(shape), dt, kind="ExternalOutput").ap()

    X = din("X", (NT, 1024))
    CA, CB1, CB2, CB3 = din("CA", (16, 128, 256)), din("CB1", (16, 128, 512)), din("CB2", (16, 512, 512)), din("CB3", (16, 2048, 512))
    STATE = din("STATE", (32, 4096))
    WA, WB = din("WA", (128, 8, 768)), din("WB", (3, 128, 8, 768))
    WD1, WOUT = din("WD1", (8, 128, 2816)), din("WOUT", (128, 8, 1024))
    WUP, WDN = din("WUP", (32, 128, 2048)), din("WDN", (128, 32, 1024))
    GT_d, GP_d, SK_d, CW_d = din("GT", (128, 2, 8)), din("GP", (128, 2, 1024)), din("SK", (1, 8)), din("CW", (128, 32, 4))
    IDENT_d = din("IDENT", (128, 128))
    TA_d, TB_d, TS_d, TSN_d = din("TA", (128, 2, 2, 4, 128)), din("TB", (3, 128, 2, 4, 128)), din("TS", (128, 80)), din("TSN", (4, 80))

    Y = dout("Y", (NT, 1024))
    NKV_P = {"A": dout("NA_P", (128, 256)), 0: dout("NB1_P", (128, 512)), 1: dout("NB2_P", (512, 512)), 2: dout("NB3_P", (2048, 512))}
    NKV_S = {"A": dout("NA_S", (16, 128, 256)), 0: dout("NB1_S", (16, 128, 512)), 1: dout("NB2_S", (16, 512, 512)), 2: dout("NB3_S", (16, 2048, 512))}
    CACHE = {"A": CA, 0: CB1, 1: CB2, 2: CB3}
    WIN = {"A": 128, 0: 128, 1: 512, 2: 2048}
    NCONV = dout("NCONV", (34, 4096))
    dbg = {}

    SB_BASE, SB_END = 17408, 229376
    cur = [SB_BASE]

    hw = [0]

    def alloc(name, shape, dt):
        nbytes = int(np.prod(shape[1:])) * (2 if dt == BF16 else 4)
        off = (cur[0] + 31) // 32 * 32
        assert off + nbytes <= SB_END, ("SBUF overflow", name, off + nbytes)
        cur[0] = off + nbytes
        hw[0] = max(hw[0], cur[0])
        return nc.alloc_sbuf_tensor_at(name, list(shape), dt, offset=off).ap()

    GT = alloc("GT", (128, 2, 8), F32)
    GP = alloc("GP", (128, 2, 1024), F32)
    CW = alloc("CW", (128, 32, 4), F32)
    IDB = alloc("IDB", (128, 128), BF16)
    IDF = alloc("IDF", (128, 128), F32)
    ONES = alloc("ONES", (128, 128), BF16)
    SKR = alloc("SKR", (1, 8), F32)
    SKE = alloc("SKE", (1, 8), F32)
    ZROW = alloc("ZROW", (1, 128), F32)
    SINKROW = alloc("SINKROW", (1, 2, 4, 128), BF16)
    STAT = alloc("STAT", (128, 8, 4), F32)
    SINKS = alloc("SINKS", (1, 32), BF16)
    QBD = alloc("QBD", (128, 4, 64, 2), BF16)
    h2t_off = (cur[0] + 31) // 32 * 32
    H2T = alloc("H2T", (128, 8, NT), BF16)
    mark_D2 = cur[0]
    UB = nc.alloc_sbuf_tensor_at("UB", [128, 2, NT], F32, offset=h2t_off).ap()
    LB = nc.alloc_sbuf_tensor_at("LB", [128, 2, NT], F32, offset=h2t_off + 2 * NT * 4).ap()
    hT = alloc("hT", (128, 8, NT), BF16)
    OA = alloc("OA", (128, 4, NT), BF16)
    OB = alloc("OB", (128, 2, NT), BF16)
    PERSIST_END = cur[0]

    def scol(ap, n):
        if len(ap.shape) == 2:
            return ap.rearrange("p (t n) -> p n t", n=16)[:, n, :]
        return ap.rearrange("p c (t n) -> p c n t", n=16)[:, :, n, :]

    psum = [nc.alloc_psum_tensor("ps%d" % i, [128, 512], F32).ap() for i in range(8)]
    bank_i = [0]

    NPB = int(os.environ.get("KNPB", "2"))
    pbank_i = [0]
    mode = {"attn": False}

    def next_bank():
        if NPB and mode["attn"]:
            k = bank_i[0] % (8 - NPB)
            bank_i[0] += 1
            return psum[k], ("ps", k)
        k = bank_i[0] % 8
        bank_i[0] += 1
        return psum[k], ("ps", k)

    def next_bank_proj():
        if not NPB:
            return next_bank()
        k = 8 - NPB + pbank_i[0] % NPB
        pbank_i[0] += 1
        return psum[k], ("ps", k)

    stat_i = [0]

    def next_stat():
        k = stat_i[0] % 8
        stat_i[0] += 1
        return STAT[:, k, :], ("stat", k)

    S.dma("sp", lambda e: e.dma_start(out=GT, in_=GT_d), w=["GT"])
    S.dma("sp", lambda e: e.dma_start(out=GP, in_=GP_d), w=["GP"])
    S.dma("sp", lambda e: e.dma_start(out=CW, in_=CW_d), w=["CW"])
    S.dma("sp", lambda e: e.dma_start(out=IDF, in_=IDENT_d), w=["IDF"])
    S.dma("sp", lambda e: e.dma_start(out=SKR, in_=SK_d), w=["SKR"])
    S.dma("pool", lambda e: e.dma_start(out=IDB, in_=IDENT_d), w=["IDB"])
    S.dve(lambda e: e.memset(ONES, 1.0), w=["ONES"])
    S.dve(lambda e: e.memset(ZROW, 0.0), w=["ZROW"])
    S.dve(lambda e: e.memset(QBD, 0.0), w=["QBD"])
    S.act(lambda e: e.activation(out=SKE, in_=SKR, func=AF.Exp), r=["SKR"], w=["SKE"])
    for gi in range(8):
        S.dve(lambda e, gi=gi: e.tensor_scalar(out=SINKROW[0:1, gi // 4, gi % 4, :], in0=ZROW, scalar1=SKE[0:1, gi:gi + 1],
                                               scalar2=None, op0=ALU.add), r=["SKE", "ZROW"], w=["SINKROW"])
        S.dve(lambda e, gi=gi: e.tensor_scalar(out=SINKS[0:1, gi * 4:gi * 4 + 4], in0=ZROW[0:1, 0:4], scalar1=SKE[0:1, gi:gi + 1],
                                               scalar2=None, op0=ALU.add), r=["SKE", "ZROW"], w=["SINKROW"])

    def rms_rstd(ss_ap, key_ss, out_ap, key_out, n):
        S.dve(lambda e: e.tensor_scalar(out=out_ap, in0=ss_ap, scalar1=1.0 / n, scalar2=1e-6, op0=ALU.mult, op1=ALU.add),
              r=[key_ss], w=[key_out])
        S.act(lambda e: e.activation(out=out_ap, in_=out_ap, func=AF.Sqrt), r=[key_out], w=[key_out])
        S.dve(lambda e: e.reciprocal(out=out_ap, in_=out_ap), r=[key_out], w=[key_out])

    next_stat_parity = [0]

    def norm_transpose(src, src_key, rows, gidx, dst, dst_keys, col0, XNr, JUNK):
        st, kst = next_stat()
        xn, kxn = XNr.next()
        S.act(lambda e: e.activation(out=xn[:rows, :], in_=src[:rows, :], func=AF.Square, accum_out=st[:rows, 0:1]),
              r=[src_key], w=[kst, kxn])
        rms_rstd(st[:rows, 0:1], kst, st[:rows, 1:2], kst, 1024)
        S.dve(lambda e: e.tensor_scalar(out=xn[:rows, :], in0=src[:rows, :], scalar1=st[:rows, 1:2], scalar2=None, op0=ALU.mult),
              r=[src_key, kst], w=[kxn])
        ps, kps = next_bank()
        psb = ps.bitcast(BF16)
        for k in range(8):
            S.pe(lambda e, k=k: e.transpose(psb[:, k * 128:k * 128 + rows], xn[:rows, k * 128:(k + 1) * 128], IDB[:rows, :rows]),
                 r=[kxn, "IDB"], w=[kps])
        gbc = GT[:, gidx, :].unsqueeze(2).to_broadcast([128, 8, rows])
        src3 = psb.rearrange("p (k q) -> p k q", k=8)[:, :, 0:rows]
        if next_stat_parity[0] % 2 == 0:
            S.dve(lambda e: e.tensor_tensor(out=dst[:, :, col0:col0 + rows], in0=src3, in1=gbc, op=ALU.mult), r=[kps, "GT"], w=dst_keys)
        else:
            for hlf in range(2):
                S.act(lambda e, hlf=hlf: e.activation(out=dst[:, 4 * hlf:4 * hlf + 4, col0:col0 + rows], in_=src3[:, 4 * hlf:4 * hlf + 4, :], func=AF.Copy), r=[kps], w=dst_keys)
            S.dve(lambda e: e.tensor_tensor(out=dst[:, :, col0:col0 + rows], in0=dst[:, :, col0:col0 + rows], in1=gbc, op=ALU.mult), r=[kps, "GT"], w=dst_keys)
        next_stat_parity[0] += 2

    mark_A = cur[0]
    XOFF = {}
    for nm_, sz_ in (("XIN0", 4096), ("XIN1", 4096), ("XN0", 2048), ("XN1", 2048)):
        XOFF[nm_] = (cur[0] + 31) // 32 * 32
        cur[0] = XOFF[nm_] + sz_
    hw[0] = max(hw[0], cur[0])
    XOFF["XIN2"] = h2t_off
    XOFF["XN2"] = h2t_off + 4096
    XINr = Ring("XIN", [nc.alloc_sbuf_tensor_at("XIN%d" % i, [128, 1024], F32, offset=XOFF["XIN%d" % i]).ap() for i in range(3)])
    XNr = Ring("XN", [nc.alloc_sbuf_tensor_at("XN%d" % i, [128, 1024], BF16, offset=XOFF["XN%d" % i]).ap() for i in range(3)])
    JUNK = None
    for blk in range(17):
        rows = 128 if blk < 16 else 64
        xin, kx = XINr.next()
        S.dma("sp", lambda e, xin=xin, blk=blk, rows=rows: e.dma_start(out=xin[:rows, :], in_=X[blk * 128:blk * 128 + rows, :]), w=[kx])
        norm_transpose(xin, kx, rows, 0, hT, [("hT", blk)], blk * 128, XNr, JUNK)

    WT = alloc("WT", (128, 8, 768), BF16)
    QT = alloc("QT", (128, 4, NT), BF16)
    KT = alloc("KT", (128, 2, NT), BF16)
    VE = alloc("VE", (128, 16, 256), BF16)
    TAB = alloc("TAB", (128, 2048), F32)
    TS = alloc("TS", (128, 80), F32)
    TSN = alloc("TSN", (4, 80), F32)
    fences = []
    TMPr = Ring("TMP", [alloc("TMP%d" % i, (128, 512), F32) for i in range(2)]
                + [nc.alloc_sbuf_tensor_at("TMP2", [128, 512], F32, offset=XOFF["XN0"]).ap()])
    fences.append((("TMP", 2), ("XN", 0), TMPr.aps[2]))
    PTr = Ring("PT", [alloc("PT%d" % i, (128, 512), BF16) for i in range(4)])
    STGr = Ring("STG", [alloc("STG%d" % i, (128, 512), F32) for i in range(3)])
    LRr = Ring("LR", [alloc("LR%d" % i, (128, 512), F32) for i in range(2)])
    CKr = Ring("CK", [alloc("CK%d" % i, (128, 4, 512), BF16) for i in range(3)]
               + [nc.alloc_sbuf_tensor_at("CK3", [128, 4, 512], BF16, offset=XOFF["XIN0"]).ap()])
    fences.append((("CK", 3), ("XIN", 0), CKr.aps[3]))
    KCTr = Ring("KCT", [alloc("KCT%d" % i, (128, 4, 2, 128), BF16) for i in range(2)]
                + [nc.alloc_sbuf_tensor_at("KCT%d" % (2 + i), [128, 4, 2, 128], BF16, offset=XOFF["XIN1"] + 2048 * i).ap() for i in range(2)])
    fences.append((("KCT", 2), ("XIN", 1), KCTr.aps[2]))
    fences.append((("KCT", 3), ("XIN", 1), KCTr.aps[3]))
    for (newk, oldk, ap_) in fences:
        S.dve(lambda e, ap_=ap_: e.memset(ap_.rearrange("p a b c -> p (a b c)")[:, 0:2] if len(ap_.shape) == 4 else (ap_.rearrange("p a b -> p (a b)")[:, 0:2] if len(ap_.shape) == 3 else ap_[:, 0:2]), 0.0),
              w=[newk, oldk])
    NKVr = Ring("NKV", [alloc("NKV%d" % i, (4, 256), BF16) for i in range(4)])
    TMPNr = Ring("TMPN", [alloc("TMPN%d" % i, (4, 32), F32) for i in range(2)])
    PTNr = Ring("PTN", [alloc("PTN%d" % i, (4, 32), BF16) for i in range(2)])

    S.dma("sp", lambda e: e.dma_start(out=TS, in_=TS_d), w=["TS"])
    S.dma("sp", lambda e: e.dma_start(out=TSN, in_=TSN_d), w=["TSN"])

    def hview(g, k):
        base = hT[:, k, 0:2048]
        if g in ("A", 0):
            return base.rearrange("p (b i) -> p b i", i=128)
        if g == 1:
            return base.rearrange("p (bb i r) -> p r bb i", r=4, i=128)
        return base.rearrange("p (i r) -> p r i", r=16)

    ticks = []

    tick_w = []

    def tick(wt=2.0):
        k = ("tick", len(ticks))
        ticks.append(k)
        tick_w.append(wt)
        return k

    def bg_shifts():
        pieces = []
        for g in GROUPS:
            W = WIN[g]
            if g == "A":
                pieces += [(g, 0, 8, 4, W), (g, 8, 16, 4, W)]
            elif g == 0:
                pieces += [(g, n0, n0 + 4, 4, W) for n0 in range(0, 16, 4)]
            elif g == 1:
                pieces += [(g, n0, n0 + 1, 4, W) for n0 in range(16)]
            else:
                for n0 in range(16):
                    pieces += [(g, n0, n0 + 1, 4 + 511 * q, 4 + 511 * (q + 1)) for q in range(4)]
        cum = np.cumsum(tick_w)
        for i, (g, n0, n1, r0, r1) in enumerate(pieces):
            tk = ticks[int(np.searchsorted(cum, (i + 0.3) * cum[-1] / len(pieces)))]
            S.dma("sp", lambda e, g=g, n0=n0, n1=n1, r0=r0, r1=r1: e.dma_start(out=NKV_S[g][n0:n1, r0 - 4:r1 - 4, :], in_=CACHE[g][n0:n1, r0:r1, :]),
                  r=[tk] + ([("bgc", i - 2)] if i >= 2 else []), w=[("bgc", i)], bg=True)

    def do_group(g):
        isA = g == "A"
        if g == 0:
            S.dve(lambda e: e.memset(UB[:, 0, 0:2], 0.0), w=[("UB", 0), ("XIN", 2), ("XN", 2)])
        nq = 4 if isA else 2
        nk = 1 if isA else 2
        kvc0 = 512 if isA else 256
        ncols = 256 if isA else 512
        vc0 = 128 if isA else 256
        W = WIN[g]
        wsrc = WA if isA else WB[g]
        S.dma("pool", lambda e: e.dma_start(out=WT, in_=wsrc), w=["WT"])
        tsrc = TA_d if isA else TB_d[g]
        tn = 2048 if isA else 1024
        S.dma("sp", lambda e: e.dma_start(out=TAB[:, 0:tn], in_=tsrc.rearrange("p a b c d -> p (a b c d)") if isA
                                          else tsrc.rearrange("p a b c -> p (a b c)")), w=["TAB"])
        allh = [("hT", b) for b in range(17)]
        evi = [0]
        for (dst, dname, dch, wc0) in [(QT, "QT", i, i * 128) for i in range(nq)] + [(KT, "KT", i, nq * 128 + i * 128) for i in range(nk)]:
            for s in (range(int(os.environ.get("KFMLIM", "5"))) if "fm" in _PARTS else ()):
                n = 512 if s < 4 else 64
                ps, kps = next_bank_proj()
                for k in range(8):
                    rhs = hT[:, k, SC0:SC0 + 64] if s == 4 else hT[:, k, s * 512:(s + 1) * 512]
                    S.pe(lambda e, ps=ps, k=k, rhs=rhs, wc0=wc0, n=n: e.matmul(ps[:, 0:n], lhsT=WT[:, k, wc0:wc0 + 128], rhs=rhs,
                                                                             start=(k == 0), stop=(k == 7)),
                         r=["WT"] + allh, w=[kps])
                src = ps[:, 0:n]
                wkeys = [(dname, dch, s)]
                if s == 4:
                    o = dst[:, dch, SC0:SC0 + 64]
                elif g in ("A", 0):
                    o = dst[:, dch, s * 512:s * 512 + n]
                else:
                    rr = 4 if g == 1 else 16
                    mm = 512 // rr
                    o = dst[:, dch, 0:2048].rearrange("p (r m) -> p m r", r=rr)[:, mm * s:mm * (s + 1), :]
                    src = ps[:, 0:512].rearrange("p (m r) -> p m r", r=rr)
                    wkeys = [(dname, dch, q) for q in range(4)]
                if evi[0] % 2 == 0:
                    S.act(lambda e, o=o, src=src: e.copy(out=o, in_=src), r=[kps], w=wkeys)
                else:
                    S.dve(lambda e, o=o, src=src: e.tensor_copy(out=o, in_=src), r=[kps], w=wkeys)
                evi[0] += 1
        for kb in (range(int(os.environ.get("KTM0", "0")), int(os.environ.get("KTM1", "17"))) if "tm" in _PARTS else ()):
            rows = 128 if kb < 16 else 64
            ps, kps = next_bank_proj()
            need = kb == 16 or (g in ("A", 0) and kb == 15) or (g == 1 and kb % 4 == 3) or g == 2
            c_lo = 0 if need else vc0
            for k in range(8):
                if kb == 16:
                    lhsT = hT[:, k, SC0:SC0 + 64]
                elif g == 1:
                    lhsT = hview(g, k)[:, kb // 4, kb % 4, :]
                else:
                    lhsT = hview(g, k)[:, kb, :]
                S.pe(lambda e, ps=ps, k=k, lhsT=lhsT, rows=rows, c_lo=c_lo: e.matmul(ps[:rows, c_lo:ncols], lhsT=lhsT, rhs=WT[:, k, kvc0 + c_lo:kvc0 + ncols],
                                                                         start=(k == 0), stop=(k == 7)), r=["WT"] + allh, w=[kps])
            if kb < 16:
                S.act(lambda e, ps=ps, kb=kb: e.copy(out=VE[:, kb, 0:ncols - vc0], in_=ps[:, vc0:ncols]), r=[kps], w=[("VE", kb), tick(1.0)])
            if need:
                stg, kstg = STGr.next()
                if True:
                    S.act(lambda e, ps=ps, stg=stg, rows=rows: e.copy(out=stg[:rows, 0:ncols], in_=ps[:rows, 0:ncols]), r=[kps], w=[kstg])
                else:
                    S.dve(lambda e, ps=ps, stg=stg, rows=rows: e.tensor_copy(out=stg[:rows, 0:ncols], in_=ps[:rows, 0:ncols]), r=[kps], w=[kstg])
                if kb == 16:
                    for t in range(4):
                        S.dma("sp", lambda e, stg=stg, t=t: e.dma_start(out=NKV_S[g][:, W - 4 + t, :], in_=stg[16 * t:16 * t + 16, 0:ncols]),
                              r=[kstg], w=[("newrows", g, t)])
                else:
                    if g in ("A", 0):
                        dstd = NKV_P[g]
                    elif g == 1:
                        dstd = NKV_P[g].rearrange("(i r) c -> r i c", r=4)[kb // 4]
                    else:
                        dstd = NKV_P[g].rearrange("(i r) c -> r i c", r=16)[kb]
                    S.dma("sp", lambda e, stg=stg, dstd=dstd: e.dma_start(out=dstd, in_=stg[:, 0:ncols]), r=[kstg])
        for b in (range(16) if "pa" in _PARTS else ()):
            if g in ("A", 0):
                has_prev = b > 0
            elif g == 1:
                has_prev = b % 4 > 0
            else:
                has_prev = False
            kinds = ([(b - 1, 0)] if has_prev else []) + [(b, 1)]
            psU, kU = next_bank()
            psL, kL = next_bank()
            qk = [("QT", i, b // 4) for i in range(nq)]
            if isA:
                for gg in range(2):
                    pts = []
                    for (kb, kind) in kinds:
                        psS, kS = next_bank()
                        S.pe(lambda e, psS=psS, gg=gg, kb=kb: e.matmul(psS.rearrange("p (i q) -> p i q", i=4), lhsT=KT[64 * gg:64 * gg + 64, 0, kb * 128:(kb + 1) * 128],
                                                                       rhs=QT[64 * gg:64 * gg + 64, 0:4, b * 128:(b + 1) * 128], start=True, stop=True),
                             r=[("KT", 0, kb // 4)] + qk, w=[kS])
                        tmp, kT = TMPr.next()
                        tab = TAB[:, (gg * 2 + kind) * 512:(gg * 2 + kind + 1) * 512]
                        S.dve(lambda e, tmp=tmp, psS=psS, tab=tab: e.scalar_tensor_tensor(out=tmp, in0=psS, scalar=0.125, in1=tab, op0=ALU.mult, op1=ALU.add),
                              r=[kS, "TAB"], w=[kT])
                        pt, kP = PTr.next()
                        S.act(lambda e, pt=pt, tmp=tmp: e.activation(out=pt, in_=tmp, func=AF.Exp), r=[kT], w=[kP] + ([tick(5.0)] if (kind == 1 and gg == 0) else []))
                        pts.append((pt, kP, kb))
                    for idx, (pt, kP, kb) in enumerate(pts):
                        S.pe(lambda e, pt=pt, gg=gg, idx=idx: e.matmul(psL[64 * gg:64 * gg + 64, :], lhsT=ONES[:, 0:64], rhs=pt, start=(idx == 0), stop=False),
                             r=[kP, "ONES"], w=[kL])
                    S.pe(lambda e, gg=gg: e.matmul(psL[64 * gg:64 * gg + 64, :], lhsT=ONES[0:1, 0:64], rhs=SINKROW[0:1, gg].rearrange("p a b -> p (a b)"),
                                                   start=False, stop=True), r=["SINKROW", "ONES"], w=[kL])
                    for idx, (pt, kP, kb) in enumerate(pts):
                        S.pe(lambda e, pt=pt, gg=gg, kb=kb, idx=idx, n=len(pts): e.matmul(
                            psU[64 * gg:64 * gg + 64, :], lhsT=VE[:, kb, gg * 64:gg * 64 + 64], rhs=pt,
                            start=(idx == 0), stop=(idx == n - 1)), r=[kP, ("VE", kb)], w=[kU])
                lr, kLr = LRr.next()
                S.act(lambda e, lr=lr: e.activation(out=lr, in_=psL, func=AF.Ln), r=[kL], w=[kLr])
                S.act(lambda e, lr=lr: e.activation(out=lr, in_=lr, func=AF.Exp, scale=-1.0), r=[kLr], w=[kLr])
                S.dve(lambda e, lr=lr: e.tensor_tensor(out=OA[:, :, b * 128:(b + 1) * 128], in0=psU.rearrange("p (i q) -> p i q", i=4),
                                                       in1=lr.rearrange("p (i q) -> p i q", i=4), op=ALU.mult), r=[kU, kLr], w=[("OA", b)])
            else:
                pts = []
                for (kb, kind) in kinds:
                    tmp, kT = TMPr.next()
                    for hh in range(2):
                        psS, kS = next_bank()
                        for j in range(2):
                            S.pe(lambda e, psS=psS, hh=hh, j=j, kb=kb: e.matmul(psS[:, j * 128:(j + 1) * 128], lhsT=KT[64 * hh:64 * hh + 64, j, kb * 128:(kb + 1) * 128],
                                                                               rhs=QT[64 * hh:64 * hh + 64, j, b * 128:(b + 1) * 128], start=True, stop=True),
                                 r=[("KT", j, kb // 4)] + qk, w=[kS])
                        tab = TAB[:, kind * 512:(kind + 1) * 512].rearrange("p (j hh q) -> p j hh q", j=2, hh=2)[:, :, hh, :]
                        tv = tmp.rearrange("p (j hh q) -> p j hh q", j=2, hh=2)[:, :, hh, :]
                        S.dve(lambda e, tv=tv, psS=psS, tab=tab: e.scalar_tensor_tensor(out=tv, in0=psS[:, 0:256].rearrange("p (j q) -> p j q", j=2), scalar=0.125, in1=tab,
                                                                                      op0=ALU.mult, op1=ALU.add), r=[kS, "TAB"], w=[kT])
                    pt, kP = PTr.next()
                    S.act(lambda e, pt=pt, tmp=tmp: e.activation(out=pt, in_=tmp, func=AF.Exp), r=[kT], w=[kP] + ([tick(5.0)] if kind == 1 else []))
                    pts.append((pt, kP, kb))
                for idx, (pt, kP, kb) in enumerate(pts):
                    S.pe(lambda e, pt=pt, idx=idx, n=len(pts): e.matmul(psL, lhsT=ONES, rhs=pt, start=(idx == 0), stop=(idx == n - 1)), r=[kP, "ONES"], w=[kL])
                for h in range(4):
                    hh, j = h % 2, h // 2
                    for idx, (pt, kP, kb) in enumerate(pts):
                        S.pe(lambda e, pt=pt, h=h, hh=hh, j=j, kb=kb, idx=idx, n=len(pts): e.matmul(
                            psU[64 * hh:64 * hh + 64, j * 128:(j + 1) * 128], lhsT=VE[:, kb, h * 64:h * 64 + 64], rhs=pt[:, h * 128:(h + 1) * 128],
                            start=(idx == 0), stop=(idx == n - 1)), r=[kP, ("VE", kb)], w=[kU])
                if g == 0:
                    uv = UB[:, :, b * 128:(b + 1) * 128]
                    lv = LB[:, :, b * 128:(b + 1) * 128]
                elif g == 1:
                    uv = UB[:, :, 0:2048].rearrange("p j (bb i r) -> p j r bb i", r=4, i=128)[:, :, b // 4, b % 4, :]
                    lv = LB[:, :, 0:2048].rearrange("p j (bb i r) -> p j r bb i", r=4, i=128)[:, :, b // 4, b % 4, :]
                else:
                    uv = UB[:, :, 0:2048].rearrange("p j (i r) -> p j r i", r=16)[:, :, b, :]
                    lv = LB[:, :, 0:2048].rearrange("p j (i r) -> p j r i", r=16)[:, :, b, :]
                pu = psU[:, 0:256].rearrange("p (j q) -> p j q", j=2)
                wk, rk = ("UB", g), ([("UB", g - 1)] if g > 0 else [])
                if os.environ.get("KNOEVAC"):
                    continue
                if g == 0:
                    S.dve(lambda e, uv=uv, pu=pu: e.tensor_copy(out=uv, in_=pu), r=[kU] + rk, w=[wk])
                else:
                    S.dve(lambda e, uv=uv, pu=pu: e.tensor_tensor(out=uv, in0=uv, in1=pu, op=ALU.add), r=[kU] + rk, w=[wk])
                for hh in range(2):
                    pl = psL.rearrange("p (j hh q) -> p j hh q", j=2, hh=2)[64 * hh:64 * hh + 64, :, hh, :]
                    lvv = lv[64 * hh:64 * hh + 64]
                    if g == 0:
                        S.dve(lambda e, lvv=lvv, pl=pl: e.tensor_copy(out=lvv, in_=pl), r=[kL] + rk, w=[wk])
                    else:
                        S.dve(lambda e, lvv=lvv, pl=pl: e.tensor_tensor(out=lvv, in0=lvv, in1=pl, op=ALU.add), r=[kL] + rk, w=[wk])

        ntile = 1 if g in ("A", 0) else 4
        nc_ = 32 if isA else 16
        ts0 = {"A": 0, 0: 32, 1: 48, 2: 64}[g]
        nkc = 1 if isA else 2
        qs = [("QT", i, 4) for i in range(nq)]
        S.act(lambda e: e.copy(out=QBD[0:64, 0:nq, :, 0], in_=QT[0:64, 0:nq, SC0:NT]), r=qs, w=["QBD"])
        S.dve(lambda e: e.tensor_copy(out=QBD[64:128, 0:nq, :, 1], in_=QT[64:128, 0:nq, SC0:NT]), r=qs, w=["QBD"])
        for n in (range(16) if "sa" in _PARTS else ()):
            ck, kck = CKr.next()
            if g in ("A", 0):
                src = CACHE[g][n]
                S.dma("pool", lambda e, ck=ck, src=src: e.dma_start(out=ck[:, 0, 0:ncols], in_=src), w=[kck])
            elif g == 1:
                src = CACHE[g][n].rearrange("(m r) c -> m r c", r=4)
                S.dma("pool", lambda e, ck=ck, src=src: e.dma_start(out=ck, in_=src), w=[kck])
            else:
                src = CACHE[g][n].rearrange("(m r) c -> m r c", r=16)[:, 0:4, :]
                S.dma("pool", lambda e, ck=ck, src=src: e.dma_start(out=ck, in_=src), w=[kck])
            nkv, knkv = NKVr.next()
            S.dma("pool", lambda e, nkv=nkv, n=n: e.dma_start(out=nkv[:, 0:ncols - vc0], in_=NKV_S[g][n, W - 4:W, vc0:ncols]),
                  r=[("newrows", g, t) for t in range(4)], w=[knkv])
            kct, kkct = KCTr.next()
            psT, kpsT = next_bank()
            psTb = psT.bitcast(BF16)
            for tl in range(ntile):
                for c in range(nkc):
                    S.pe(lambda e, tl=tl, c=c, ck=ck: e.transpose(psTb[:, (tl * 2 + c) * 128:(tl * 2 + c + 1) * 128], ck[:, tl, c * 128:(c + 1) * 128], IDB),
                         r=[kck, "IDB"], w=[kpsT])
            ncp = 128 if isA else ntile * 256
            S.act(lambda e, kct=kct: e.copy(out=kct.rearrange("p a b c -> p (a b c)")[:, 0:ncp], in_=psTb[:, 0:ncp]), r=[kpsT], w=[kkct, tick(6.0)])
            psS, kS = next_bank()
            if isA:
                rhs = QBD[:, 0:4, :, :].rearrange("p i (t n) g -> p n g i t", n=16)[:, n]
                S.pe(lambda e, rhs=rhs, kct=kct: e.matmul(psS[:, 0:32].rearrange("p (g i t) -> p g i t", g=2, i=4), lhsT=kct[:, 0, 0, :], rhs=rhs, start=True, stop=True),
                     r=[kkct, "QBD"], w=[kS])
                S.pe(lambda e, rhs=rhs: e.matmul(psS[0:4, 256:288].rearrange("p (g i t) -> p g i t", g=2, i=4), lhsT=scol(KT[:, 0, SC0:NT], n), rhs=rhs, start=True, stop=True),
                     r=[("KT", 0, 4), "QBD"], w=[kS])
            else:
                for j in range(2):
                    rhs = QBD[:, j, :, :].rearrange("p (t n) h -> p n t h", n=16)[:, n]
                    if ntile == 1:
                        S.pe(lambda e, j=j, rhs=rhs, kct=kct: e.matmul(psS[:, j * 8:j * 8 + 8].rearrange("p (t h) -> p t h", h=2), lhsT=kct[:, 0, j, :], rhs=rhs, start=True, stop=True),
                             r=[kkct, "QBD"], w=[kS])
                    else:
                        for t in range(4):
                            S.pe(lambda e, j=j, t=t, rhs=rhs, kct=kct: e.matmul(psS[:, j * 8 + t * 2:j * 8 + t * 2 + 2], lhsT=kct[:, t, j, :], rhs=rhs[:, t, :], start=True, stop=True),
                                 r=[kkct, "QBD"], w=[kS])
                    S.pe(lambda e, j=j, rhs=rhs: e.matmul(psS[0:4, 256 + j * 8:256 + j * 8 + 8].rearrange("p (t h) -> p t h", h=2), lhsT=scol(KT[:, j, SC0:NT], n), rhs=rhs, start=True, stop=True),
                         r=[("KT", j, 4), "QBD"], w=[kS])
            tmp, kT = TMPr.next()
            S.dve(lambda e, tmp=tmp, psS=psS: e.scalar_tensor_tensor(out=tmp[:, 0:nc_], in0=psS[:, 0:nc_], scalar=0.125, in1=TS[:, ts0:ts0 + nc_], op0=ALU.mult, op1=ALU.add),
                  r=[kS, "TS"], w=[kT])
            pt, kP = PTr.next()
            S.act(lambda e, pt=pt, tmp=tmp: e.activation(out=pt[:, 0:nc_], in_=tmp[:, 0:nc_], func=AF.Exp), r=[kT], w=[kP])
            tmpn, kTn = TMPNr.next()
            S.dve(lambda e, tmpn=tmpn, psS=psS: e.scalar_tensor_tensor(out=tmpn[:, 0:nc_], in0=psS[0:4, 256:256 + nc_], scalar=0.125, in1=TSN[:, ts0:ts0 + nc_], op0=ALU.mult, op1=ALU.add),
                  r=[kS, "TSN"], w=[kTn])
            ptn, kPn = PTNr.next()
            S.act(lambda e, ptn=ptn, tmpn=tmpn: e.activation(out=ptn[:, 0:nc_], in_=tmpn[:, 0:nc_], func=AF.Exp), r=[kTn], w=[kPn])
            psU, kU = next_bank()
            psL = psU[:, 256:512]
            S.pe(lambda e, pt=pt: e.matmul(psL[:, 0:nc_], lhsT=ONES, rhs=pt[:, 0:nc_], start=True, stop=False), r=[kP, "ONES"], w=[kU])
            S.pe(lambda e, ptn=ptn: e.matmul(psL[:, 0:nc_], lhsT=ONES[0:4, :], rhs=ptn[:, 0:nc_], start=False, stop=(not isA)), r=[kPn, "ONES"], w=[kU])
            if isA:
                S.pe(lambda e: e.matmul(psL[:, 0:nc_], lhsT=ONES[0:1, :], rhs=SINKS, start=False, stop=True), r=["SINKROW", "ONES"], w=[kU])
                S.pe(lambda e, ck=ck, pt=pt: e.matmul(psU[:, 0:32], lhsT=ck[:, 0, 128:256], rhs=pt[:, 0:32], start=True, stop=False), r=[kP, kck], w=[kU])
                S.pe(lambda e, nkv=nkv, ptn=ptn: e.matmul(psU[:, 0:32], lhsT=nkv[:, 0:128], rhs=ptn[:, 0:32], start=False, stop=True), r=[kPn, knkv], w=[kU])
                lr, kLr = LRr.next()
                for gg in range(2):
                    S.dve(lambda e, lr=lr, gg=gg: e.reciprocal(out=lr[64 * gg:64 * gg + 64, 0:16], in_=psL[64 * gg:64 * gg + 64, gg * 16:gg * 16 + 16]), r=[kU], w=[kLr])
                for gg in range(2):
                    S.dve(lambda e, lr=lr, n=n, gg=gg: e.tensor_tensor(out=scol(OA[64 * gg:64 * gg + 64, :, SC0:NT], n), in0=psU[64 * gg:64 * gg + 64, gg * 16:gg * 16 + 16].rearrange("p (i t) -> p i t", i=4),
                                                                   in1=lr[64 * gg:64 * gg + 64, 0:16].rearrange("p (i t) -> p i t", i=4), op=ALU.mult), r=[kU, kLr], w=[("OA", 16)])
            else:
                for j in range(2):
                    vj = slice(256 + j * 128, 256 + (j + 1) * 128)
                    vn = slice(j * 128, (j + 1) * 128)
                    if ntile == 1:
                        S.pe(lambda e, j=j, ck=ck, pt=pt, vj=vj: e.matmul(psU[:, j * 8:j * 8 + 8], lhsT=ck[:, 0, vj], rhs=pt[:, j * 8:j * 8 + 8], start=True, stop=False), r=[kP, kck], w=[kU])
                        S.pe(lambda e, j=j, nkv=nkv, ptn=ptn, vn=vn: e.matmul(psU[:, j * 8:j * 8 + 8], lhsT=nkv[:, vn], rhs=ptn[:, j * 8:j * 8 + 8], start=False, stop=True), r=[kPn, knkv], w=[kU])
                    else:
                        S.pe(lambda e, j=j, nkv=nkv, ptn=ptn, vn=vn: e.matmul(psU[:, j * 8:j * 8 + 8], lhsT=nkv[:, vn], rhs=ptn[:, j * 8:j * 8 + 8], start=True, stop=False, skip_group_check=True), r=[kPn, knkv], w=[kU])
                        for t in range(4):
                            S.pe(lambda e, j=j, t=t, ck=ck, pt=pt, vj=vj: e.matmul(psU[:, j * 8 + t * 2:j * 8 + t * 2 + 2], lhsT=ck[:, t, vj], rhs=pt[:, j * 8 + t * 2:j * 8 + t * 2 + 2],
                                                                                 start=False, stop=True, skip_group_check=True), r=[kP, kck], w=[kU])
                wk, rk = ("UB", g), ([("UB", g - 1)] if g > 0 else [])
                for hh in range(2):
                    hs = slice(64 * hh, 64 * hh + 64)
                    pu = psU[hs, 0:16].rearrange("p (j t h) -> p j t h", j=2, h=2)[:, :, :, hh]
                    pl = psL[hs, 0:16].rearrange("p (j t h) -> p j t h", j=2, h=2)[:, :, :, hh]
                    uv = scol(UB[hs, :, SC0:NT], n)
                    lvv = scol(LB[hs, :, SC0:NT], n)
                    if g == 0:
                        S.dve(lambda e, uv=uv, pu=pu: e.tensor_copy(out=uv, in_=pu), r=[kU] + rk, w=[wk])
                        S.dve(lambda e, lvv=lvv, pl=pl: e.tensor_copy(out=lvv, in_=pl), r=[kU] + rk, w=[wk])
                    else:
                        S.dve(lambda e, uv=uv, pu=pu: e.tensor_tensor(out=uv, in0=uv, in1=pu, op=ALU.add), r=[kU] + rk, w=[wk])
                        S.dve(lambda e, lvv=lvv, pl=pl: e.tensor_tensor(out=lvv, in0=lvv, in1=pl, op=ALU.add), r=[kU] + rk, w=[wk])

    if "B" in phases:
        mode["attn"] = True
        for g in GROUPS:
            if _GSEL and str(g) not in _GSEL.split(","):
                continue
            do_group(g)
        mode["attn"] = False
        bg_shifts()
        for s in range(5):
            c0, n = (s * 512, 512) if s < 4 else (SC0, 64)
            for j in range(2):
                lr, kLr = LRr.next()
                S.act(lambda e, lr=lr, j=j, c0=c0, n=n: e.activation(out=lr[:, 0:n], in_=LB[:, j, c0:c0 + n], func=AF.Ln), r=[("UB", 2)], w=[kLr])
                S.act(lambda e, lr=lr, n=n: e.activation(out=lr[:, 0:n], in_=lr[:, 0:n], func=AF.Exp, scale=-1.0), r=[kLr], w=[kLr])
                S.dve(lambda e, lr=lr, j=j, c0=c0, n=n: e.tensor_tensor(out=OB[:, j, c0:c0 + n], in0=UB[:, j, c0:c0 + n], in1=lr[:, 0:n], op=ALU.mult),
                      r=[("UB", 2), kLr], w=[("OB", s)])
    if debug:
        dbg["hT"] = dout("D_hT", (128, 8, NT), BF16)
        dbg["OA"] = dout("D_OA", (128, 4, NT), BF16)
        dbg["OB"] = dout("D_OB", (128, 2, NT), BF16)
        S.dma("sp", lambda e: e.dma_start(out=dbg["hT"], in_=hT), r=[("hT", b) for b in range(17)])
        S.dma("sp", lambda e: e.dma_start(out=dbg["OA"], in_=OA), r=[("OA", b) for b in range(17)])
        S.dma("sp", lambda e: e.dma_start(out=dbg["OB"], in_=OB), r=[("OB", s) for s in range(5)])

    TILES = [(0, 512), (512, 512), (1024, 512), (1536, 512), (SC0, 64)]
    if os.environ.get("KVERB"):
        print("SBUF end of phase B", cur[0], flush=True)
    S.barrier()
    cur[0] = PERSIST_END
    if "D1" in phases:
        MIXT = alloc("MIXT", (128, 8, NT), BF16)
        WOUTs = alloc("WOUTs", (128, 8, 1024), BF16)
        WD1r = Ring("WD1", [alloc("WD1_%d" % i, (128, 2816), BF16) for i in range(2)])
        SGr = Ring("SG", [alloc("SG%d" % i, (128, 512), F32) for i in range(4)])
        XIN2r = Ring("XIN2", [alloc("XIN2_%d" % i, (128, 1024), F32) for i in range(3)])
        X1r = Ring("X1", [alloc("X1_%d" % i, (128, 1024), F32) for i in range(3)])
        XN2r = Ring("XN2", [alloc("XN2_%d" % i, (128, 1024), BF16) for i in range(3)])
        JUNK2 = alloc("JUNK2", (128, 512), BF16)
        for ei in range(8):
            wd, kwd = WD1r.next()
            S.dma("pool", lambda e, wd=wd, ei=ei: e.dma_start(out=wd, in_=WD1[ei]), w=[kwd])
            if ei == 2:
                S.dma("pool", lambda e: e.dma_start(out=WOUTs, in_=WOUT), w=["WOUT"])
            for ti, (c0, n) in enumerate(TILES):
                hk = [("hT", b) for b in range(17)]
                banks = [next_bank() for _ in range(4)]
                (pGA, kGA), (pGB, kGB), (pYA, kYA), (pYB, kYB) = banks
                for k in range(8):
                    S.pe(lambda e, wd=wd, k=k, c0=c0, n=n, pGA=pGA: e.matmul(pGA[:, 0:n], lhsT=wd[:, k * 256:k * 256 + 128], rhs=hT[:, k, c0:c0 + n], start=(k == 0), stop=(k == 7)),
                         r=[kwd] + hk, w=[kGA])
                for k in range(8):
                    S.pe(lambda e, wd=wd, k=k, c0=c0, n=n, pGB=pGB: e.matmul(pGB[:, 0:n], lhsT=wd[:, k * 256 + 128:k * 256 + 256], rhs=hT[:, k, c0:c0 + n], start=(k == 0), stop=(k == 7)),
                         r=[kwd] + hk, w=[kGB])
                for i in range(4):
                    S.pe(lambda e, wd=wd, i=i, c0=c0, n=n, pYA=pYA: e.matmul(pYA[:, 0:n], lhsT=wd[:, 2048 + i * 128:2048 + (i + 1) * 128], rhs=OA[:, i, c0:c0 + n], start=(i == 0), stop=(i == 3)),
                         r=[kwd] + [("OA", b) for b in range(17)], w=[kYA])
                for j in range(2):
                    S.pe(lambda e, wd=wd, j=j, c0=c0, n=n, pYB=pYB: e.matmul(pYB[:, 0:n], lhsT=wd[:, 2560 + j * 128:2560 + (j + 1) * 128], rhs=OB[:, j, c0:c0 + n], start=(j == 0), stop=(j == 1)),
                         r=[kwd] + [("OB", s) for s in range(5)], w=[kYB])
                sa, ksa = SGr.next()
                sb_, ksb = SGr.next()
                S.act(lambda e, sa=sa, pGA=pGA, n=n: e.activation(out=sa[:, 0:n], in_=pGA[:, 0:n], func=AF.Sigmoid), r=[kGA], w=[ksa])
                S.act(lambda e, sb_=sb_, pGB=pGB, n=n: e.activation(out=sb_[:, 0:n], in_=pGB[:, 0:n], func=AF.Sigmoid), r=[kGB], w=[ksb])
                S.dve(lambda e, sa=sa, pYA=pYA, n=n: e.tensor_tensor(out=sa[:, 0:n], in0=sa[:, 0:n], in1=pYA[:, 0:n], op=ALU.mult), r=[ksa, kYA], w=[ksa])
                S.dve(lambda e, sb_=sb_, pYB=pYB, n=n: e.tensor_tensor(out=sb_[:, 0:n], in0=sb_[:, 0:n], in1=pYB[:, 0:n], op=ALU.mult), r=[ksb, kYB], w=[ksb])
                S.pool(lambda e, sa=sa, sb_=sb_, ei=ei, c0=c0, n=n: e.tensor_tensor(out=MIXT[:, ei, c0:c0 + n], in0=sa[:, 0:n], in1=sb_[:, 0:n], op=ALU.add),
                       r=[ksa, ksb], w=[("MIXT", ti)])
        for blk in range(17):
            rows = 128 if blk < 16 else 64
            col0 = blk * 128
            ti = blk // 4
            ph = [next_bank(), next_bank()]
            for half in range(2):
                pm, kpm = ph[half]
                for k in range(8):
                    S.pe(lambda e, pm=pm, k=k, half=half, rows=rows, col0=col0: e.matmul(pm[:rows, :], lhsT=MIXT[:, k, col0:col0 + rows], rhs=WOUTs[:, k, half * 512:(half + 1) * 512],
                                                                                      start=(k == 0), stop=(k == 7)), r=[("MIXT", ti), "WOUT"], w=[kpm])
            st, kst = next_stat()
            for half in range(2):
                pm, kpm = ph[half]
                S.act(lambda e, pm=pm, half=half, rows=rows, st=st: e.activation(out=JUNK2[:rows, 0:512], in_=pm[:rows, :], func=AF.Square, accum_out=st[:rows, half:half + 1]),
                      r=[kpm], w=[kst, "JUNK2"])
            S.dve(lambda e, st=st, rows=rows: e.tensor_tensor(out=st[:rows, 2:3], in0=st[:rows, 0:1], in1=st[:rows, 1:2], op=ALU.add), r=[kst], w=[kst])
            rms_rstd(st[:rows, 2:3], kst, st[:rows, 3:4], kst, 1024)
            xin, kx = XIN2r.next()
            S.dma("sp", lambda e, xin=xin, rows=rows, col0=col0: e.dma_start(out=xin[:rows, :], in_=X[col0:col0 + rows, :]), w=[kx])
            x1, kx1 = X1r.next()
            for half in range(2):
                pm, kpm = ph[half]
                S.dve(lambda e, pm=pm, half=half, rows=rows, st=st, x1=x1: e.scalar_tensor_tensor(out=x1[:rows, half * 512:(half + 1) * 512], in0=pm[:rows, :], scalar=st[:rows, 3:4],
                                                                                              in1=GP[:rows, 0, half * 512:(half + 1) * 512], op0=ALU.mult, op1=ALU.mult),
                      r=[kpm, kst, "GP"], w=[kx1])
            S.pool(lambda e, x1=x1, xin=xin, rows=rows: e.tensor_tensor(out=x1[:rows, :], in0=x1[:rows, :], in1=xin[:rows, :], op=ALU.add), r=[kx1, kx], w=[kx1])
            S.dma("sp", lambda e, x1=x1, rows=rows, col0=col0: e.dma_start(out=Y[col0:col0 + rows, :], in_=x1[:rows, :]), r=[kx1], w=[("Yx1", blk)])
            norm_transpose(x1, kx1, rows, 1, H2T, [("H2T", blk)], col0, XN2r, JUNK2)
    if debug:
        dbg["H2T"] = dout("D_H2T", (128, 8, NT), BF16)
        S.dma("sp", lambda e: e.dma_start(out=dbg["H2T"], in_=H2T), r=[("H2T", b) for b in range(17)])

    S.barrier()
    cur[0] = mark_D2
    if "D2" in phases:
        WDNs = alloc("WDNs", (128, 32, 1024), BF16)
        UT = alloc("UT", (128, 32, 576), BF16)
        WUPr = Ring("WUP", [alloc("WUP%d" % i, (128, 2048), BF16) for i in range(3)])
        ABr = Ring("AB", [alloc("AB%d" % i, (128, 516), F32) for i in range(2)])
        CCr = Ring("CC", [alloc("CC%d" % i, (128, 512), F32) for i in range(2)])
        GLr = Ring("GL", [alloc("GL%d" % i, (128, 512), F32) for i in range(2)])
        CARRY = alloc("CARRY", (128, 32, 2), F32)
        STT = alloc("STT", (128, 32, 32), F32)
        TAILA = alloc("TAILA", (128, 32, 34), F32)
        SSTGr = Ring("SSTG", [alloc("SSTG%d" % i, (32, 1024), F32) for i in range(2)])
        YBr = Ring("YB", [alloc("YB%d" % i, (128, 1024), F32) for i in range(2)])
        X1Rr = Ring("X1R", [alloc("X1R%d" % i, (128, 1024), F32) for i in range(2)])
        JUNK3 = alloc("JUNK3", (128, 512), BF16)
        S.dve(lambda e: e.memset(CARRY, 0.0), w=["CARRY"])
        for j0 in range(0, 32, 8):
            sstg, ksstg = SSTGr.next()
            S.dma("sp", lambda e, sstg=sstg, j0=j0: e.dma_start(out=sstg, in_=STATE[:, j0 * 128:(j0 + 8) * 128]), w=[ksstg])
            ps, kps = next_bank()
            for j in range(j0, j0 + 8):
                S.pe(lambda e, ps=ps, j=j, j0=j0, sstg=sstg: e.transpose(ps[:, (j - j0) * 32:(j - j0 + 1) * 32], sstg[:, (j - j0) * 128:(j - j0 + 1) * 128], IDF[0:32, 0:32]),
                     r=[ksstg, "IDF"], w=[kps])
            S.act(lambda e, ps=ps, j0=j0: e.copy(out=STT[:, j0:j0 + 8, :].rearrange("p a b -> p (a b)"), in_=ps[:, 0:256]), r=[kps], w=["STT"])
        def up_part(ti, j, wu, kwu, c0, n, smp, u0):
            hk = [("H2T", b) for b in range(17)]
            (pA, kA), (pG, kG) = next_bank(), next_bank()
            for k in range(8):
                S.pe(lambda e, k=k: e.matmul(pA[:, 0:n], lhsT=wu[:, k * 256:k * 256 + 128], rhs=H2T[:, k, c0:c0 + n], start=(k == 0), stop=(k == 7)),
                     r=[kwu] + hk, w=[kA])
            for k in range(8):
                S.pe(lambda e, k=k: e.matmul(pG[:, 0:n], lhsT=wu[:, k * 256 + 128:k * 256 + 256], rhs=H2T[:, k, c0:c0 + n], start=(k == 0), stop=(k == 7)),
                     r=[kwu] + hk, w=[kG])
            ab, kab = ABr.next()
            cc, kcc = CCr.next()
            gl, kgl = GLr.next()
            sh = 16 if smp else 1
            hal = 2 * sh
            if smp:
                S.pool(lambda e: e.tensor_copy(out=ab[:, 0:32], in_=STT[:, j, :]), r=["STT"], w=[kab])
            else:
                S.pool(lambda e: e.tensor_copy(out=ab[:, 0:2], in_=CARRY[:, j, :]), r=["CARRY"], w=[kab])
            S.act(lambda e: e.copy(out=ab[:, hal:hal + n], in_=pA[:, 0:n]), r=[kA], w=[kab])
            if not smp:
                S.pool(lambda e: e.tensor_copy(out=CARRY[:, j, :], in_=ab[:, n:n + 2]), r=[kab], w=["CARRY"])
                if ti == 3:
                    S.pool(lambda e: e.tensor_copy(out=TAILA[:, j, 32:34], in_=ab[:, n:n + 2]), r=[kab], w=["TAILA"])
            else:
                S.pool(lambda e: e.tensor_copy(out=TAILA[:, j, 0:32], in_=ab[:, 32 + 32:32 + 64]), r=[kab], w=["TAILA"])
            S.act(lambda e: e.activation(out=cc[:, 0:n], in_=pA[:, 0:n], func=AF.Identity, scale=CW[:, j, 2:3], bias=CW[:, j, 3:4]),
                  r=[kA, "CW"], w=[kcc])
            S.dve(lambda e: e.scalar_tensor_tensor(out=cc[:, 0:n], in0=ab[:, sh:sh + n], scalar=CW[:, j, 1:2], in1=cc[:, 0:n], op0=ALU.mult, op1=ALU.add),
                  r=[kab, kcc, "CW"], w=[kcc])
            S.dve(lambda e: e.scalar_tensor_tensor(out=cc[:, 0:n], in0=ab[:, 0:n], scalar=CW[:, j, 0:1], in1=cc[:, 0:n], op0=ALU.mult, op1=ALU.add),
                  r=[kab, kcc, "CW"], w=[kcc])
            S.act(lambda e: e.activation(out=gl[:, 0:n], in_=cc[:, 0:n], func=AF.Gelu_apprx_tanh), r=[kcc], w=[kgl])
            S.dve(lambda e: e.tensor_tensor(out=UT[:, j, u0:u0 + n], in0=gl[:, 0:n], in1=pG[:, 0:n], op=ALU.mult), r=[kgl, kG], w=[("UT", j, u0)])

        def down_block(blk, rows, col0, u0):
            ph = [next_bank(), next_bank()]
            for half in range(2):
                pf, kpf = ph[half]
                for j in range(32):
                    S.pe(lambda e, pf=pf, j=j, half=half: e.matmul(pf[:rows, :], lhsT=UT[:, j, u0:u0 + rows], rhs=WDNs[:, j, half * 512:(half + 1) * 512],
                                                                  start=(j == 0), stop=(j == 31)), r=[("UT", j, 0 if u0 < 512 else 512), ("WDN", j // 8)], w=[kpf])
            st, kst = next_stat()
            for half in range(2):
                pf, kpf = ph[half]
                S.act(lambda e, pf=pf, half=half: e.activation(out=JUNK3[:rows, :], in_=pf[:rows, :], func=AF.Square, accum_out=st[:rows, half:half + 1]),
                      r=[kpf], w=[kst, "JUNK3"])
            S.dve(lambda e: e.tensor_tensor(out=st[:rows, 2:3], in0=st[:rows, 0:1], in1=st[:rows, 1:2], op=ALU.add), r=[kst], w=[kst])
            rms_rstd(st[:rows, 2:3], kst, st[:rows, 3:4], kst, 1024)
            x1r, kx1r = X1Rr.next()
            S.dma("sp", lambda e: e.dma_start(out=x1r[:rows, :], in_=Y[col0:col0 + rows, :]), r=[("Yx1", blk)], w=[kx1r])
            yb, kyb = YBr.next()
            for half in range(2):
                pf, kpf = ph[half]
                S.dve(lambda e, pf=pf, half=half: e.scalar_tensor_tensor(out=yb[:rows, half * 512:(half + 1) * 512], in0=pf[:rows, :], scalar=st[:rows, 3:4],
                                                                        in1=GP[:rows, 1, half * 512:(half + 1) * 512], op0=ALU.mult, op1=ALU.mult),
                      r=[kpf, kst, "GP"], w=[kyb])
            S.pool(lambda e: e.tensor_tensor(out=yb[:rows, :], in0=yb[:rows, :], in1=x1r[:rows, :], op=ALU.add), r=[kyb, kx1r], w=[kyb])
            S.dma("sp", lambda e: e.dma_start(out=Y[col0:col0 + rows, :], in_=yb[:rows, :]), r=[kyb, kx1r], w=[("Yx1", blk)])

        for ti in range(4):
            c0 = ti * 512
            for j in range(32):
                wu, kwu = WUPr.next()
                S.dma("pool", lambda e, wu=wu, j=j: e.dma_start(out=wu, in_=WUP[j]), w=[kwu])
                if ti == 0 and j in (3, 6, 9, 12):
                    q4 = (j - 3) // 3
                    S.dma("pool", lambda e, q4=q4: e.dma_start(out=WDNs[:, q4 * 8:(q4 + 1) * 8, :], in_=WDN[:, q4 * 8:(q4 + 1) * 8, :]), w=[("WDN", q4)])
                up_part(ti, j, wu, kwu, c0, 512, False, 0)
                if ti == 3:
                    up_part(ti, j, wu, kwu, SC0, 64, True, 512)
            for bl in range(4):
                down_block(ti * 4 + bl, 128, c0 + bl * 128, bl * 128)
            if ti == 3:
                down_block(16, 64, SC0, 512)
        for j0 in range(0, 32, 4):
            ps, kps = next_bank()
            for j in range(j0, j0 + 4):
                S.pe(lambda e, ps=ps, j=j, j0=j0: e.transpose(ps[0:34, (j - j0) * 128:(j - j0 + 1) * 128], TAILA[:, j, :], IDF), r=["TAILA", "IDF"], w=[kps])
            yb, kyb = YBr.next()
            S.act(lambda e, ps=ps, yb=yb: e.copy(out=yb[0:34, 0:512], in_=ps[0:34, :]), r=[kps], w=[kyb])
            S.dma("sp", lambda e, yb=yb, j0=j0: e.dma_start(out=NCONV[:, j0 * 128:(j0 + 4) * 128], in_=yb[0:34, 0:512]), r=[kyb])
    if os.environ.get("KVERB"):
        print("SBUF high water", hw[0], "of", SB_END, flush=True)
    S.emit()
    return nc, sorted(dbg.keys())


_CACHE = {}


def _get_nc(debug=False, phases=("A", "B", "D1", "D2")):
    key = (debug, tuple(phases))
    if key not in _CACHE:
        _CACHE[key] = build(debug, phases)
    return _CACHE[key]


def run_cores(inputs, debug=False, phases=("A", "B", "D1", "D2"), cores=range(8)):
    nc, dbgnames = _get_nc(debug, phases)
    sh = _prep_shared(inputs)
    in_maps = []
    for c in cores:
        m = dict(sh)
        m.update(_prep_core(inputs, c))
        in_maps.append(m)
    res = run_bass_kernel_spmd(nc, in_maps, core_ids=list(range(len(in_maps))))
    return res.results


def kernel(**inputs):
    inputs = {k: np.asarray(v) for k, v in inputs.items()}
    res = run_cores(inputs)
    yp = np.stack([r["Y"][0:2048] for r in res], 0)
    ys = np.concatenate([r["Y"][2048:2112].reshape(4, 16, 1024).transpose(1, 0, 2) for r in res], 0)
    outs = [yp.astype(np.float32), ys.astype(np.float32)]
    for (pn, sn, W, H) in (("NA_P", "NA_S", 128, 2), ("NB1_P", "NB1_S", 128, 4), ("NB2_P", "NB2_S", 512, 4), ("NB3_P", "NB3_S", 2048, 4)):
        outs.append(np.stack([r[pn] for r in res], 0).reshape(1, 8, W, 2, H, 64).astype(np.float32))
        outs.append(np.concatenate([r[sn] for r in res], 0).reshape(1, 128, W, 2, H, 64).astype(np.float32))
    outs.append(np.stack([r["NCONV"][32:34] for r in res], 0).reshape(1, 8, 2, 4096).astype(np.float32))
    outs.append(np.concatenate([r["NCONV"][0:32].reshape(2, 16, 4096).transpose(1, 0, 2) for r in res], 0).reshape(1, 128, 2, 4096).astype(np.float32))
    return tuple(outs)
```
